# Optimizing a Trainium2 kernel written in Bass

```python
import jax, jax.numpy as jnp
from jax import lax
import numpy as np

D_MODEL = 1024
BATCH = 16
SEQ = 256
DEPTH = 1
DEC_BATCH = 2
DEC_SEQ = 2048
PAST_LEN = 256

GRID_W = 64
MLA_HEADS = 8
Q_LORA = 384
KV_LORA = 256
QK_NOPE = 64
QK_ROPE = 32
V_DIM = 64
QK_SCALE = (QK_NOPE + QK_ROPE) ** -0.5
QBLK = 128
ROPE_BASE = 10000.0
HG_HEADS = 8
HG_DK = 64
HG_DV = 64
CHUNK = 16
D_FF = 2816
EPS = 1e-6
N_MOD = 9
MLA_W = MLA_HEADS * V_DIM
HG_KW = HG_HEADS * HG_DK
HG_W = HG_HEADS * HG_DV
IN_SPLITS = (Q_LORA, KV_LORA, QK_ROPE, HG_KW, HG_KW, HG_KW, HG_W, HG_W, D_MODEL, D_MODEL)
D_IN = Q_LORA + KV_LORA + QK_ROPE + 3 * HG_KW + 2 * HG_W + 2 * D_MODEL

kernel_name = "hybrid_mla_hgrn2_prefix_dit_step"


def rmsnorm(x, g):
    xf = x.astype(jnp.float32)
    xf = xf * lax.rsqrt(jnp.mean(xf * xf, axis=-1, keepdims=True) + EPS)
    return (xf * g.astype(jnp.float32)).astype(x.dtype)


def modulate(x, g, shift, scale):
    return rmsnorm(x, g) * (1.0 + scale) + shift


def swiglu(h, w1, w3, w2):
    return (jax.nn.silu(h @ w1) * (h @ w3)) @ w2


def axial_rope(x):
    T = x.shape[1]
    rows = T // GRID_W
    row = jnp.repeat(jnp.arange(rows), GRID_W).astype(jnp.float32)
    col = jnp.tile(jnp.arange(GRID_W), rows).astype(jnp.float32)
    half = QK_ROPE // 2
    quarter = half // 2
    inv = ROPE_BASE ** (-jnp.arange(quarter, dtype=jnp.float32) / quarter)
    shape = (1, T) + (1,) * (x.ndim - 3) + (quarter,)

    def rot(xa, pos):
        ang = pos[:, None] * inv
        c = jnp.cos(ang).reshape(shape).astype(x.dtype)
        s = jnp.sin(ang).reshape(shape).astype(x.dtype)
        x1, x2 = xa[..., :quarter], xa[..., quarter:]
        return jnp.concatenate([x1 * c - x2 * s, x2 * c + x1 * s], axis=-1)

    return jnp.concatenate([rot(x[..., :half], row), rot(x[..., half:], col)], axis=-1)


def mla_attention(q_nope, q_rope, k_nope, k_rope, v):
    B, Tq = q_nope.shape[:2]
    nb = Tq // QBLK

    def block(qs):
        qn, qr = qs
        s = jnp.einsum('bqhd,bkhd->bhqk', qn, k_nope) + jnp.einsum('bqhr,bkr->bhqk', qr, k_rope)
        p = jax.nn.softmax(s.astype(jnp.float32) * QK_SCALE, axis=-1).astype(v.dtype)
        return jnp.einsum('bhqk,bkhd->bqhd', p, v)

    qn = q_nope.reshape(B, nb, QBLK, MLA_HEADS, QK_NOPE).transpose(1, 0, 2, 3, 4)
    qr = q_rope.reshape(B, nb, QBLK, MLA_HEADS, QK_ROPE).transpose(1, 0, 2, 3, 4)
    o = lax.map(block, (qn, qr))
    return o.transpose(1, 0, 2, 3, 4).reshape(B, Tq, MLA_HEADS * V_DIM)


def hgrn2_chunked(q, k, logf, v, s0):
    B, T, H, DK = q.shape
    DV = v.shape[-1]
    n = T // CHUNK
    qc = q.reshape(B, n, CHUNK, H, DK)
    kc = k.reshape(B, n, CHUNK, H, DK)
    vc = v.reshape(B, n, CHUNK, H, DV)
    b = jnp.cumsum(logf.reshape(B, n, CHUNK, H, DK), axis=2)
    mask = jnp.tril(jnp.ones((CHUNK, CHUNK), dtype=bool))[:, :, None, None]
    diff = b[:, :, :, None] - b[:, :, None, :]
    decay = jnp.where(mask, jnp.exp(jnp.where(mask, diff, 0.0)), 0.0)
    A = jnp.einsum('bntshk,bnthk,bnshk->bnhts', decay, qc, kc)
    o_intra = jnp.einsum('bnhts,bnshv->bnthv', A, vc)
    g = b[:, :, -1]
    dS = jnp.einsum('bnshk,bnshv->bnhkv', kc * jnp.exp(g[:, :, None] - b), vc)

    def step(S, inp):
        gj, dSj = inp
        return jnp.exp(gj)[..., None] * S + dSj, S

    s_final, S_prev = lax.scan(step, s0.astype(jnp.float32),
                               (g.transpose(1, 0, 2, 3), dS.transpose(1, 0, 2, 3, 4)))
    S_prev = S_prev.transpose(1, 0, 2, 3, 4)
    o_inter = jnp.einsum('bnthk,bnhkv->bnthv', qc * jnp.exp(b), S_prev)
    return (o_intra + o_inter).reshape(B, T, H, DV), s_final


def token_mixer(h, cache, lb, w_in, q_norm_g, w_qb, kv_norm_g, w_kvb, w_up_attn, hg_norm_g, w_up_rec, w_out):
    B, T, _ = h.shape
    f32 = jnp.float32
    split_idx = [int(i) for i in np.cumsum(IN_SPLITS)[:-1]]
    qc, kvc, kr, hq, hf_f, hf_b, hi, hgate, ga, gr = jnp.split(h @ w_in, split_idx, axis=-1)

    q = (rmsnorm(qc, q_norm_g) @ w_qb).reshape(B, T, MLA_HEADS, QK_NOPE + QK_ROPE)
    q_nope, q_rope = q[..., :QK_NOPE], q[..., QK_NOPE:]
    ckv = rmsnorm(kvc, kv_norm_g)
    if cache is None:
        ckv_all, kr_all = ckv, kr
        s0 = jnp.zeros((B, 2, HG_HEADS, HG_DK, HG_DV), f32)
    else:
        cache_ckv, cache_kr, cache_state = cache
        q_rope = axial_rope(q_rope)
        ckv_all = jnp.concatenate([cache_ckv, ckv], axis=1)
        kr_all = jnp.concatenate([cache_kr, axial_rope(kr)], axis=1)
        s0 = cache_state.astype(f32)
    kv = (ckv_all @ w_kvb).reshape(B, ckv_all.shape[1], MLA_HEADS, QK_NOPE + V_DIM)
    o_attn = mla_attention(q_nope, q_rope, kv[..., :QK_NOPE], kr_all, kv[..., QK_NOPE:])

    heads = lambda t, d: t.reshape(B, T, HG_HEADS, d).astype(f32)
    flip = lambda t: jnp.flip(t, axis=1)
    q_h = jax.nn.silu(heads(hq, HG_DK))
    v_h = heads(hi, HG_DV)
    lb_h = lb.reshape(2, HG_HEADS, HG_DK)
    f_fw = lb_h[0] + (1.0 - lb_h[0]) * jax.nn.sigmoid(heads(hf_f, HG_DK))
    f_bw = lb_h[1] + (1.0 - lb_h[1]) * jax.nn.sigmoid(heads(hf_b, HG_DK))
    o_fw, s_fw = hgrn2_chunked(q_h, 1.0 - f_fw, jnp.log(f_fw), v_h, s0[:, 0])
    o_bw, s_bw = hgrn2_chunked(flip(q_h), flip(1.0 - f_bw), flip(jnp.log(f_bw)), flip(v_h), s0[:, 1])
    o_rec = rmsnorm(o_fw + flip(o_bw), hg_norm_g) * jax.nn.silu(heads(hgate, HG_DV))
    o_rec = o_rec.reshape(B, T, HG_W).astype(h.dtype)

    merged = jax.nn.sigmoid(ga) * (o_attn @ w_up_attn) + jax.nn.sigmoid(gr) * (o_rec @ w_up_rec)
    out = merged @ w_out
    if cache is None:
        return out, (ckv, kr, jnp.stack([s_fw, s_bw], axis=1).astype(h.dtype))
    return out, None


def trunk_layer(x, mod, cache, lb, norm_g, ffn1, ffn2, mixer_w):
    m = [mod[:, i][:, None, :] for i in range(N_MOD)]
    x = x + 0.5 * m[2] * swiglu(modulate(x, norm_g[0], m[0], m[1]), *ffn1)
    mix, ctx = token_mixer(modulate(x, norm_g[1], m[3], m[4]), cache, lb, *mixer_w)
    x = x + m[5] * mix
    x = x + 0.5 * m[8] * swiglu(modulate(x, norm_g[2], m[6], m[7]), *ffn2)
    return x, ctx


def setup_inputs(seed: int = 0) -> dict:
    key = jax.random.key(seed)
    ks = list(jax.random.split(key, 32))
    D = D_MODEL

    def nrm(shape, s=1.0):
        return s * jax.random.normal(ks.pop(), shape, jnp.float32)

    def gain(shape):
        return 1.0 + nrm(shape, 0.1)

    return {
        "x_prompt": nrm((BATCH, SEQ, D)),
        "x_sample": nrm((DEC_BATCH, DEC_SEQ, D)),
        "cache_ckv": nrm((DEC_BATCH, DEPTH, PAST_LEN, KV_LORA)),
        "cache_krope": nrm((DEC_BATCH, DEPTH, PAST_LEN, QK_ROPE)),
        "state_hgrn": nrm((DEC_BATCH, DEPTH, 2, HG_HEADS, HG_DK, HG_DV), 0.5),
        "c": nrm((DEC_BATCH, D)),
        "c_ctx": nrm((D,)),
        "w_ada": nrm((DEPTH, D, N_MOD * D), 0.5 * D ** -0.5),
        "b_ada": nrm((DEPTH, N_MOD * D), 0.01),
        "norm_g": gain((DEPTH, 3, D)),
        "ffn1_w1": nrm((DEPTH, D, D_FF), D ** -0.5),
        "ffn1_w3": nrm((DEPTH, D, D_FF), D ** -0.5),
        "ffn1_w2": nrm((DEPTH, D_FF, D), D_FF ** -0.5),
        "ffn2_w1": nrm((DEPTH, D, D_FF), D ** -0.5),
        "ffn2_w3": nrm((DEPTH, D, D_FF), D ** -0.5),
        "ffn2_w2": nrm((DEPTH, D_FF, D), D_FF ** -0.5),
        "w_in": nrm((DEPTH, D, D_IN), D ** -0.5),
        "q_norm_g": gain((DEPTH, Q_LORA)),
        "w_qb": nrm((DEPTH, Q_LORA, MLA_HEADS * (QK_NOPE + QK_ROPE)), Q_LORA ** -0.5),
        "kv_norm_g": gain((DEPTH, KV_LORA)),
        "w_kvb": nrm((DEPTH, KV_LORA, MLA_HEADS * (QK_NOPE + V_DIM)), KV_LORA ** -0.5),
        "w_up_attn": nrm((DEPTH, MLA_W, D), MLA_W ** -0.5),
        "hg_gamma": nrm((2, DEPTH + 1, HG_KW), 0.5),
        "hg_norm_g": gain((DEPTH, HG_DV)),
        "w_up_rec": nrm((DEPTH, HG_W, D), HG_W ** -0.5),
        "w_out": nrm((DEPTH, D, D), D ** -0.5),
        "final_g": gain((D,)),
    }


def reference(x_prompt, x_sample, cache_ckv, cache_krope, state_hgrn, c, c_ctx, w_ada, b_ada, norm_g,
              ffn1_w1, ffn1_w3, ffn1_w2, ffn2_w1, ffn2_w3, ffn2_w2, w_in, q_norm_g, w_qb, kv_norm_g, w_kvb,
              w_up_attn, hg_gamma, hg_norm_g, w_up_rec, w_out, final_g):
    lb_all = jnp.cumsum(jax.nn.softmax(hg_gamma.astype(jnp.float32), axis=1), axis=1)
    xp, xs = x_prompt, x_sample
    new_ckv, new_kr, new_st = [], [], []
    for l in range(DEPTH):
        mod_p = (jax.nn.silu(c_ctx) @ w_ada[l] + b_ada[l]).reshape(1, N_MOD, D_MODEL)
        mod_s = (jax.nn.silu(c) @ w_ada[l] + b_ada[l]).reshape(c.shape[0], N_MOD, D_MODEL)
        ffn1 = (ffn1_w1[l], ffn1_w3[l], ffn1_w2[l])
        ffn2 = (ffn2_w1[l], ffn2_w3[l], ffn2_w2[l])
        mixer_w = (w_in[l], q_norm_g[l], w_qb[l], kv_norm_g[l], w_kvb[l], w_up_attn[l],
                   hg_norm_g[l], w_up_rec[l], w_out[l])
        lb = lb_all[:, l]
        xp, (ckv, kr, st) = trunk_layer(xp, mod_p, None, lb, norm_g[l], ffn1, ffn2, mixer_w)
        new_ckv.append(ckv)
        new_kr.append(kr)
        new_st.append(st)
        xs, _ = trunk_layer(xs, mod_s, (cache_ckv[:, l], cache_krope[:, l], state_hgrn[:, l]), lb,
                            norm_g[l], ffn1, ffn2, mixer_w)
    y_prompt = rmsnorm(xp, final_g)
    y_sample = rmsnorm(xs, final_g)
    new_ckv_arr = jnp.stack(new_ckv, axis=1)
    new_kr_arr = jnp.stack(new_kr, axis=1)
    new_st_arr = jnp.stack(new_st, axis=1)
    return (y_prompt, y_sample, new_ckv_arr, new_kr_arr, new_st_arr)
```

```python
import os
import numpy as np
from contextlib import ExitStack, contextmanager
import concourse.bass as bass
import concourse.mybir as mybir
from concourse.bass_utils import run_bass_kernel_spmd

F32 = mybir.dt.float32
BF16 = mybir.dt.bfloat16
AF = mybir.ActivationFunctionType
ALU = mybir.AluOpType
AX = mybir.AxisListType

ENGS = ["tensor", "vector", "scalar", "gpsimd", "sync"]
D = 1024
DFF = 2816
NT = 512
KC = 8
NFF = 22
EPS = 1e-6


class Buf:
    def __init__(self, name):
        self.name = name
        self.w = {}
        self.r = {}
        self.dsem = None
        self.dcount = 0
        self.excl = False


class V:
    def __init__(self, ap, bufs):
        self.ap = ap
        self.bufs = list(bufs)


class TT:
    def __init__(self, P, name, shape, dt, split=None, psum=False):
        P.uid += 1
        nm = "t%d_%s" % (P.uid, name)
        self.t = P.psum_t(nm, shape, dt) if psum else P.sbuf_t(nm, shape, dt)
        self.shape = list(shape)
        self.split = split
        n = 1 if split is None else shape[split]
        self.bufs = [Buf("%s.%d" % (name, i)) for i in range(n)]
        for b in self.bufs:
            b.excl = psum

    def __getitem__(self, idx):
        if not isinstance(idx, tuple):
            idx = (idx,)
        if self.split is None or len(idx) <= self.split:
            bufs = self.bufs
        else:
            s = idx[self.split]
            if isinstance(s, slice):
                bufs = self.bufs[s]
            else:
                bufs = [self.bufs[s]]
        return V(self.t[idx], bufs)


class Prog:
    def __init__(self, nc, es, n_dma_sems=90, same_engine_waits=True):
        self.nc = nc
        self.es = es
        self.streams = {e: [] for e in ENGS}
        self.sem = {e: es.enter_context(nc.semaphore("s_" + e)) for e in ENGS}
        self.count = {e: 0 for e in ENGS}
        self.known = {e: {} for e in ENGS}
        self.dma_pool = [es.enter_context(nc.semaphore("d%d" % i)) for i in range(n_dma_sems)]
        self.dma_used = 0
        self.same_engine_waits = same_engine_waits
        self.out_tickets = []
        self.ninstr = {e: 0 for e in ENGS}
        self.alloc = es
        self.uid = 0
        self.dma_latest = {}

    def sbuf_t(self, name, shape, dt):
        return self.alloc.enter_context(self.nc.sbuf_tensor(name, list(shape), dt))

    def psum_t(self, name, shape, dt=F32):
        return self.alloc.enter_context(self.nc.psum_tensor(name, list(shape), dt))

    def _gather(self, eng, reads, writes):
        waits = {}

        def add(d):
            for k, (s, v) in d.items():
                if k not in waits or waits[k][1] < v:
                    waits[k] = (s, v)

        for b in reads:
            add(b.w)
        for b in writes:
            add(b.w)
            add(b.r)
        need = []
        kn = self.known[eng]
        for k, (s, v) in waits.items():
            if k == eng:
                if eng == "tensor" or not self.same_engine_waits:
                    continue
                v = min(v, self.count[eng])
                if v <= 0:
                    continue
            if kn.get(k, 0) < v:
                kn[k] = v
                need.append((s, v))
        return need

    def _record(self, key, ticket, reads, writes):
        for b in writes:
            b.w = {key: ticket}
            b.r = {}
        for b in reads:
            if b in writes:
                continue
            old = b.r.get(key)
            if old is None or old[1] < ticket[1]:
                b.r[key] = ticket

    def op(self, eng, fn, reads=(), writes=(), signal=True):
        reads = list(reads)
        writes = list(writes)
        if eng != "tensor":
            writes = writes + [b for b in reads if b.excl and b not in writes]
            reads = [b for b in reads if not b.excl]
        need = self._gather(eng, reads, writes)
        sem = self.sem[eng]
        if signal:
            self.count[eng] += 1
            val = self.count[eng]
        else:
            val = self.count[eng] + 1

        def run(e, need=need, fn=fn, signal=signal, sem=sem):
            for s, v in need:
                e.wait_ge(s, v)
            ins = fn(e)
            if signal:
                ins.then_inc(sem, 1)

        self.streams[eng].append(run)
        self.ninstr[eng] += 1
        self._record(eng, (sem, val), reads, writes)

    def _dsem(self, primary):
        if primary.dsem is None:
            primary.dsem = self.dma_pool[self.dma_used]
            primary.dkey = "dma%d" % self.dma_used
            self.dma_used += 1

    def dma(self, queue, out, in_, is_output=False, extra_reads=(), **kw):
        reads = list(in_.bufs) + list(extra_reads)
        writes = list(out.bufs)
        primary = writes[0] if writes else reads[0]
        self._dsem(primary)
        need = self._gather(queue, reads, writes)
        primary.dcount += 16
        sem = primary.dsem
        ticket = (sem, primary.dcount)
        out_ap, in_ap = out.ap, in_.ap

        def run(e, need=need, sem=sem, out_ap=out_ap, in_ap=in_ap, kw=kw):
            for s, v in need:
                e.wait_ge(s, v)
            e.dma_start(out=out_ap, in_=in_ap, **kw).then_inc(sem, 16)

        self.streams[queue].append(run)
        self.ninstr[queue] += 1
        self._record(primary.dkey, ticket, reads, writes)
        self.dma_latest[primary.dkey] = ticket
        if is_output:
            self.out_tickets.append((primary.dkey, ticket))

    def custom(self, eng, fn, reads, writes, sem_buf, inc=1):
        reads = list(reads)
        writes = list(writes)
        self._dsem(sem_buf)
        need = self._gather(eng, reads, writes)
        sem_buf.dcount += inc
        sem = sem_buf.dsem
        ticket = (sem, sem_buf.dcount)

        def run(e, need=need, sem=sem, fn=fn, inc=inc):
            for s, v in need:
                e.wait_ge(s, v)
            fn(e).then_inc(sem, inc)

        self.streams[eng].append(run)
        self._record(sem_buf.dkey, ticket, reads, writes)
        self.dma_latest[sem_buf.dkey] = ticket

    def mm(self, out, lhsT, rhs, start=True, stop=True, signal=None, **kw):
        if signal is None:
            signal = stop
        self.op("tensor", lambda e: e.matmul(out.ap, lhsT=lhsT.ap, rhs=rhs.ap, start=start, stop=stop, **kw),
                reads=lhsT.bufs + rhs.bufs, writes=out.bufs, signal=signal)

    def transpose(self, out, in_, ident, signal=True):
        self.op("tensor", lambda e: e.transpose(out.ap, in_.ap, ident.ap),
                reads=in_.bufs + ident.bufs, writes=out.bufs, signal=signal)

    def act(self, out, in_, func, scale=1.0, bias=None, eng="scalar"):
        reads = list(in_.bufs)
        kw = {}
        if isinstance(scale, V):
            reads += scale.bufs
            kw["scale"] = scale.ap
        else:
            kw["scale"] = scale
        if isinstance(bias, V):
            reads += bias.bufs
            kw["bias"] = bias.ap
        elif bias is not None:
            kw["bias"] = bias
        self.op(eng, lambda e: e.activation(out=out.ap, in_=in_.ap, func=func, **kw), reads=reads, writes=out.bufs)

    def tt(self, out, a, b, op, eng="vector"):
        self.op(eng, lambda e: e.tensor_tensor(out=out.ap, in0=a.ap, in1=b.ap, op=op),
                reads=a.bufs + b.bufs, writes=out.bufs)

    def ts(self, out, a, s1, op0, s2=None, op1=None, eng="vector"):
        reads = list(a.bufs)
        s1a = s1
        s2a = s2
        if isinstance(s1, V):
            reads += s1.bufs
            s1a = s1.ap
        if isinstance(s2, V):
            reads += s2.bufs
            s2a = s2.ap
        kw = {}
        if op1 is not None:
            kw = dict(scalar2=s2a, op1=op1)
        else:
            kw = dict(scalar2=None)
        self.op(eng, lambda e: e.tensor_scalar(out=out.ap, in0=a.ap, scalar1=s1a, op0=op0, **kw),
                reads=reads, writes=out.bufs)

    def stt(self, out, a, s, b, op0, op1):
        reads = a.bufs + b.bufs
        sa = s
        if isinstance(s, V):
            reads = reads + s.bufs
            sa = s.ap
        self.op("vector", lambda e: e.scalar_tensor_tensor(out=out.ap, in0=a.ap, scalar=sa, in1=b.ap, op0=op0, op1=op1),
                reads=reads, writes=out.bufs)

    def copy(self, out, in_, eng="vector"):
        if eng == "scalar":
            self.act(out, in_, AF.Copy)
        else:
            self.op(eng, lambda e: e.tensor_copy(out=out.ap, in_=in_.ap), reads=in_.bufs, writes=out.bufs)

    def memset(self, out, val, eng="vector"):
        self.op(eng, lambda e: e.memset(out.ap, val), writes=out.bufs)

    def barrier(self, engs=("tensor", "vector", "scalar", "gpsimd"), queues=("sync",)):
        for e in tuple(engs) + tuple(queues):
            need = []
            for dk, (s_, v_) in self.dma_latest.items():
                if self.known[e].get(dk, 0) < v_:
                    self.known[e][dk] = v_
                    need.append((s_, v_))
            for o in engs:
                if o == e:
                    continue
                v = self.count[o]
                if v > 0 and self.known[e].get(o, 0) < v:
                    self.known[e][o] = v
                    need.append((self.sem[o], v))

            def run(eng, need=need):
                for s_, v_ in need:
                    eng.wait_ge(s_, v_)

            self.streams[e].append(run)

    def finish(self):
        final = {}
        for k, (s, v) in self.out_tickets:
            if k not in final or final[k][1] < v:
                final[k] = (s, v)
        fl = list(final.values())

        def run(e, fl=fl):
            for s, v in fl:
                e.wait_ge(s, v)

        self.streams["sync"].append(run)
        with self.nc.Block() as block:
            for name in ENGS:
                stream = self.streams[name]

                def body(e, stream=stream):
                    for f in stream:
                        f(e)

                getattr(block, name)(body)


class WeightRing:
    def __init__(self, P, nslots, slot_elems):
        self.P = P
        self.slots = [TT(P, "wslot%d" % i, [128, slot_elems], BF16) for i in range(nslots)]
        self.n = nslots
        self.pieces = []
        self.issued = 0
        self.taken = 0

    def plan(self, pieces):
        self.pieces += pieces

    def _issue(self, i):
        slot = self.slots[i % self.n]
        off = 0
        for (src_ap, shape) in self.pieces[i]:
            ne = int(np.prod(shape))
            dst = slot.t[:, off:off + ne]
            if len(shape) == 2:
                dst = dst.rearrange("p (a b) -> p a b", b=shape[1])
            self.P.dma("gpsimd", V(dst, slot.bufs), V(src_ap, []))
            off += ne

    def next(self):
        i = self.taken
        while self.issued < min(len(self.pieces), i + self.n):
            self._issue(self.issued)
            self.issued += 1
        self.taken += 1
        slot = self.slots[i % self.n]
        views = []
        off = 0
        for (src_ap, shape) in self.pieces[i]:
            ne = int(np.prod(shape))
            t = slot.t[:, off:off + ne]
            if len(shape) == 2:
                t = t.rearrange("p (a b) -> p a b", b=shape[1])
            views.append(V(t, slot.bufs))
            off += ne
        return views


def build_program(stage=99):
    nc = bass.Bass("TRN2", target_bir_lowering=False)

    def din(name, shape, dt=F32):
        return nc.dram_tensor(name, list(shape), dt, kind="ExternalInput").ap()

    def dout(name, shape, dt=F32):
        return nc.dram_tensor(name, list(shape), dt, kind="ExternalOutput").ap()

    x_d = [din("xs", [NT, D]), din("xp", [NT, D])]
    csel_d = din("csel", [128, KC, 2])
    wada_d = din("w_ada", [D, 9 * D])
    bada_d = din("b_ada", [128, 72])
    normg_d = din("norm_g", [128, 3, KC])
    finalg_d = din("final_g", [128, KC])
    ident_d = din("ident", [128, 128])
    ffn_d = [[din("ffn%d_w%d" % (f, k), [D, DFF] if k != 2 else [DFF, D]) for k in (1, 3, 2)] for f in (1, 2)]
    win_d = din("w_in", [D, 5280])
    wkrx_d = din("w_krx", [D, 128])
    wqbx_d = din("w_qbx", [384, 1024])
    wkvbx_d = din("w_kvbx", [256, 1024])
    wua_d = din("w_up_attn", [512, D])
    wur_d = din("w_up_rec_p", [512, D])
    whg_d = din("w_hgate_p", [D, 512])
    wout_d = din("w_out", [D, D])
    rope_d = din("rope", [128, NT])
    maskT_d = din("maskT", [128, 2, 512])
    scanm_d = din("scanm", [128, NT])
    rowm_d = din("rowm", [128, 2])
    bdiag_d = din("bdiag", [128, 128])
    hgam_d = din("hgam", [128, 2, 2, 4])
    hgng_d = din("hgng", [128, 1])
    qng_d = din("qng", [128, 3])
    kvng_d = din("kvng", [128, 2])
    sel_d = din("sel", [128, 8])
    cckv_d = din("cckv", [256, 256])
    ckr_d = din("ckr", [256, 32])
    s0_d = din("s0", [2, 8, 64, 64])
    cc1_in = nc.dram_tensor("cc1_in", [128, 1536], BF16)
    cc1_out = nc.dram_tensor("cc1_out", [512, 1536], BF16)
    cc2_in = nc.dram_tensor("cc2_in", [128, 520], F32)
    cc2_out = nc.dram_tensor("cc2_out", [512, 520], F32)
    y_d = [dout("ys", [NT, D]), dout("yp", [NT, D])]
    nckv_d = dout("nckv", [NT, 256])
    nkr_d = dout("nkr", [NT, 32])
    nst_d = dout("nst", [2, 2, 8, 64, 64])

    with ExitStack() as es:
        P = Prog(nc, es)
        ring = WeightRing(P, 3, 6144)

        ident = TT(P, "ident", [128, 128], F32)
        P.dma("sync", ident[:], V(ident_d[:, :], []))
        ones_bf = TT(P, "ones_bf", [128, 128], BF16)
        P.memset(ones_bf[:], 1.0)
        epsb = TT(P, "epsb", [128, 1], F32)
        P.memset(epsb[:], EPS)
        xT = [TT(P, "xT%d" % g, [128, KC, NT], F32, split=1) for g in range(2)]
        hT = [TT(P, "hT%d" % g, [128, KC, NT], BF16, split=1) for g in range(2)]
        banks = [TT(P, "bank%d" % i, [128, 512], F32, psum=True) for i in range(8)]
        bank_rr = [0]

        def bank():
            b = banks[bank_rr[0] % 8]
            bank_rr[0] += 1
            return b

        normg = TT(P, "normg", [128, 3, KC], F32)
        P.dma("sync", normg[:], V(normg_d[:, :, :], []))
        finalg = TT(P, "finalg", [128, KC], F32)
        P.dma("sync", finalg[:], V(finalg_d[:, :], []))
        bada = TT(P, "bada", [128, 72], F32)
        P.dma("sync", bada[:], V(bada_d[:, :], []))
        csel = TT(P, "csel", [128, KC, 2], F32)
        P.dma("sync", csel[:], V(csel_d[:, :, :], []))
        scb = TT(P, "scb", [128, KC, 2], BF16)
        P.act(scb[:], csel[:], AF.Silu)
        mod = TT(P, "mod", [128, 72, 2], F32)

        wada_v = wada_d.rearrange("(kc p) n -> p kc n", p=128)

        def ada_piece(i):
            return [[(wada_v[:, :, i * D + h * 512: i * D + (h + 1) * 512], [KC, 512])] for h in range(2)]

        def ffn_pieces(f):
            w1, w3, w2 = ffn_d[f]
            w1v = w1.rearrange("(kc p) n -> p kc n", p=128)
            w3v = w3.rearrange("(kc p) n -> p kc n", p=128)
            w2v = w2.rearrange("(kc p) n -> p kc n", p=128)
            ps = []
            for j in range(NFF // 2):
                ps.append([(w1v[:, :, j * 256:(j + 1) * 256], [KC, 256]), (w3v[:, :, j * 256:(j + 1) * 256], [KC, 256])])
            for j in range(4):
                ps.append([(w2v[:, :, j * 256:(j + 1) * 256], [NFF, 256])])
            return ps

        ring.plan(ada_piece(0) + ada_piece(1))
        ring.plan(ffn_pieces(0)[:11])
        for i in range(2, 9):
            ring.plan(ada_piece(i))
        ring.plan(ffn_pieces(0)[11:])
        winv = win_d.rearrange("(kc p) n -> p kc n", p=128)
        wuav = wua_d.rearrange("(kc p) n -> p kc n", p=128)
        wurv = wur_d.rearrange("(kc p) n -> p kc n", p=128)
        woutv = wout_d.rearrange("(kc p) n -> p kc n", p=128)
        if stage >= 5:
            KSUB = int(os.environ.get("KSUB", "99"))
            KGRP = os.environ.get("KGRP", "10")
            for g in [int(ch) for ch in KGRP]:
                ring.plan([[(winv[:, :, 0:640], [KC, 640])]])
                if KSUB >= 2:
                    whgv = whg_d.rearrange("(kc p) n -> p kc n", p=128)
                    for c0 in (672, 2208, 2720, 1184, 1696):
                        if c0 == 2720:
                            ring.plan([[(whgv[:, :, :], [KC, 512])]])
                        else:
                            ring.plan([[(winv[:, :, c0:c0 + 512], [KC, 512])]])
                for j in range(4 if KSUB >= 5 else 0):
                    ring.plan([[(winv[:, :, 3232 + j * 256:3232 + (j + 1) * 256], [KC, 256]),
                                (winv[:, :, 4256 + j * 256:4256 + (j + 1) * 256], [KC, 256]),
                                (wuav[:, :, j * 256:(j + 1) * 256], [4, 256]),
                                (wurv[:, :, j * 256:(j + 1) * 256], [4, 256])]])
                for j in range(2 if KSUB >= 5 else 0):
                    ring.plan([[(woutv[:, :, j * 512:(j + 1) * 512], [KC, 512])]])
        ring.plan(ffn_pieces(1))

        with ExitStack() as ph:
            P.alloc = ph
            xtok = [TT(P, "xtok%d" % i, [128, D], F32) for i in range(2)]
            for g in range(2):
                for tb in range(4):
                    xt = xtok[(g * 4 + tb) % 2]
                    P.dma("sync", xt[:], V(x_d[g][tb * 128:(tb + 1) * 128, :], []))
                    for half in range(2):
                        bk = bank()
                        for q in range(4):
                            c = half * 4 + q
                            P.transpose(bk[:, q * 128:(q + 1) * 128], xt[:, c * 128:(c + 1) * 128], ident[:], signal=(q == 3))
                        P.copy(xT[g][:, half * 4:half * 4 + 4, tb * 128:(tb + 1) * 128],
                               V(bk.t[:, :].rearrange("p (q t) -> p q t", t=128), bk.bufs),
                               eng="scalar" if half else "vector")
            P.barrier()
            P.alloc = es

        def ada_compute(i):
            bk = bank()
            for h in range(2):
                (wv,) = ring.next()
                for cc in range(4):
                    c = h * 4 + cc
                    for kc in range(KC):
                        P.mm(bk[:, c * 2:c * 2 + 2], wv_slice(wv, kc, cc), scb[:, kc, :], start=(kc == 0), stop=(kc == KC - 1))
            P.tt(mod[:, i * 8:(i + 1) * 8, :], V(bk.t[:, 0:16].rearrange("p (c g) -> p c g", g=2), bk.bufs),
                 V(bada.t[:, i * 8:(i + 1) * 8].unsqueeze(2).to_broadcast([128, 8, 2]), bada.bufs), ALU.add)

        def wv_slice(wv, kc, cc):
            return V(wv.ap[:, kc, cc * 128:(cc + 1) * 128], wv.bufs)

        nscr = TT(P, "nscr", [128, 2, NT], BF16, split=1)
        nscr_f = TT(P, "nscr_f", [128, 2, NT], F32, split=1)
        amod = TT(P, "amod", [128, KC], F32)
        rr = [0]

        def rstd_gen(srcs, nfeat, ones=None, bk=None):
            if bk is None:
                bk = bank()
            if ones is None:
                ones = ones_bf[:]
            for c, sv in enumerate(srcs):
                s = nscr[:, rr[0] % 2, :]
                rr[0] += 1
                P.act(s, sv, AF.Square)
                P.mm(bk[:, :], ones, s, start=(c == 0), stop=(c == len(srcs) - 1), signal=True)
            P.act(bk[:, :], bk[:, :], AF.Ln, scale=1.0 / nfeat, bias=epsb[:, 0:1])
            P.act(bk[:, :], bk[:, :], AF.Exp, scale=-0.5)
            return bk

        def rstd_of(g):
            return rstd_gen([xT[g][:, c, :] for c in range(KC)], D)

        def norm_mod(g, ni, i_shift, i_scale):
            P.ts(amod[:], mod[:, i_scale * 8:(i_scale + 1) * 8, g], 1.0, ALU.add)
            P.tt(amod[:], amod[:], normg[:, ni, :], ALU.mult)
            bk = rstd_of(g)
            for c in range(KC):
                s = nscr_f[:, rr[0] % 2, :]
                rr[0] += 1
                P.tt(s, xT[g][:, c, :], bk[:, :], ALU.mult)
                P.act(hT[g][:, c, :], s, AF.Identity, scale=amod[:, c:c + 1], bias=mod[:, i_shift * 8 + c, g:g + 1])

        def ffn(i_gate, actT):
            gs = TT(P, "gs%d" % i_gate, [128, KC, 2], F32)
            for j in range(NFF // 2):
                w1v, w3v = ring.next()
                for cc in range(2):
                    ch = j * 2 + cc
                    for g in range(2):
                        ba, bb = bank(), bank()
                        for kc in range(KC):
                            P.mm(ba[:, :], wv_slice(w1v, kc, cc), hT[g][:, kc, :], start=(kc == 0), stop=(kc == KC - 1))
                        for kc in range(KC):
                            P.mm(bb[:, :], wv_slice(w3v, kc, cc), hT[g][:, kc, :], start=(kc == 0), stop=(kc == KC - 1))
                        s = nscr_f[:, rr[0] % 2, :]
                        rr[0] += 1
                        P.act(s, ba[:, :], AF.Silu)
                        P.tt(actT[g][:, ch, :], s, bb[:, :], ALU.mult)
            return gs

        def ffn_down(i_gate, actT, gs):
            P.ts(gs[:], mod[:, i_gate * 8:(i_gate + 1) * 8, :], 0.5, ALU.mult)
            for j in range(4):
                (w2v,) = ring.next()
                for cc in range(2):
                    dc = j * 2 + cc
                    for g in range(2):
                        bk = bank()
                        for kc in range(NFF):
                            P.mm(bk[:, :], wv_slice(w2v, kc, cc), actT[g][:, kc, :], start=(kc == 0), stop=(kc == NFF - 1))
                        P.stt(xT[g][:, dc, :], bk[:, :], gs[:, dc, g:g + 1], xT[g][:, dc, :], ALU.mult, ALU.add)

        if stage >= 1:
            ada_compute(0)
            ada_compute(1)
        with ExitStack() as ph:
            P.alloc = ph
            actT = [TT(P, "actT%d" % g, [128, NFF, NT], BF16, split=1) for g in range(2)]
            if stage >= 2:
                for g in range(2):
                    norm_mod(g, 0, 0, 1)
            if stage >= 3:
                gs = ffn(2, actT)
                for i in range(2, 9):
                    ada_compute(i)
                ffn_down(2, actT, gs)
            P.barrier()
            P.alloc = es

        def mixer():
            ph = P.alloc
            QKS = (64 + 32) ** -0.5
            rope = TT(P, "rope", [128, NT], F32)
            P.dma("sync", rope[:], V(rope_d[:, :], []))
            maskT = TT(P, "maskT", [128, 2, 512], BF16)
            P.dma("gpsimd", maskT[:], V(maskT_d[:, :, :], []))
            scanm = TT(P, "scanm", [128, NT], F32)
            P.dma("sync", scanm[:], V(scanm_d[:, :], []))
            rowm = TT(P, "rowm", [128, 2], F32)
            P.dma("sync", rowm[:], V(rowm_d[:, :], []))
            bdiag = TT(P, "bdiag", [128, 128], BF16)
            P.dma("gpsimd", bdiag[:], V(bdiag_d[:, :], []))
            hgam = TT(P, "hgam", [128, 2, 2, 4], F32)
            P.dma("sync", hgam[:], V(hgam_d[:, :, :, :], []))
            hgng = TT(P, "hgng", [128, 1], F32)
            P.dma("sync", hgng[:], V(hgng_d[:, :], []))
            qng = TT(P, "qng", [128, 3], F32)
            P.dma("sync", qng[:], V(qng_d[:, :], []))
            kvng = TT(P, "kvng", [128, 2], F32)
            P.dma("sync", kvng[:], V(kvng_d[:, :], []))
            sel = TT(P, "sel", [128, 8], F32)
            P.dma("sync", sel[:], V(sel_d[:, :], []))
            lb = TT(P, "lb", [128, 2, 4], F32)
            oml = TT(P, "oml", [128, 2, 4], F32)
            P.tt(lb[:], hgam[:, :, 0, :], hgam[:, :, 1, :], ALU.subtract)
            P.act(lb[:], lb[:], AF.Sigmoid)
            P.ts(oml[:], lb[:], -1.0, ALU.mult, 1.0, ALU.add)
            g5 = TT(P, "g5", [128, KC, 2], F32)
            P.copy(g5[:], mod[:, 40:48, :])

            tmpf = TT(P, "tmpf", [128, 4, NT], F32, split=1)
            tr = [0]

            @contextmanager
            def phase():
                outer = P.alloc
                with ExitStack() as st_:
                    P.alloc = st_
                    yield
                    P.barrier()
                P.alloc = outer

            def tmp():
                t = tmpf[:, tr[0] % 4, :]
                tr[0] += 1
                return t

            def proj_chunks(wv, ncols_chunks, g, fn):
                for ci in range(ncols_chunks):
                    bk = bank()
                    for kc in range(KC):
                        P.mm(bk[:, :], V(wv.ap[:, kc, ci * 128:(ci + 1) * 128], wv.bufs), hT[g][:, kc, :],
                             start=(kc == 0), stop=(kc == KC - 1))
                    fn(ci, bk)

            def group(g):
                sample = (g == 0)
                with phase():
                    norm_mod(g, 1, 3, 4)
                    Q = TT(P, "Q", [128, 8, NT], BF16, split=1)
                    ckvT = TT(P, "ckvT", [128, 2, NT], BF16)
                    krb = TT(P, "krb", [128, NT], BF16)
                    P.memset(krb[:], 0.0)
                    ohT = TT(P, "ohT", [128, 4, NT], F32, split=1)
                    hgs = TT(P, "hgs", [128, 4, NT], BF16, split=1)
                    oaT = TT(P, "oaT", [128, 4, NT], BF16, split=1)
                    orT = TT(P, "orT", [128, 4, NT], BF16, split=1)
                    Qf = [TT(P, "Qf%d" % d, [128, 4, NT], BF16, split=1) for d in range(2)] if sample else None
                    with phase():
                        front(g, sample, Q, ckvT, krb)
                    if KSUB >= 2:
                        with phase():
                            hgrn(g, sample, ohT, hgs, Qf)
                    if KSUB >= 3:
                        with phase():
                            attention(g, sample, Q, ckvT, krb, oaT)
                    if KSUB >= 4:
                        with phase():
                            if sample:
                                hgrn_fix(ohT, Qf)
                            hgrn_norm(ohT, hgs, orT)
                    if KSUB >= 5:
                        with phase():
                            merge(g, oaT, orT)

            def front(g, sample, Q, ckvT, krb):
                wkrx = TT(P, "wkrx", [128, KC, 128], BF16)
                P.dma("gpsimd", wkrx[:], V(wkrx_d.rearrange("(kc p) n -> p kc n", p=128), []))
                wqbx = TT(P, "wqbx", [128, 3, 1024], BF16)
                P.dma("gpsimd", wqbx[:], V(wqbx_d.rearrange("(kc p) n -> p kc n", p=128), []))
                qkT = TT(P, "qkT", [128, 5, NT], F32, split=1)
                (w1v,) = ring.next()
                proj_chunks(w1v, 5, g, lambda ci, bk: P.copy(qkT[:, ci, :], bk[:, :], eng="scalar"))
                qnT = TT(P, "qnT", [128, 3, NT], BF16)
                bk = rstd_gen([qkT[:, c, :] for c in range(3)], 384)
                for c in range(3):
                    P.stt(qnT[:, c, :], qkT[:, c, :], qng[:, c:c + 1], bk[:, :], ALU.mult, ALU.mult)
                bk = rstd_gen([qkT[:, 3 + c, :] for c in range(2)], 256)
                for c in range(2):
                    P.stt(qkT[:, 3 + c, :], qkT[:, 3 + c, :], kvng[:, c:c + 1], bk[:, :], ALU.mult, ALU.mult)
                    P.copy(ckvT[:, c, :], qkT[:, 3 + c, :], eng="scalar")
                krf = TT(P, "krf", [128, NT], F32)
                bk = bank()
                for kc in range(KC):
                    P.mm(bk[:, :], wkrx[:, kc, :], hT[g][:, kc, :], start=(kc == 0), stop=(kc == KC - 1))
                P.copy(krf[:], bk[:, :], eng="scalar")
                if not sample:
                    otok = TT(P, "otok", [128, 2, 288], F32, split=1)
                    for tb in range(4):
                        bk = bank()
                        for c in range(2):
                            P.transpose(bk[:, c * 128:(c + 1) * 128], qkT[:, 3 + c, tb * 128:(tb + 1) * 128], ident[:], signal=False)
                        P.transpose(bk[:, 256:288], krf[0:32, tb * 128:(tb + 1) * 128], ident[0:32, 0:32], signal=True)
                        P.copy(otok[:, tb % 2, :], bk[:, 0:288])
                        P.dma("sync", V(nckv_d[tb * 128:(tb + 1) * 128, :], []), otok[:, tb % 2, 0:256], is_output=True)
                        P.dma("sync", V(nkr_d[tb * 128:(tb + 1) * 128, :], []), otok[:, tb % 2, 256:288], is_output=True)
                    P.copy(krb[64:96, :], krf[64:96, :])
                else:
                    t1 = tmp()
                    t2 = tmp()
                    P.tt(V(t1.ap[64:96, :], t1.bufs), krf[64:96, :], rope[64:96, :], ALU.mult)
                    P.tt(V(t2.ap[64:96, :], t2.bufs), krf[96:128, :], rope[96:128, :], ALU.mult)
                    P.tt(krb[64:96, :], V(t1.ap[64:96, :], t1.bufs), V(t2.ap[64:96, :], t2.bufs), ALU.add)
                for h in range(8):
                    bk = bank()
                    for kc in range(3):
                        P.mm(bk[:, :], wqbx[:, kc, h * 128:(h + 1) * 128], qnT[:, kc, :], start=(kc == 0), stop=(kc == 2))
                    if sample:
                        P.copy(Q[0:64, h, :], bk[0:64, :], eng="scalar")
                        t1 = tmp()
                        t2 = tmp()
                        P.tt(V(t1.ap[64:96, :], t1.bufs), bk[64:96, :], rope[64:96, :], ALU.mult)
                        P.tt(V(t2.ap[64:96, :], t2.bufs), bk[96:128, :], rope[96:128, :], ALU.mult)
                        P.tt(Q[64:96, h, :], V(t1.ap[64:96, :], t1.bufs), V(t2.ap[64:96, :], t2.bufs), ALU.add)
                    else:
                        P.copy(Q[0:96, h, :], bk[0:96, :], eng="scalar")
                if sample:
                    P.dma("sync", V(cc1_in[:, 0:1024].rearrange("p (c t) -> p c t", t=NT), [b_cc1i]), ckvT[:])
                    P.dma("sync", V(cc1_in[:, 1024:1536], [b_cc1i]), krb[:, :])
                    P.custom("gpsimd", lambda e: e.collective_compute("AllGather", ALU.bypass, replica_groups=[[0, 1, 2, 3], [4, 5, 6, 7]],
                                                                     ins=[cc1_in.ap().opt()], outs=[cc1_out.ap().opt()]),
                             reads=[b_cc1i], writes=[b_cc1o], sem_buf=b_cc1o)

            def attention(g, sample, Q, ckvT, krb, oaT):
                nk = 2304 if sample else NT
                nkt = nk // 128
                wkvbx = TT(P, "wkvbx", [128, 2, 1024], BF16)
                P.dma("gpsimd", wkvbx[:], V(wkvbx_d.rearrange("(kc p) n -> p kc n", p=128), []))
                if sample:
                    krT = TT(P, "krT", [128, nk], BF16)
                    ckvA = TT(P, "ckvA", [128, 2, nk], BF16)
                    ctok = TT(P, "ctok", [128, 2, 288], F32)
                    for t2_ in range(2):
                        P.dma("sync", ctok[:, t2_, 0:256], V(cckv_d[t2_ * 128:(t2_ + 1) * 128, :], []))
                        P.dma("sync", ctok[:, t2_, 256:288], V(ckr_d[t2_ * 128:(t2_ + 1) * 128, :], []))
                    for t2_ in range(2):
                        bk = bank()
                        for c in range(2):
                            P.transpose(bk[:, c * 128:(c + 1) * 128], ctok[:, t2_, c * 128:(c + 1) * 128], ident[:], signal=False)
                        P.transpose(bk[0:32, 256:384], ctok[:, t2_, 256:288], ident[:], signal=True)
                        P.copy(ckvA[:, :, t2_ * 128:(t2_ + 1) * 128], V(bk.t[:, 0:256].rearrange("p (c t) -> p c t", t=128), bk.bufs))
                        P.copy(krT[64:96, t2_ * 128:(t2_ + 1) * 128], bk[0:32, 256:384], eng="scalar")
                    for r in range(4):
                        P.dma("sync", V(ckvA.t[:, :, 256 + r * NT:256 + (r + 1) * NT], ckvA.bufs),
                              V(cc1_out[r * 128:(r + 1) * 128, 0:1024].rearrange("p (c t) -> p c t", t=NT), [b_cc1o]))
                        P.dma("sync", V(krT.t[64:96, 256 + r * NT:256 + (r + 1) * NT], krT.bufs),
                              V(cc1_out[r * 128 + 64:r * 128 + 96, 1024:1536], [b_cc1o]))
                else:
                    krT, ckvA = krb, ckvT
                Vt = TT(P, "Vt", [128, nkt, 512], BF16, split=1)
                for kt in range(nkt):
                    bk = bank()
                    for kc in range(2):
                        P.mm(bk[:, :], ckvA[:, kc, kt * 128:(kt + 1) * 128], wkvbx[:, kc, 512:1024], start=(kc == 0), stop=(kc == 1))
                    P.copy(Vt[:, kt, :], bk[:, :], eng="scalar" if kt % 2 else "vector")
                KnT = TT(P, "KnT", [96, 2, nk], BF16, split=1)
                pT = TT(P, "pT", [128, 3, NT], BF16, split=1)
                pr = [0]
                rs = TT(P, "rs", [64, 2, NT], F32, split=1)
                blocks = [(0, NT, list(range(nkt)))] if sample else [(0, 256, [0, 1]), (256, 512, [2, 3])]
                sb_i = [0]
                for h in range(8):
                    kn = KnT[:, h % 2, :]
                    n0 = 0
                    while n0 < nk:
                        n1 = min(nk, n0 + 512)
                        bk = bank4()
                        for kc in range(2):
                            P.mm(bk[0:64, 0:n1 - n0], wkvbx[:, kc, h * 64:(h + 1) * 64], ckvA[:, kc, n0:n1], start=(kc == 0), stop=(kc == 1))
                        P.copy(V(kn.ap[0:64, n0:n1], kn.bufs), bk[0:64, 0:n1 - n0], eng="scalar" if (n0 // 512) % 2 else "vector")
                        n0 = n1
                    P.copy(V(kn.ap[64:96, :], kn.bufs), krT[64:96, 0:nk], eng="gpsimd")
                    for (q0, q1, kts) in blocks:
                        nq = q1 - q0
                        o_ps = banks[4 + (sb_i[0] % 2) * 2]
                        s_ps = banks[5 + (sb_i[0] % 2) * 2]
                        sb_i[0] += 1
                        for i, kt in enumerate(kts):
                            sc = bank4()
                            P.mm(sc[:, 0:nq], V(kn.ap[:, kt * 128:(kt + 1) * 128], kn.bufs), Q[0:96, h, q0:q1], start=True, stop=True)
                            p = pT[:, pr[0] % 3, 0:nq]
                            pr[0] += 1
                            P.act(p, sc[:, 0:nq], AF.Exp, scale=QKS)
                            last = (i == len(kts) - 1)
                            P.mm(o_ps[0:64, 0:nq], Vt[:, kt, h * 64:(h + 1) * 64], p, start=(i == 0), stop=last, signal=True)
                            P.mm(s_ps[0:64, 0:nq], ones_bf[:, 0:64], p, start=(i == 0), stop=last, signal=True)
                        r = rs[:, h % 2, 0:nq]
                        P.act(r, s_ps[0:64, 0:nq], AF.Ln)
                        P.act(r, r, AF.Exp, scale=-1.0)
                        r0 = (h % 2) * 64
                        P.tt(oaT[r0:r0 + 64, h // 2, q0:q1], o_ps[0:64, 0:nq], r, ALU.mult)

            b4 = [0]

            def bank4():
                b = banks[b4[0] % 4]
                b4[0] += 1
                return b

            def hgrn(g, sample, ohT, hgs, Qf):
                qh = TT(P, "qh", [128, 4, NT], F32, split=1)
                Vt = TT(P, "hV", [128, 4, 512], BF16, split=1)
                Vm = [TT(P, "hVm%d" % i, [128, 4, 512], BF16, split=1) for i in range(2)]
                Qt = [TT(P, "Qt%d" % d, [128, 4, NT], BF16, split=1) for d in range(2)]
                Kt = [TT(P, "Kt%d" % d, [128, 4, NT], BF16, split=1) for d in range(2)]
                Kh = [TT(P, "Kh%d" % d, [128, 4, 512], BF16, split=1) for d in range(2)]
                eg = TT(P, "eg", [128, 2, 4, 32], F32)
                xpay = TT(P, "xpay", [128, 520], F32) if sample else None
                (wv,) = ring.next()
                proj_chunks(wv, 4, g, lambda ci, bk: P.act(qh[:, ci, :], bk[:, :], AF.Silu))
                (wv,) = ring.next()
                HSKIP = os.environ.get("HSKIP", "")
                for tb in range(4):
                    if "v" in HSKIP:
                        break
                    bk = bank()
                    for kc in range(KC):
                        P.mm(bk[:, :], hT[g][:, kc, tb * 128:(tb + 1) * 128], V(wv.ap[:, kc, :], wv.bufs), start=(kc == 0), stop=(kc == KC - 1))
                    P.copy(Vt[:, tb, :], bk[:, :], eng="scalar")
                    if "m" in HSKIP:
                        continue
                    P.ts(Vm[0][:, tb, :], bk[:, :], rowm[:, 0:1], ALU.mult)
                    P.ts(Vm[1][:, tb, :], bk[:, :], rowm[:, 1:2], ALU.mult)
                (wv,) = ring.next()
                proj_chunks(wv, 4, g, lambda ci, bk: P.act(hgs[:, ci, :], bk[:, :], AF.Silu))
                HSUB = int(os.environ.get("HSUB", "99"))
                if HSUB < 2:
                    ring.next()
                    ring.next()
                    return
                with phase():
                    lg = TT(P, "lg", [128, 2, NT], F32, split=1)
                    bb = TT(P, "bb", [128, 1, NT], F32, split=1)
                    ff = TT(P, "ff", [128, 1, NT], F32, split=1)
                    khf = TT(P, "khf", [128, 1, NT], F32, split=1)
                    gch = TT(P, "gch", [128, 2, 32], F32, split=1)
                    tot = TT(P, "tot", [128, 2, 1], F32, split=1)
                    ii = [0]
                    for d in range(2):
                        (wv,) = ring.next()

                        def fgate(j, bk, d=d):
                            k = ii[0] % 2
                            ii[0] += 1
                            f, l, b, kf = ff[:, 0, :], lg[:, k, :], bb[:, 0, :], khf[:, 0, :]
                            gc = gch[:, k, :]
                            P.act(f, bk[:, :], AF.Sigmoid)
                            P.ts(f, f, oml[:, d, j:j + 1], ALU.mult, lb[:, d, j:j + 1], ALU.add)
                            P.act(l, f, AF.Ln)
                            P.ts(f, f, -1.0, ALU.mult, 1.0, ALU.add)
                            P.op("vector", lambda e: e.tensor_tensor_scan(out=b.ap, data0=scanm.t[:, :], data1=l.ap, initial=0.0,
                                                                         op0=ALU.mult, op1=ALU.add),
                                 reads=l.bufs + scanm.bufs, writes=b.bufs)
                            b3 = V(b.ap.rearrange("p (n c) -> p n c", c=16), b.bufs)
                            P.copy(gc, V(b3.ap[:, :, 15], b.bufs))
                            gbc = V(gc.ap.unsqueeze(2).to_broadcast([128, 32, 16]), gc.bufs)
                            if sample:
                                t = tmp()
                                P.op("vector", lambda e, t=t: e.tensor_tensor_scan(out=t.ap, data0=ones_f.t[:, 0:1].to_broadcast([128, NT]), data1=l.ap,
                                                                                  initial=0.0, op0=ALU.mult, op1=ALU.add),
                                     reads=l.bufs + ones_f.bufs, writes=t.bufs)
                                P.copy(xpay[:, 512 + d * 4 + j:512 + d * 4 + j + 1], V(t.ap[:, NT - 1:NT], t.bufs))
                                if d == 1:
                                    tt_ = tot[:, k, :]
                                    P.copy(tt_, V(t.ap[:, NT - 1:NT], t.bufs))
                                    P.stt(t, t, -1.0, l, ALU.mult, ALU.add)
                                    P.ts(t, t, tt_, ALU.add)
                                P.act(t, t, AF.Exp)
                                P.tt(Qf[d][:, j, :], qh[:, j, :], t, ALU.mult)
                            if d == 1:
                                P.tt(b, l, b, ALU.subtract)
                                P.tt(b3, b3, gbc, ALU.add)
                            P.act(eg[:, d, j, :], gc, AF.Exp)
                            t = tmp()
                            P.act(t, b, AF.Exp)
                            P.tt(Qt[d][:, j, :], qh[:, j, :], t, ALU.mult)
                            t = tmp()
                            P.act(t, b, AF.Exp, scale=-1.0)
                            P.tt(Kt[d][:, j, :], f, t, ALU.mult)
                            t = tmp()
                            P.tt(V(t.ap.rearrange("p (n c) -> p n c", c=16), t.bufs), gbc, b3, ALU.subtract)
                            P.act(t, t, AF.Exp)
                            P.tt(kf, f, t, ALU.mult)
                            bk2 = bank()
                            for tb in range(4):
                                P.transpose(bk2[:, tb * 128:(tb + 1) * 128], V(kf.ap[:, tb * 128:(tb + 1) * 128], kf.bufs), ident[:], signal=(tb == 3))
                            P.copy(Kh[d][:, :, j * 128:(j + 1) * 128], V(bk2.t[:, :].rearrange("p (t f) -> p t f", f=128), bk2.bufs), eng="scalar")

                        proj_chunks(wv, 4, g, fgate)
                if HSUB >= 3:
                    _hgrn_scan(g, sample, Vt, Vm, Qt, Kt, Kh, eg, ohT, xpay)

            def hmap(h):
                e = h // 2
                return ((e // 2) if h % 2 == 0 else 2 + e // 2, (e % 2) * 64)

            def _hgrn_scan(g, sample, Vt, Vm, Qt, Kt, Kh, eg, ohT, xpay):
                oacc = banks[0:4]
                ATs = TT(P, "ATs", [128, 4, 512], BF16, split=1)
                ai = [0]
                S = TT(P, "S", [128, 2, 4, 64], F32, split=1)
                Sb = TT(P, "Sb", [128, 2, 512], BF16, split=1)
                P.memset(S[:], 0.0)
                started = {}

                def st(key):
                    if key not in started:
                        started[key] = True
                        return True
                    return False

                for tb in range(4):
                    for d in range(2):
                        ats = []
                        for par in range(2):
                            Ab = banks[6 + par]
                            for idx in range(4):
                                h = idx * 2 + par
                                j, r0 = h // 2, (h % 2) * 64
                                P.mm(Ab[:, idx * 128:(idx + 1) * 128], Kt[d][r0:r0 + 64, j, tb * 128:(tb + 1) * 128],
                                     Qt[d][r0:r0 + 64, j, tb * 128:(tb + 1) * 128], start=True, stop=True, signal=(idx == 3))
                            at = ATs[:, (ai[0] % 2) * 2 + par, :]
                            P.tt(at, Ab[:, :], maskT[:, d, :], ALU.mult)
                            ats.append(at)
                        ai[0] += 1
                        for par in range(2):
                            for idx in range(4):
                                h = idx * 2 + par
                                jj, po = hmap(h)
                                P.mm(oacc[jj][po:po + 64, tb * 128:(tb + 1) * 128], Vt[:, tb, h * 64:(h + 1) * 64],
                                     V(ats[par].ap[:, idx * 128:(idx + 1) * 128], ats[par].bufs), start=st((jj, po)), stop=False,
                                     signal=(idx == 3), skip_group_check=True)
                if int(os.environ.get("HSUB", "99")) < 4:
                    return
                if sample:
                    order = [list(range(32)), list(range(31, -1, -1))]
                    resets = []
                else:
                    order = [list(range(32)), list(range(15, -1, -1)) + list(range(31, 15, -1))]
                    resets = [16]
                zero_state = True
                Sflat = lambda d: V(S.t[:, d, :, :].rearrange("p j v -> p (j v)"), [S.bufs[d]])
                for i in range(32):
                    if i in resets:
                        P.memset(S[:], 0.0)
                        zero_state = True
                    for d in range(2):
                        dSb = banks[4 + d]
                        c = order[d][i]
                        tbk, m, par = c // 8, (c % 8) // 2, c % 2
                        rows = slice(m * 32, m * 32 + 32)
                        for h in range(8):
                            j, r0 = h // 2, (h % 2) * 64
                            P.mm(dSb[r0:r0 + 64, j * 64:(j + 1) * 64], Kh[d][rows, tbk, h * 64:(h + 1) * 64],
                                 Vm[par][rows, tbk, h * 64:(h + 1) * 64], start=True, stop=True,
                                 signal=(h == 7), skip_group_check=True, tile_position=(m * 32, r0))
                    if not zero_state:
                        sb = Sb[:, i % 2, :]
                        P.copy(sb, V(S.t[:, :, :, :].rearrange("p d j v -> p (d j v)"), S.bufs), eng="scalar")
                        for d in range(2):
                            c = order[d][i]
                            for h in range(8):
                                j, r0 = h // 2, (h % 2) * 64
                                jj, po = hmap(h)
                                lastmm = (i == 31 and d == 1)
                                P.mm(oacc[jj][po:po + 64, c * 16:(c + 1) * 16],
                                     V(sb.ap[r0:r0 + 64, (d * 4 + j) * 64:(d * 4 + j + 1) * 64], sb.bufs),
                                     Qt[d][r0:r0 + 64, j, c * 16:(c + 1) * 16], start=False, stop=lastmm,
                                     signal=(lastmm or h == 7), skip_group_check=True)
                    for d in range(2):
                        c = order[d][i]
                        if not zero_state:
                            P.tt(S[:, d, :, :], S[:, d, :, :],
                                 V(eg.t[:, d, :, c:c + 1].to_broadcast([128, 4, 64]), eg.bufs), ALU.mult)
                            P.tt(Sflat(d), Sflat(d), banks[4 + d][:, 0:256], ALU.add)
                        else:
                            P.copy(Sflat(d), banks[4 + d][:, 0:256])
                    zero_state = False
                    if (not sample) and i in (15, 31):
                        sq = i // 16
                        P.dma("sync", V(nst_d[sq].rearrange("d (j hh) k v -> (hh k) d j v", hh=2), []), S[:], is_output=True)
                for j in range(4):
                    P.copy(ohT[:, j, :], oacc[j][:, :], eng="scalar" if j % 2 else "vector")
                if sample:
                    P.copy(xpay[:, 0:512], V(S.t[:, :, :, :].rearrange("p d j v -> p (d j v)"), S.bufs))
                    P.dma("sync", V(cc2_in[:, :], [b_cc2i]), xpay[:])
                    P.custom("gpsimd", lambda e: e.collective_compute("AllGather", ALU.bypass, replica_groups=[[0, 1, 2, 3], [4, 5, 6, 7]],
                                                                     ins=[cc2_in.ap().opt()], outs=[cc2_out.ap().opt()]),
                             reads=[b_cc2i], writes=[b_cc2o], sem_buf=b_cc2o)

            def hgrn_fix(ohT, Qf):
                gath = TT(P, "gath", [128, 4, 520], F32)
                P.dma("sync", gath[:], V(cc2_out.ap().rearrange("(r p) f -> p r f", p=128), [b_cc2o]))
                Sin = TT(P, "Sin", [128, 2, 4, 64], F32)
                P.dma("sync", Sin[:], V(s0_d.rearrange("d (j hh) k v -> (hh k) d j v", hh=2), []))
                egr = TT(P, "egr", [128, 4, 8], F32)
                P.act(egr[:], gath[:, :, 512:520], AF.Exp)
                t1 = TT(P, "sfx1", [128, 4, 64], F32)
                for d in range(2):
                    rs_ = [0, 1, 2] if d == 0 else [3, 2, 1]
                    for r in rs_:
                        sd = S_d = Sin[:, d, :, :]
                        P.tt(t1[:], sd, V(egr.t[:, r, d * 4:(d + 1) * 4].unsqueeze(2).to_broadcast([128, 4, 64]), egr.bufs), ALU.mult)
                        P.tt(t1[:], t1[:], V(gath.t[:, r, d * 256:(d + 1) * 256].rearrange("p (j v) -> p j v", v=64), gath.bufs), ALU.add)
                        P.tt(t1[:], t1[:], sd, ALU.subtract)
                        P.stt(sd, t1[:], sel[:, d * 4 + r:d * 4 + r + 1], sd, ALU.mult, ALU.add)
                Sinb = TT(P, "Sinb", [128, 512], BF16)
                P.copy(Sinb[:], V(Sin.t[:, :, :, :].rearrange("p d j v -> p (d j v)"), Sin.bufs))
                for d in range(2):
                    for h in range(8):
                        j, r0 = h // 2, (h % 2) * 64
                        jj, po = hmap(h)
                        P.mm(banks[jj][po:po + 64, :], Sinb[r0:r0 + 64, (d * 4 + j) * 64:(d * 4 + j + 1) * 64], Qf[d][r0:r0 + 64, j, :],
                             start=(d == 0), stop=(d == 1), signal=(d == 1), skip_group_check=True)
                for jj in range(4):
                    P.tt(ohT[:, jj, :], ohT[:, jj, :], banks[jj][:, :], ALU.add)

            def hgrn_norm(ohT, hgs, orT):
                for j in range(4):
                    bk = rstd_gen([ohT[:, j, :]], 64, ones=bdiag[:], bk=bank4())
                    t = tmp()
                    P.tt(t, ohT[:, j, :], bk[:, :], ALU.mult)
                    P.stt(orT[:, j, :], t, hgng[:, 0:1], hgs[:, j, :], ALU.mult, ALU.mult)

            def merge(g, oaT, orT):
                mT = TT(P, "mT", [128, KC, NT], BF16, split=1)
                for jp in range(4):
                    wga, wgr, wua, wur = ring.next()
                    for cc in range(2):
                        dc = jp * 2 + cc
                        b_ga, b_gr, b_ua, b_ur = bank(), bank(), bank(), bank()
                        for kc in range(KC):
                            P.mm(b_ga[:, :], V(wga.ap[:, kc, cc * 128:(cc + 1) * 128], wga.bufs), hT[g][:, kc, :], start=(kc == 0), stop=(kc == KC - 1))
                        for kc in range(KC):
                            P.mm(b_gr[:, :], V(wgr.ap[:, kc, cc * 128:(cc + 1) * 128], wgr.bufs), hT[g][:, kc, :], start=(kc == 0), stop=(kc == KC - 1))
                        for kc in range(4):
                            P.mm(b_ua[:, :], V(wua.ap[:, kc, cc * 128:(cc + 1) * 128], wua.bufs), oaT[:, kc, :], start=(kc == 0), stop=(kc == 3))
                        for kc in range(4):
                            P.mm(b_ur[:, :], V(wur.ap[:, kc, cc * 128:(cc + 1) * 128], wur.bufs), orT[:, kc, :], start=(kc == 0), stop=(kc == 3))
                        sa, sr = tmp(), tmp()
                        P.act(sa, b_ga[:, :], AF.Sigmoid)
                        P.act(sr, b_gr[:, :], AF.Sigmoid)
                        P.tt(sa, sa, b_ua[:, :], ALU.mult)
                        P.tt(sr, sr, b_ur[:, :], ALU.mult)
                        P.tt(mT[:, dc, :], sa, sr, ALU.add)
                for jp in range(2):
                    (wo,) = ring.next()
                    for cc in range(4):
                        dc = jp * 4 + cc
                        bk = bank()
                        for kc in range(KC):
                            P.mm(bk[:, :], V(wo.ap[:, kc, cc * 128:(cc + 1) * 128], wo.bufs), mT[:, kc, :], start=(kc == 0), stop=(kc == KC - 1))
                        P.stt(xT[g][:, dc, :], bk[:, :], g5[:, dc, g:g + 1], xT[g][:, dc, :], ALU.mult, ALU.add)

            for ch in KGRP:
                group(int(ch))

        if stage >= 5:
            b_cc1i, b_cc1o, b_cc2i, b_cc2o = Buf("cc1i"), Buf("cc1o"), Buf("cc2i"), Buf("cc2o")
            ones_f = TT(P, "ones_f", [128, 1], F32)
            P.memset(ones_f[:], 1.0)
            with ExitStack() as mph:
                P.alloc = mph
                mixer()
                P.barrier()
                P.alloc = es

        with ExitStack() as ph:
            P.alloc = ph
            actT = [TT(P, "actT%d" % g, [128, NFF, NT], BF16, split=1) for g in range(2)]
            if stage >= 4:
                for g in range(2):
                    norm_mod(g, 2, 6, 7)
                gs = ffn(8, actT)
                ffn_down(8, actT, gs)
            P.barrier()
            P.alloc = es

        with ExitStack() as ph:
            P.alloc = ph
            ytok = [TT(P, "ytok%d" % i, [128, D], F32) for i in range(2)]
            yT = TT(P, "yT", [128, KC, NT], F32, split=1)
            for g in range(2):
                bk = rstd_of(g)
                for c in range(KC):
                    P.stt(yT[:, c, :], xT[g][:, c, :], finalg[:, c:c + 1], bk[:, :], ALU.mult, ALU.mult)
                for tb in range(4):
                    yt = ytok[(g * 4 + tb) % 2]
                    for half in range(2):
                        b2 = bank()
                        for q in range(4):
                            c = half * 4 + q
                            P.transpose(b2[:, q * 128:(q + 1) * 128], yT[:, c, tb * 128:(tb + 1) * 128], ident[:], signal=(q == 3))
                        P.copy(yt[:, half * 512:(half + 1) * 512], b2[:, :], eng="scalar" if half else "vector")
                    P.dma("sync", V(y_d[g][tb * 128:(tb + 1) * 128, :], []), yt[:], is_output=True)
            P.barrier()
            P.alloc = es

        P.finish()
        print("instr counts", P.ninstr, "dma sems", P.dma_used)
    return nc


_NC_CACHE = {}


def _rope_tables():
    t = np.arange(2048)
    row = (t // 64).astype(np.float32)
    col = (t % 64).astype(np.float32)
    inv = (10000.0 ** (-np.arange(8, dtype=np.float32) / 8)).astype(np.float32)
    cos = np.zeros((32, 2048), np.float32)
    sin = np.zeros((32, 2048), np.float32)
    for half, pos in ((0, row), (1, col)):
        ang = pos[None, :] * inv[:, None]
        c, s_ = np.cos(ang).astype(np.float32), np.sin(ang).astype(np.float32)
        cos[half * 16:half * 16 + 8] = c
        cos[half * 16 + 8:half * 16 + 16] = c
        sin[half * 16:half * 16 + 8] = -s_
        sin[half * 16 + 8:half * 16 + 16] = s_
    return cos, sin


_HPERM = np.array([0, 2, 4, 6, 1, 3, 5, 7])
_PERM = np.array([j + 8 if (j % 16) < 8 else j - 8 for j in range(32)])


def _prep_inputs(inp):
    f = lambda a: np.ascontiguousarray(np.asarray(a, dtype=np.float32))
    xp = f(inp["x_prompt"])
    xs = f(inp["x_sample"])
    c = f(inp["c"])
    cctx = f(inp["c_ctx"])
    w_in = f(inp["w_in"][0])
    w_qb = f(inp["w_qb"][0])
    w_kvb = f(inp["w_kvb"][0])
    kr = w_in[:, 640:672]
    qbx = []
    for h in range(8):
        rp = w_qb[:, h * 96 + 64:h * 96 + 96]
        qbx += [w_qb[:, h * 96:h * 96 + 64], rp, rp[:, _PERM]]
    kvb = w_kvb.reshape(256, 8, 128)
    s_ = np.arange(128)[:, None]
    t_ = np.arange(128)[None, :]
    same = (s_ // 16) == (t_ // 16)
    mf = (same & (s_ <= t_)).astype(np.float32)
    mb = (same & (s_ >= t_)).astype(np.float32)
    maskT = np.stack([np.tile(mf, (1, 4)), np.tile(mb, (1, 4))], axis=1)
    scanm = np.ones((128, NT), np.float32)
    scanm[:, ::16] = 0.0
    par = (np.arange(128) // 16) % 2
    rowm = np.stack([(par == 0), (par == 1)], axis=1).astype(np.float32)
    bdiag = np.kron(np.eye(2, dtype=np.float32), np.ones((64, 64), np.float32))
    cos, sin = _rope_tables()
    hg = f(inp["hg_gamma"])
    shared = {
        "w_ada": f(inp["w_ada"][0]),
        "b_ada": f(inp["b_ada"][0].reshape(72, 128).T),
        "norm_g": f(inp["norm_g"][0].reshape(3, KC, 128).transpose(2, 0, 1)),
        "final_g": f(inp["final_g"].reshape(KC, 128).T),
        "ident": np.eye(128, dtype=np.float32),
        "w_in": w_in,
        "w_krx": f(np.concatenate([kr, kr, kr, kr[:, _PERM]], axis=1)),
        "w_qbx": f(np.concatenate(qbx, axis=1)),
        "w_kvbx": f(np.concatenate([kvb[:, :, :64].reshape(256, 512), kvb[:, :, 64:].reshape(256, 512)], axis=1)),
        "w_up_attn": f(inp["w_up_attn"][0]),
        "w_up_rec_p": f(inp["w_up_rec"][0].reshape(8, 64, D)[_HPERM].reshape(512, D)),
        "w_hgate_p": f(w_in[:, 2720:3232].reshape(D, 8, 64)[:, _HPERM].reshape(D, 512)),
        "w_out": f(inp["w_out"][0]),
        "maskT": f(maskT),
        "scanm": scanm,
        "rowm": rowm,
        "bdiag": bdiag,
        "hgam": f(hg.reshape(2, 2, 4, 128).transpose(3, 0, 1, 2)),
        "hgng": f(np.tile(inp["hg_norm_g"][0], 2).reshape(128, 1)),
        "qng": f(inp["q_norm_g"][0].reshape(3, 128).T),
        "kvng": f(inp["kv_norm_g"][0].reshape(2, 128).T),
    }
    for fi, nm in ((1, "ffn1"), (2, "ffn2")):
        for k in (1, 3, 2):
            shared["ffn%d_w%d" % (fi, k)] = f(inp["%s_w%d" % (nm, k)][0])
    maps = []
    for core in range(8):
        b, seg = core // 4, core % 4
        m = dict(shared)
        m["xs"] = f(xs[b, seg * NT:(seg + 1) * NT])
        m["xp"] = f(xp[2 * core:2 * core + 2].reshape(NT, D))
        cs = np.stack([c[b], cctx], axis=0)
        m["csel"] = f(cs.reshape(2, KC, 128).transpose(2, 1, 0))
        rope = np.zeros((128, NT), np.float32)
        rope[64:96] = cos[:, seg * NT:(seg + 1) * NT]
        rope[96:128] = sin[:, seg * NT:(seg + 1) * NT]
        m["rope"] = rope
        sel = np.zeros((128, 8), np.float32)
        for r in range(4):
            sel[:, r] = 1.0 if r < seg else 0.0
            sel[:, 4 + r] = 1.0 if r > seg else 0.0
        m["sel"] = sel
        m["cckv"] = f(inp["cache_ckv"][b, 0])
        m["ckr"] = f(inp["cache_krope"][b, 0])
        m["s0"] = f(inp["state_hgrn"][b, 0])
        maps.append(m)
    return maps


def kernel(**inp):
    if "nc" not in _NC_CACHE:
        _NC_CACHE["nc"] = build_program(int(os.environ.get("KSTAGE", "99")))
    nc = _NC_CACHE["nc"]
    maps = _prep_inputs(inp)
    res = run_bass_kernel_spmd(nc, maps, core_ids=list(range(8)))
    R = res.results
    y_prompt = np.zeros((16, 256, D), np.float32)
    y_sample = np.zeros((2, 2048, D), np.float32)
    new_ckv = np.zeros((16, 1, 256, 256), np.float32)
    new_kr = np.zeros((16, 1, 256, 32), np.float32)
    new_st = np.zeros((16, 1, 2, 8, 64, 64), np.float32)
    for core in range(8):
        b, seg = core // 4, core % 4
        y_sample[b, seg * NT:(seg + 1) * NT] = R[core]["ys"]
        y_prompt[2 * core:2 * core + 2] = R[core]["yp"].reshape(2, 256, D)
        new_ckv[2 * core:2 * core + 2, 0] = R[core]["nckv"].reshape(2, 256, 256)
        new_kr[2 * core:2 * core + 2, 0] = R[core]["nkr"].reshape(2, 256, 32)
        new_st[2 * core:2 * core + 2, 0] = R[core]["nst"]
    return (y_prompt, y_sample, new_ckv, new_kr, new_st)
```

```python
import os
import numpy as np
from contextlib import ExitStack, contextmanager
import concourse.bass as bass
import concourse.mybir as mybir
from concourse.bass_utils import run_bass_kernel_spmd

F32 = mybir.dt.float32
BF16 = mybir.dt.bfloat16
AF = mybir.ActivationFunctionType
ALU = mybir.AluOpType
AX = mybir.AxisListType

ENGS = ["tensor", "vector", "scalar", "gpsimd", "sync"]
D = 1024
DFF = 2816
NT = 512
KC = 8
NFF = 22
EPS = 1e-6


class Buf:
    def __init__(self, name):
        self.name = name
        self.w = {}
        self.r = {}
        self.dsem = None
        self.dcount = 0
        self.excl = False


class V:
    def __init__(self, ap, bufs):
        self.ap = ap
        self.bufs = list(bufs)


class TT:
    def __init__(self, P, name, shape, dt, split=None, psum=False):
        P.uid += 1
        nm = "t%d_%s" % (P.uid, name)
        self.t = P.psum_t(nm, shape, dt) if psum else P.sbuf_t(nm, shape, dt)
        self.shape = list(shape)
        self.split = split
        n = 1 if split is None else shape[split]
        self.bufs = [Buf("%s.%d" % (name, i)) for i in range(n)]
        for b in self.bufs:
            b.excl = psum

    def __getitem__(self, idx):
        if not isinstance(idx, tuple):
            idx = (idx,)
        if self.split is None or len(idx) <= self.split:
            bufs = self.bufs
        else:
            s = idx[self.split]
            if isinstance(s, slice):
                bufs = self.bufs[s]
            else:
                bufs = [self.bufs[s]]
        return V(self.t[idx], bufs)


class Prog:
    def __init__(self, nc, es, n_dma_sems=90, same_engine_waits=True):
        self.nc = nc
        self.es = es
        self.streams = {e: [] for e in ENGS}
        self.sem = {e: es.enter_context(nc.semaphore("s_" + e)) for e in ENGS}
        self.count = {e: 0 for e in ENGS}
        self.known = {e: {} for e in ENGS}
        self.dma_pool = [es.enter_context(nc.semaphore("d%d" % i)) for i in range(n_dma_sems)]
        self.dma_used = 0
        self.same_engine_waits = same_engine_waits
        self.out_tickets = []
        self.ninstr = {e: 0 for e in ENGS}
        self.alloc = es
        self.uid = 0
        self.dma_latest = {}

    def sbuf_t(self, name, shape, dt):
        return self.alloc.enter_context(self.nc.sbuf_tensor(name, list(shape), dt))

    def psum_t(self, name, shape, dt=F32):
        return self.alloc.enter_context(self.nc.psum_tensor(name, list(shape), dt))

    def _gather(self, eng, reads, writes):
        waits = {}

        def add(d):
            for k, (s, v) in d.items():
                if k not in waits or waits[k][1] < v:
                    waits[k] = (s, v)

        for b in reads:
            add(b.w)
        for b in writes:
            add(b.w)
            add(b.r)
        need = []
        kn = self.known[eng]
        for k, (s, v) in waits.items():
            if k == eng:
                if eng == "tensor" or not self.same_engine_waits:
                    continue
                v = min(v, self.count[eng])
                if v <= 0:
                    continue
            if kn.get(k, 0) < v:
                kn[k] = v
                need.append((s, v))
        return need

    def _record(self, key, ticket, reads, writes):
        for b in writes:
            b.w = {key: ticket}
            b.r = {}
        for b in reads:
            if b in writes:
                continue
            old = b.r.get(key)
            if old is None or old[1] < ticket[1]:
                b.r[key] = ticket

    def op(self, eng, fn, reads=(), writes=(), signal=True):
        reads = list(reads)
        writes = list(writes)
        if eng != "tensor":
            writes = writes + [b for b in reads if b.excl and b not in writes]
            reads = [b for b in reads if not b.excl]
        need = self._gather(eng, reads, writes)
        sem = self.sem[eng]
        if signal:
            self.count[eng] += 1
            val = self.count[eng]
        else:
            val = self.count[eng] + 1

        def run(e, need=need, fn=fn, signal=signal, sem=sem):
            for s, v in need:
                e.wait_ge(s, v)
            ins = fn(e)
            if signal:
                ins.then_inc(sem, 1)

        self.streams[eng].append(run)
        self.ninstr[eng] += 1
        self._record(eng, (sem, val), reads, writes)

    def _dsem(self, primary):
        if primary.dsem is None:
            primary.dsem = self.dma_pool[self.dma_used]
            primary.dkey = "dma%d" % self.dma_used
            self.dma_used += 1

    def dma(self, queue, out, in_, is_output=False, extra_reads=(), **kw):
        reads = list(in_.bufs) + list(extra_reads)
        writes = list(out.bufs)
        primary = writes[0] if writes else reads[0]
        self._dsem(primary)
        need = self._gather(queue, reads, writes)
        primary.dcount += 16
        sem = primary.dsem
        ticket = (sem, primary.dcount)
        out_ap, in_ap = out.ap, in_.ap

        def run(e, need=need, sem=sem, out_ap=out_ap, in_ap=in_ap, kw=kw):
            for s, v in need:
                e.wait_ge(s, v)
            e.dma_start(out=out_ap, in_=in_ap, **kw).then_inc(sem, 16)

        self.streams[queue].append(run)
        self.ninstr[queue] += 1
        self._record(primary.dkey, ticket, reads, writes)
        self.dma_latest[primary.dkey] = ticket
        if is_output:
            self.out_tickets.append((primary.dkey, ticket))

    def custom(self, eng, fn, reads, writes, sem_buf, inc=1):
        reads = list(reads)
        writes = list(writes)
        self._dsem(sem_buf)
        need = self._gather(eng, reads, writes)
        sem_buf.dcount += inc
        sem = sem_buf.dsem
        ticket = (sem, sem_buf.dcount)

        def run(e, need=need, sem=sem, fn=fn, inc=inc):
            for s, v in need:
                e.wait_ge(s, v)
            fn(e).then_inc(sem, inc)

        self.streams[eng].append(run)
        self._record(sem_buf.dkey, ticket, reads, writes)
        self.dma_latest[sem_buf.dkey] = ticket

    def mm(self, out, lhsT, rhs, start=True, stop=True, signal=None, **kw):
        if signal is None:
            signal = stop
        self.op("tensor", lambda e: e.matmul(out.ap, lhsT=lhsT.ap, rhs=rhs.ap, start=start, stop=stop, **kw),
                reads=lhsT.bufs + rhs.bufs, writes=out.bufs, signal=signal)

    def transpose(self, out, in_, ident, signal=True):
        self.op("tensor", lambda e: e.transpose(out.ap, in_.ap, ident.ap),
                reads=in_.bufs + ident.bufs, writes=out.bufs, signal=signal)

    def act(self, out, in_, func, scale=1.0, bias=None, eng="scalar"):
        reads = list(in_.bufs)
        kw = {}
        if isinstance(scale, V):
            reads += scale.bufs
            kw["scale"] = scale.ap
        else:
            kw["scale"] = scale
        if isinstance(bias, V):
            reads += bias.bufs
            kw["bias"] = bias.ap
        elif bias is not None:
            kw["bias"] = bias
        self.op(eng, lambda e: e.activation(out=out.ap, in_=in_.ap, func=func, **kw), reads=reads, writes=out.bufs)

    def tt(self, out, a, b, op, eng="vector"):
        self.op(eng, lambda e: e.tensor_tensor(out=out.ap, in0=a.ap, in1=b.ap, op=op),
                reads=a.bufs + b.bufs, writes=out.bufs)

    def ts(self, out, a, s1, op0, s2=None, op1=None, eng="vector"):
        reads = list(a.bufs)
        s1a = s1
        s2a = s2
        if isinstance(s1, V):
            reads += s1.bufs
            s1a = s1.ap
        if isinstance(s2, V):
            reads += s2.bufs
            s2a = s2.ap
        kw = {}
        if op1 is not None:
            kw = dict(scalar2=s2a, op1=op1)
        else:
            kw = dict(scalar2=None)
        self.op(eng, lambda e: e.tensor_scalar(out=out.ap, in0=a.ap, scalar1=s1a, op0=op0, **kw),
                reads=reads, writes=out.bufs)

    def stt(self, out, a, s, b, op0, op1):
        reads = a.bufs + b.bufs
        sa = s
        if isinstance(s, V):
            reads = reads + s.bufs
            sa = s.ap
        self.op("vector", lambda e: e.scalar_tensor_tensor(out=out.ap, in0=a.ap, scalar=sa, in1=b.ap, op0=op0, op1=op1),
                reads=reads, writes=out.bufs)

    def copy(self, out, in_, eng="vector"):
        if eng == "scalar":
            self.act(out, in_, AF.Copy)
        else:
            self.op(eng, lambda e: e.tensor_copy(out=out.ap, in_=in_.ap), reads=in_.bufs, writes=out.bufs)

    def memset(self, out, val, eng="vector"):
        self.op(eng, lambda e: e.memset(out.ap, val), writes=out.bufs)

    def barrier(self, engs=("tensor", "vector", "scalar", "gpsimd"), queues=("sync",)):
        for e in tuple(engs) + tuple(queues):
            need = []
            for dk, (s_, v_) in self.dma_latest.items():
                if self.known[e].get(dk, 0) < v_:
                    self.known[e][dk] = v_
                    need.append((s_, v_))
            for o in engs:
                if o == e:
                    continue
                v = self.count[o]
                if v > 0 and self.known[e].get(o, 0) < v:
                    self.known[e][o] = v
                    need.append((self.sem[o], v))

            def run(eng, need=need):
                for s_, v_ in need:
                    eng.wait_ge(s_, v_)

            self.streams[e].append(run)

    def finish(self):
        final = {}
        for k, (s, v) in self.out_tickets:
            if k not in final or final[k][1] < v:
                final[k] = (s, v)
        fl = list(final.values())

        def run(e, fl=fl):
            for s, v in fl:
                e.wait_ge(s, v)

        self.streams["sync"].append(run)
        with self.nc.Block() as block:
            for name in ENGS:
                stream = self.streams[name]

                def body(e, stream=stream):
                    for f in stream:
                        f(e)

                getattr(block, name)(body)


class WeightRing:
    def __init__(self, P, nslots, slot_elems):
        self.P = P
        self.slots = [TT(P, "wslot%d" % i, [128, slot_elems], BF16) for i in range(nslots)]
        self.n = nslots
        self.pieces = []
        self.issued = 0
        self.taken = 0

    def plan(self, pieces):
        self.pieces += pieces

    def _issue(self, i):
        slot = self.slots[i % self.n]
        off = 0
        for (src_ap, shape) in self.pieces[i]:
            ne = int(np.prod(shape))
            dst = slot.t[:, off:off + ne]
            if len(shape) == 2:
                dst = dst.rearrange("p (a b) -> p a b", b=shape[1])
            self.P.dma("gpsimd", V(dst, slot.bufs), V(src_ap, []))
            off += ne

    def next(self):
        i = self.taken
        while self.issued < min(len(self.pieces), i + self.n):
            self._issue(self.issued)
            self.issued += 1
        self.taken += 1
        slot = self.slots[i % self.n]
        views = []
        off = 0
        for (src_ap, shape) in self.pieces[i]:
            ne = int(np.prod(shape))
            t = slot.t[:, off:off + ne]
            if len(shape) == 2:
                t = t.rearrange("p (a b) -> p a b", b=shape[1])
            views.append(V(t, slot.bufs))
            off += ne
        return views


def build_program(stage=99):
    nc = bass.Bass("TRN2", target_bir_lowering=False)

    def din(name, shape, dt=F32):
        return nc.dram_tensor(name, list(shape), dt, kind="ExternalInput").ap()

    def dout(name, shape, dt=F32):
        return nc.dram_tensor(name, list(shape), dt, kind="ExternalOutput").ap()

    x_d = [din("xs", [NT, D]), din("xp", [NT, D])]
    csel_d = din("csel", [128, KC, 2])
    wada_d = din("w_ada", [D, 9 * D])
    bada_d = din("b_ada", [128, 72])
    normg_d = din("norm_g", [128, 3, KC])
    finalg_d = din("final_g", [128, KC])
    ident_d = din("ident", [128, 128])
    ffn_d = [[din("ffn%d_w%d" % (f, k), [D, DFF] if k != 2 else [DFF, D]) for k in (1, 3, 2)] for f in (1, 2)]
    win_d = din("w_in", [D, 5280])
    wkrx_d = din("w_krx", [D, 128])
    wqbx_d = din("w_qbx", [384, 1024])
    wkvbx_d = din("w_kvbx", [256, 1024])
    wua_d = din("w_up_attn", [512, D])
    wur_d = din("w_up_rec_p", [512, D])
    whg_d = din("w_hgate_p", [D, 512])
    wout_d = din("w_out", [D, D])
    rope_d = din("rope", [128, NT])
    maskT_d = din("maskT", [128, 2, 512])
    scanm_d = din("scanm", [128, NT])
    rowm_d = din("rowm", [128, 2])
    bdiag_d = din("bdiag", [128, 128])
    hgam_d = din("hgam", [128, 2, 2, 4])
    hgng_d = din("hgng", [128, 1])
    qng_d = din("qng", [128, 3])
    kvng_d = din("kvng", [128, 2])
    sel_d = din("sel", [128, 8])
    cckv_d = din("cckv", [256, 256])
    ckr_d = din("ckr", [256, 32])
    s0_d = din("s0", [2, 8, 64, 64])
    cc1_in = nc.dram_tensor("cc1_in", [128, 1536], BF16)
    cc1_out = nc.dram_tensor("cc1_out", [512, 1536], BF16)
    cc2_in = nc.dram_tensor("cc2_in", [128, 520], F32)
    cc2_out = nc.dram_tensor("cc2_out", [512, 520], F32)
    y_d = [dout("ys", [NT, D]), dout("yp", [NT, D])]
    nckv_d = dout("nckv", [NT, 256])
    nkr_d = dout("nkr", [NT, 32])
    nst_d = dout("nst", [2, 2, 8, 64, 64])

    with ExitStack() as es:
        P = Prog(nc, es)
        ring = WeightRing(P, 3, 6144)

        ident = TT(P, "ident", [128, 128], F32)
        P.dma("sync", ident[:], V(ident_d[:, :], []))
        ones_bf = TT(P, "ones_bf", [128, 128], BF16)
        P.memset(ones_bf[:], 1.0)
        epsb = TT(P, "epsb", [128, 1], F32)
        P.memset(epsb[:], EPS)
        xT = [TT(P, "xT%d" % g, [128, KC, NT], F32, split=1) for g in range(2)]
        hT = [TT(P, "hT%d" % g, [128, KC, NT], BF16, split=1) for g in range(2)]
        banks = [TT(P, "bank%d" % i, [128, 512], F32, psum=True) for i in range(8)]
        bank_rr = [0]

        def bank():
            b = banks[bank_rr[0] % 8]
            bank_rr[0] += 1
            return b

        normg = TT(P, "normg", [128, 3, KC], F32)
        P.dma("sync", normg[:], V(normg_d[:, :, :], []))
        finalg = TT(P, "finalg", [128, KC], F32)
        P.dma("sync", finalg[:], V(finalg_d[:, :], []))
        bada = TT(P, "bada", [128, 72], F32)
        P.dma("sync", bada[:], V(bada_d[:, :], []))
        csel = TT(P, "csel", [128, KC, 2], F32)
        P.dma("sync", csel[:], V(csel_d[:, :, :], []))
        scb = TT(P, "scb", [128, KC, 2], BF16)
        P.act(scb[:], csel[:], AF.Silu)
        mod = TT(P, "mod", [128, 72, 2], F32)

        wada_v = wada_d.rearrange("(kc p) n -> p kc n", p=128)

        def ada_piece(i):
            return [[(wada_v[:, :, i * D + h * 512: i * D + (h + 1) * 512], [KC, 512])] for h in range(2)]

        def ffn_pieces(f):
            w1, w3, w2 = ffn_d[f]
            w1v = w1.rearrange("(kc p) n -> p kc n", p=128)
            w3v = w3.rearrange("(kc p) n -> p kc n", p=128)
            w2v = w2.rearrange("(kc p) n -> p kc n", p=128)
            ps = []
            for j in range(NFF // 2):
                ps.append([(w1v[:, :, j * 256:(j + 1) * 256], [KC, 256]), (w3v[:, :, j * 256:(j + 1) * 256], [KC, 256])])
            for j in range(4):
                ps.append([(w2v[:, :, j * 256:(j + 1) * 256], [NFF, 256])])
            return ps

        ring.plan(ada_piece(0) + ada_piece(1))
        ada_rest = [(i, h) for i in range(2, 9) for h in range(2)]
        ada_sched = {}
        pos = 0
        for j in range(11):
            nh = 2 if j < 3 else 1
            ada_sched[j] = ada_rest[pos:pos + nh]
            pos += nh
        fp0 = ffn_pieces(0)
        for j in range(11):
            ring.plan([fp0[j]])
            for (i, h) in ada_sched[j]:
                ring.plan([ada_piece(i)[h]])
        ring.plan(fp0[11:])
        winv = win_d.rearrange("(kc p) n -> p kc n", p=128)
        wuav = wua_d.rearrange("(kc p) n -> p kc n", p=128)
        wurv = wur_d.rearrange("(kc p) n -> p kc n", p=128)
        woutv = wout_d.rearrange("(kc p) n -> p kc n", p=128)
        if stage >= 5:
            KSUB = int(os.environ.get("KSUB", "99"))
            KGRP = os.environ.get("KGRP", "10")
            for g in [int(ch) for ch in KGRP]:
                ring.plan([[(winv[:, :, 0:640], [KC, 640])]])
                if KSUB >= 2:
                    whgv = whg_d.rearrange("(kc p) n -> p kc n", p=128)
                    for c0 in (672, 2208, 2720, 1184, 1696):
                        if c0 == 2720:
                            ring.plan([[(whgv[:, :, :], [KC, 512])]])
                        else:
                            ring.plan([[(winv[:, :, c0:c0 + 512], [KC, 512])]])
                for j in range(4 if KSUB >= 5 else 0):
                    ring.plan([[(winv[:, :, 3232 + j * 256:3232 + (j + 1) * 256], [KC, 256]),
                                (winv[:, :, 4256 + j * 256:4256 + (j + 1) * 256], [KC, 256]),
                                (wuav[:, :, j * 256:(j + 1) * 256], [4, 256]),
                                (wurv[:, :, j * 256:(j + 1) * 256], [4, 256])]])
                for j in range(2 if KSUB >= 5 else 0):
                    ring.plan([[(woutv[:, :, j * 512:(j + 1) * 512], [KC, 512])]])
        ring.plan(ffn_pieces(1))

        with ExitStack() as ph:
            P.alloc = ph
            xtok = [TT(P, "xtok%d" % i, [128, D], F32) for i in range(2)]
            for g in range(2):
                for tb in range(4):
                    xt = xtok[(g * 4 + tb) % 2]
                    P.dma("sync", xt[:], V(x_d[g][tb * 128:(tb + 1) * 128, :], []))
                    for half in range(2):
                        bk = bank()
                        for q in range(4):
                            c = half * 4 + q
                            P.transpose(bk[:, q * 128:(q + 1) * 128], xt[:, c * 128:(c + 1) * 128], ident[:], signal=(q == 3))
                        P.copy(xT[g][:, half * 4:half * 4 + 4, tb * 128:(tb + 1) * 128],
                               V(bk.t[:, :].rearrange("p (q t) -> p q t", t=128), bk.bufs),
                               eng="scalar" if half else "vector")
            P.barrier()
            P.alloc = es

        def ada_half(i, h):
            bk = bank()
            (wv,) = ring.next()
            for cc in range(4):
                for kc in range(KC):
                    P.mm(bk[:, cc * 2:cc * 2 + 2], wv_slice(wv, kc, cc), scb[:, kc, :], start=(kc == 0), stop=(kc == KC - 1))
            c0 = i * 8 + h * 4
            P.tt(mod[:, c0:c0 + 4, :], V(bk.t[:, 0:8].rearrange("p (c g) -> p c g", g=2), bk.bufs),
                 V(bada.t[:, c0:c0 + 4].unsqueeze(2).to_broadcast([128, 4, 2]), bada.bufs), ALU.add)

        def ada_compute(i):
            ada_half(i, 0)
            ada_half(i, 1)

        def wv_slice(wv, kc, cc):
            return V(wv.ap[:, kc, cc * 128:(cc + 1) * 128], wv.bufs)

        nscr = TT(P, "nscr", [128, 2, NT], BF16, split=1)
        nscr_f = TT(P, "nscr_f", [128, 2, NT], F32, split=1)
        amod = TT(P, "amod", [128, KC], F32)
        rr = [0]

        def rstd_gen(srcs, nfeat, ones=None, bk=None):
            if bk is None:
                bk = bank()
            if ones is None:
                ones = ones_bf[:]
            for c, sv in enumerate(srcs):
                s = nscr[:, rr[0] % 2, :]
                rr[0] += 1
                P.act(s, sv, AF.Square)
                P.mm(bk[:, :], ones, s, start=(c == 0), stop=(c == len(srcs) - 1), signal=True)
            P.act(bk[:, :], bk[:, :], AF.Ln, scale=1.0 / nfeat, bias=epsb[:, 0:1])
            P.act(bk[:, :], bk[:, :], AF.Exp, scale=-0.5)
            return bk

        def rstd_of(g):
            return rstd_gen([xT[g][:, c, :] for c in range(KC)], D)

        def norm_mod(g, ni, i_shift, i_scale):
            P.ts(amod[:], mod[:, i_scale * 8:(i_scale + 1) * 8, g], 1.0, ALU.add)
            P.tt(amod[:], amod[:], normg[:, ni, :], ALU.mult)
            bk = rstd_of(g)
            for c in range(KC):
                s = nscr_f[:, rr[0] % 2, :]
                rr[0] += 1
                P.tt(s, xT[g][:, c, :], bk[:, :], ALU.mult)
                P.act(hT[g][:, c, :], s, AF.Identity, scale=amod[:, c:c + 1], bias=mod[:, i_shift * 8 + c, g:g + 1])

        def ffn(i_gate, actT, extra=None):
            gs = TT(P, "gs%d" % i_gate, [128, KC, 2], F32)
            for j in range(NFF // 2):
                if extra is not None and j > 0:
                    extra(j - 1)
                w1v, w3v = ring.next()
                for cc in range(2):
                    ch = j * 2 + cc
                    for g in range(2):
                        ba, bb = bank(), bank()
                        for kc in range(KC):
                            P.mm(ba[:, :], wv_slice(w1v, kc, cc), hT[g][:, kc, :], start=(kc == 0), stop=(kc == KC - 1))
                        for kc in range(KC):
                            P.mm(bb[:, :], wv_slice(w3v, kc, cc), hT[g][:, kc, :], start=(kc == 0), stop=(kc == KC - 1))
                        s = nscr_f[:, rr[0] % 2, :]
                        rr[0] += 1
                        P.act(s, ba[:, :], AF.Silu)
                        P.tt(actT[g][:, ch, :], s, bb[:, :], ALU.mult)
            return gs

        def ffn_down(i_gate, actT, gs):
            P.ts(gs[:], mod[:, i_gate * 8:(i_gate + 1) * 8, :], 0.5, ALU.mult)
            for j in range(4):
                (w2v,) = ring.next()
                for cc in range(2):
                    dc = j * 2 + cc
                    for g in range(2):
                        bk = bank()
                        for kc in range(NFF):
                            P.mm(bk[:, :], wv_slice(w2v, kc, cc), actT[g][:, kc, :], start=(kc == 0), stop=(kc == NFF - 1))
                        P.stt(xT[g][:, dc, :], bk[:, :], gs[:, dc, g:g + 1], xT[g][:, dc, :], ALU.mult, ALU.add)

        if stage >= 1:
            ada_compute(0)
            ada_compute(1)
        with ExitStack() as ph:
            P.alloc = ph
            actT = [TT(P, "actT%d" % g, [128, NFF, NT], BF16, split=1) for g in range(2)]
            if stage >= 2:
                for g in range(2):
                    norm_mod(g, 0, 0, 1)
            if stage >= 3:
                def ada_extra(j):
                    for (i, h) in ada_sched[j]:
                        ada_half(i, h)

                gs = ffn(2, actT, extra=ada_extra)
                ada_extra(10)
                ffn_down(2, actT, gs)
            P.barrier()
            P.alloc = es

        def mixer():
            ph = P.alloc
            QKS = (64 + 32) ** -0.5
            rope = TT(P, "rope", [128, NT], F32)
            P.dma("sync", rope[:], V(rope_d[:, :], []))
            maskT = TT(P, "maskT", [128, 2, 512], BF16)
            P.dma("gpsimd", maskT[:], V(maskT_d[:, :, :], []))
            scanm = TT(P, "scanm", [128, NT], F32)
            P.dma("sync", scanm[:], V(scanm_d[:, :], []))
            rowm = TT(P, "rowm", [128, 2], F32)
            P.dma("sync", rowm[:], V(rowm_d[:, :], []))
            bdiag = TT(P, "bdiag", [128, 128], BF16)
            P.dma("gpsimd", bdiag[:], V(bdiag_d[:, :], []))
            hgam = TT(P, "hgam", [128, 2, 2, 4], F32)
            P.dma("sync", hgam[:], V(hgam_d[:, :, :, :], []))
            hgng = TT(P, "hgng", [128, 1], F32)
            P.dma("sync", hgng[:], V(hgng_d[:, :], []))
            qng = TT(P, "qng", [128, 3], F32)
            P.dma("sync", qng[:], V(qng_d[:, :], []))
            kvng = TT(P, "kvng", [128, 2], F32)
            P.dma("sync", kvng[:], V(kvng_d[:, :], []))
            sel = TT(P, "sel", [128, 8], F32)
            P.dma("sync", sel[:], V(sel_d[:, :], []))
            lb = TT(P, "lb", [128, 2, 4], F32)
            oml = TT(P, "oml", [128, 2, 4], F32)
            P.tt(lb[:], hgam[:, :, 0, :], hgam[:, :, 1, :], ALU.subtract)
            P.act(lb[:], lb[:], AF.Sigmoid)
            P.ts(oml[:], lb[:], -1.0, ALU.mult, 1.0, ALU.add)
            g5 = TT(P, "g5", [128, KC, 2], F32)
            P.copy(g5[:], mod[:, 40:48, :])

            tmpf = TT(P, "tmpf", [128, 4, NT], F32, split=1)
            tr = [0]

            @contextmanager
            def phase():
                outer = P.alloc
                with ExitStack() as st_:
                    P.alloc = st_
                    yield
                    P.barrier()
                P.alloc = outer

            def tmp():
                t = tmpf[:, tr[0] % 4, :]
                tr[0] += 1
                return t

            def proj_chunks(wv, ncols_chunks, g, fn):
                for ci in range(ncols_chunks):
                    bk = bank()
                    for kc in range(KC):
                        P.mm(bk[:, :], V(wv.ap[:, kc, ci * 128:(ci + 1) * 128], wv.bufs), hT[g][:, kc, :],
                             start=(kc == 0), stop=(kc == KC - 1))
                    fn(ci, bk)

            def group(g):
                sample = (g == 0)
                with phase():
                    norm_mod(g, 1, 3, 4)
                    Q = TT(P, "Q", [128, 8, NT], BF16, split=1)
                    ckvT = TT(P, "ckvT", [128, 2, NT], BF16)
                    krb = TT(P, "krb", [128, NT], BF16)
                    P.memset(krb[:], 0.0)
                    ohT = TT(P, "ohT", [128, 4, NT], F32, split=1)
                    hgs = TT(P, "hgs", [128, 4, NT], BF16, split=1)
                    oaT = TT(P, "oaT", [128, 4, NT], BF16, split=1)
                    orT = TT(P, "orT", [128, 4, NT], BF16, split=1)
                    Qf = [TT(P, "Qf%d" % d, [128, 4, NT], BF16, split=1) for d in range(2)] if sample else None
                    with phase():
                        front(g, sample, Q, ckvT, krb)
                    if KSUB >= 2:
                        with phase():
                            hgrn(g, sample, ohT, hgs, Qf)
                    if KSUB >= 3:
                        with phase():
                            attention(g, sample, Q, ckvT, krb, oaT)
                    if KSUB >= 4:
                        with phase():
                            if sample:
                                hgrn_fix(ohT, Qf)
                            hgrn_norm(ohT, hgs, orT)
                    if KSUB >= 5:
                        with phase():
                            merge(g, oaT, orT)

            def front(g, sample, Q, ckvT, krb):
                wkrx = TT(P, "wkrx", [128, KC, 128], BF16)
                P.dma("gpsimd", wkrx[:], V(wkrx_d.rearrange("(kc p) n -> p kc n", p=128), []))
                wqbx = TT(P, "wqbx", [128, 3, 1024], BF16)
                P.dma("gpsimd", wqbx[:], V(wqbx_d.rearrange("(kc p) n -> p kc n", p=128), []))
                qkT = TT(P, "qkT", [128, 5, NT], F32, split=1)
                (w1v,) = ring.next()
                proj_chunks(w1v, 5, g, lambda ci, bk: P.copy(qkT[:, ci, :], bk[:, :], eng="scalar"))
                qnT = TT(P, "qnT", [128, 3, NT], BF16)
                bk = rstd_gen([qkT[:, c, :] for c in range(3)], 384)
                for c in range(3):
                    P.stt(qnT[:, c, :], qkT[:, c, :], qng[:, c:c + 1], bk[:, :], ALU.mult, ALU.mult)
                bk = rstd_gen([qkT[:, 3 + c, :] for c in range(2)], 256)
                for c in range(2):
                    P.stt(qkT[:, 3 + c, :], qkT[:, 3 + c, :], kvng[:, c:c + 1], bk[:, :], ALU.mult, ALU.mult)
                    P.copy(ckvT[:, c, :], qkT[:, 3 + c, :], eng="scalar")
                krf = TT(P, "krf", [128, NT], F32)
                bk = bank()
                for kc in range(KC):
                    P.mm(bk[:, :], wkrx[:, kc, :], hT[g][:, kc, :], start=(kc == 0), stop=(kc == KC - 1))
                P.copy(krf[:], bk[:, :], eng="scalar")
                if not sample:
                    otok = TT(P, "otok", [128, 2, 288], F32, split=1)
                    for tb in range(4):
                        bk = bank()
                        for c in range(2):
                            P.transpose(bk[:, c * 128:(c + 1) * 128], qkT[:, 3 + c, tb * 128:(tb + 1) * 128], ident[:], signal=False)
                        P.transpose(bk[:, 256:288], krf[0:32, tb * 128:(tb + 1) * 128], ident[0:32, 0:32], signal=True)
                        P.copy(otok[:, tb % 2, :], bk[:, 0:288])
                        P.dma("sync", V(nckv_d[tb * 128:(tb + 1) * 128, :], []), otok[:, tb % 2, 0:256], is_output=True)
                        P.dma("sync", V(nkr_d[tb * 128:(tb + 1) * 128, :], []), otok[:, tb % 2, 256:288], is_output=True)
                    P.copy(krb[64:96, :], krf[64:96, :])
                else:
                    t1 = tmp()
                    t2 = tmp()
                    P.tt(V(t1.ap[64:96, :], t1.bufs), krf[64:96, :], rope[64:96, :], ALU.mult)
                    P.tt(V(t2.ap[64:96, :], t2.bufs), krf[96:128, :], rope[96:128, :], ALU.mult)
                    P.tt(krb[64:96, :], V(t1.ap[64:96, :], t1.bufs), V(t2.ap[64:96, :], t2.bufs), ALU.add)
                for h in range(8):
                    bk = bank()
                    for kc in range(3):
                        P.mm(bk[:, :], wqbx[:, kc, h * 128:(h + 1) * 128], qnT[:, kc, :], start=(kc == 0), stop=(kc == 2))
                    if sample:
                        P.copy(Q[0:64, h, :], bk[0:64, :], eng="scalar")
                        t1 = tmp()
                        t2 = tmp()
                        P.tt(V(t1.ap[64:96, :], t1.bufs), bk[64:96, :], rope[64:96, :], ALU.mult)
                        P.tt(V(t2.ap[64:96, :], t2.bufs), bk[96:128, :], rope[96:128, :], ALU.mult)
                        P.tt(Q[64:96, h, :], V(t1.ap[64:96, :], t1.bufs), V(t2.ap[64:96, :], t2.bufs), ALU.add)
                    else:
                        P.copy(Q[0:96, h, :], bk[0:96, :], eng="scalar")
                if sample:
                    P.dma("sync", V(cc1_in[:, 0:1024].rearrange("p (c t) -> p c t", t=NT), [b_cc1i]), ckvT[:])
                    P.dma("sync", V(cc1_in[:, 1024:1536], [b_cc1i]), krb[:, :])
                    P.custom("gpsimd", lambda e: e.collective_compute("AllGather", ALU.bypass, replica_groups=[[0, 1, 2, 3], [4, 5, 6, 7]],
                                                                     ins=[cc1_in.ap().opt()], outs=[cc1_out.ap().opt()]),
                             reads=[b_cc1i], writes=[b_cc1o], sem_buf=b_cc1o)

            def attention(g, sample, Q, ckvT, krb, oaT):
                nk = 2304 if sample else NT
                nkt = nk // 128
                wkvbx = TT(P, "wkvbx", [128, 2, 1024], BF16)
                P.dma("gpsimd", wkvbx[:], V(wkvbx_d.rearrange("(kc p) n -> p kc n", p=128), []))
                if sample:
                    krT = TT(P, "krT", [128, nk], BF16)
                    ckvA = TT(P, "ckvA", [128, 2, nk], BF16)
                    ctok = TT(P, "ctok", [128, 2, 288], F32)
                    for t2_ in range(2):
                        P.dma("sync", ctok[:, t2_, 0:256], V(cckv_d[t2_ * 128:(t2_ + 1) * 128, :], []))
                        P.dma("sync", ctok[:, t2_, 256:288], V(ckr_d[t2_ * 128:(t2_ + 1) * 128, :], []))
                    for t2_ in range(2):
                        bk = bank()
                        for c in range(2):
                            P.transpose(bk[:, c * 128:(c + 1) * 128], ctok[:, t2_, c * 128:(c + 1) * 128], ident[:], signal=False)
                        P.transpose(bk[0:32, 256:384], ctok[:, t2_, 256:288], ident[:], signal=True)
                        P.copy(ckvA[:, :, t2_ * 128:(t2_ + 1) * 128], V(bk.t[:, 0:256].rearrange("p (c t) -> p c t", t=128), bk.bufs))
                        P.copy(krT[64:96, t2_ * 128:(t2_ + 1) * 128], bk[0:32, 256:384], eng="scalar")
                    for r in range(4):
                        P.dma("sync", V(ckvA.t[:, :, 256 + r * NT:256 + (r + 1) * NT], ckvA.bufs),
                              V(cc1_out[r * 128:(r + 1) * 128, 0:1024].rearrange("p (c t) -> p c t", t=NT), [b_cc1o]))
                        P.dma("sync", V(krT.t[64:96, 256 + r * NT:256 + (r + 1) * NT], krT.bufs),
                              V(cc1_out[r * 128 + 64:r * 128 + 96, 1024:1536], [b_cc1o]))
                else:
                    krT, ckvA = krb, ckvT
                Vt = TT(P, "Vt", [128, nkt, 512], BF16, split=1)
                for kt in range(nkt):
                    bk = bank()
                    for kc in range(2):
                        P.mm(bk[:, :], ckvA[:, kc, kt * 128:(kt + 1) * 128], wkvbx[:, kc, 512:1024], start=(kc == 0), stop=(kc == 1))
                    P.copy(Vt[:, kt, :], bk[:, :], eng="scalar" if kt % 2 else "vector")
                KnT = TT(P, "KnT", [96, 2, nk], BF16, split=1)
                pT = TT(P, "pT", [128, 3, NT], BF16, split=1)
                pr = [0]
                rs = TT(P, "rs", [64, 2, NT], F32, split=1)
                blocks = [(0, NT, list(range(nkt)))] if sample else [(0, 256, [0, 1]), (256, 512, [2, 3])]
                sb_i = [0]
                for h in range(8):
                    kn = KnT[:, h % 2, :]
                    n0 = 0
                    while n0 < nk:
                        n1 = min(nk, n0 + 512)
                        bk = bank4()
                        for kc in range(2):
                            P.mm(bk[0:64, 0:n1 - n0], wkvbx[:, kc, h * 64:(h + 1) * 64], ckvA[:, kc, n0:n1], start=(kc == 0), stop=(kc == 1))
                        P.copy(V(kn.ap[0:64, n0:n1], kn.bufs), bk[0:64, 0:n1 - n0], eng="scalar" if (n0 // 512) % 2 else "vector")
                        n0 = n1
                    P.copy(V(kn.ap[64:96, :], kn.bufs), krT[64:96, 0:nk], eng="gpsimd")
                    for (q0, q1, kts) in blocks:
                        nq = q1 - q0
                        o_ps = banks[4 + (sb_i[0] % 2) * 2]
                        s_ps = banks[5 + (sb_i[0] % 2) * 2]
                        sb_i[0] += 1
                        def issue_score(kt):
                            sc_ = bank4()
                            P.mm(sc_[:, 0:nq], V(kn.ap[:, kt * 128:(kt + 1) * 128], kn.bufs), Q[0:96, h, q0:q1], start=True, stop=True)
                            return sc_

                        sc_next = issue_score(kts[0])
                        for i, kt in enumerate(kts):
                            sc = sc_next
                            if i + 1 < len(kts):
                                sc_next = issue_score(kts[i + 1])
                            p = pT[:, pr[0] % 3, 0:nq]
                            pr[0] += 1
                            P.act(p, sc[:, 0:nq], AF.Exp, scale=QKS)
                            last = (i == len(kts) - 1)
                            P.mm(o_ps[0:64, 0:nq], Vt[:, kt, h * 64:(h + 1) * 64], p, start=(i == 0), stop=last, signal=True)
                            P.mm(s_ps[0:64, 0:nq], ones_bf[:, 0:64], p, start=(i == 0), stop=last, signal=True)
                        r = rs[:, h % 2, 0:nq]
                        P.act(r, s_ps[0:64, 0:nq], AF.Ln)
                        P.act(r, r, AF.Exp, scale=-1.0)
                        r0 = (h % 2) * 64
                        P.tt(oaT[r0:r0 + 64, h // 2, q0:q1], o_ps[0:64, 0:nq], r, ALU.mult)

            b4 = [0]

            def bank4():
                b = banks[b4[0] % 4]
                b4[0] += 1
                return b

            def hgrn(g, sample, ohT, hgs, Qf):
                qh = TT(P, "qh", [128, 4, NT], F32, split=1)
                Vt = TT(P, "hV", [128, 4, 512], BF16, split=1)
                Vm = [TT(P, "hVm%d" % i, [128, 4, 512], BF16, split=1) for i in range(2)]
                Qt = [TT(P, "Qt%d" % d, [128, 4, NT], BF16, split=1) for d in range(2)]
                Kt = [TT(P, "Kt%d" % d, [128, 4, NT], BF16, split=1) for d in range(2)]
                Kh = [TT(P, "Kh%d" % d, [128, 4, 512], BF16, split=1) for d in range(2)]
                eg = TT(P, "eg", [128, 2, 4, 32], F32)
                xpay = TT(P, "xpay", [128, 520], F32) if sample else None
                (wv,) = ring.next()
                proj_chunks(wv, 4, g, lambda ci, bk: P.act(qh[:, ci, :], bk[:, :], AF.Silu))
                (wv,) = ring.next()
                HSKIP = os.environ.get("HSKIP", "")
                for tb in range(4):
                    if "v" in HSKIP:
                        break
                    bk = bank()
                    for kc in range(KC):
                        P.mm(bk[:, :], hT[g][:, kc, tb * 128:(tb + 1) * 128], V(wv.ap[:, kc, :], wv.bufs), start=(kc == 0), stop=(kc == KC - 1))
                    P.copy(Vt[:, tb, :], bk[:, :], eng="scalar")
                    if "m" in HSKIP:
                        continue
                    P.ts(Vm[0][:, tb, :], bk[:, :], rowm[:, 0:1], ALU.mult)
                    P.ts(Vm[1][:, tb, :], bk[:, :], rowm[:, 1:2], ALU.mult)
                (wv,) = ring.next()
                proj_chunks(wv, 4, g, lambda ci, bk: P.act(hgs[:, ci, :], bk[:, :], AF.Silu))
                HSUB = int(os.environ.get("HSUB", "99"))
                if HSUB < 2:
                    ring.next()
                    ring.next()
                    return
                with phase():
                    lg = TT(P, "lg", [128, 2, NT], F32, split=1)
                    bb = TT(P, "bb", [128, 1, NT], F32, split=1)
                    ff = TT(P, "ff", [128, 1, NT], F32, split=1)
                    khf = TT(P, "khf", [128, 1, NT], F32, split=1)
                    gch = TT(P, "gch", [128, 2, 32], F32, split=1)
                    tot = TT(P, "tot", [128, 2, 1], F32, split=1)
                    ii = [0]
                    for d in range(2):
                        (wv,) = ring.next()

                        def fgate(j, bk, d=d):
                            k = ii[0] % 2
                            ii[0] += 1
                            f, l, b, kf = ff[:, 0, :], lg[:, k, :], bb[:, 0, :], khf[:, 0, :]
                            gc = gch[:, k, :]
                            P.act(f, bk[:, :], AF.Sigmoid)
                            P.ts(f, f, oml[:, d, j:j + 1], ALU.mult, lb[:, d, j:j + 1], ALU.add)
                            P.act(l, f, AF.Ln)
                            P.ts(f, f, -1.0, ALU.mult, 1.0, ALU.add)
                            P.op("vector", lambda e: e.tensor_tensor_scan(out=b.ap, data0=scanm.t[:, :], data1=l.ap, initial=0.0,
                                                                         op0=ALU.mult, op1=ALU.add),
                                 reads=l.bufs + scanm.bufs, writes=b.bufs)
                            b3 = V(b.ap.rearrange("p (n c) -> p n c", c=16), b.bufs)
                            P.copy(gc, V(b3.ap[:, :, 15], b.bufs))
                            gbc = V(gc.ap.unsqueeze(2).to_broadcast([128, 32, 16]), gc.bufs)
                            if sample:
                                t = tmp()
                                P.op("vector", lambda e, t=t: e.tensor_tensor_scan(out=t.ap, data0=ones_f.t[:, 0:1].to_broadcast([128, NT]), data1=l.ap,
                                                                                  initial=0.0, op0=ALU.mult, op1=ALU.add),
                                     reads=l.bufs + ones_f.bufs, writes=t.bufs)
                                P.copy(xpay[:, 512 + d * 4 + j:512 + d * 4 + j + 1], V(t.ap[:, NT - 1:NT], t.bufs))
                                if d == 1:
                                    tt_ = tot[:, k, :]
                                    P.copy(tt_, V(t.ap[:, NT - 1:NT], t.bufs))
                                    P.stt(t, t, -1.0, l, ALU.mult, ALU.add)
                                    P.ts(t, t, tt_, ALU.add)
                                P.act(t, t, AF.Exp)
                                P.tt(Qf[d][:, j, :], qh[:, j, :], t, ALU.mult)
                            if d == 1:
                                P.tt(b, l, b, ALU.subtract)
                                P.tt(b3, b3, gbc, ALU.add)
                            P.act(eg[:, d, j, :], gc, AF.Exp)
                            t = tmp()
                            P.act(t, b, AF.Exp)
                            P.tt(Qt[d][:, j, :], qh[:, j, :], t, ALU.mult)
                            t = tmp()
                            P.act(t, b, AF.Exp, scale=-1.0)
                            P.tt(Kt[d][:, j, :], f, t, ALU.mult)
                            t = tmp()
                            P.tt(V(t.ap.rearrange("p (n c) -> p n c", c=16), t.bufs), gbc, b3, ALU.subtract)
                            P.act(t, t, AF.Exp)
                            P.tt(kf, f, t, ALU.mult)
                            bk2 = bank()
                            for tb in range(4):
                                P.transpose(bk2[:, tb * 128:(tb + 1) * 128], V(kf.ap[:, tb * 128:(tb + 1) * 128], kf.bufs), ident[:], signal=(tb == 3))
                            P.copy(Kh[d][:, :, j * 128:(j + 1) * 128], V(bk2.t[:, :].rearrange("p (t f) -> p t f", f=128), bk2.bufs), eng="scalar")

                        proj_chunks(wv, 4, g, fgate)
                if HSUB >= 3:
                    _hgrn_scan(g, sample, Vt, Vm, Qt, Kt, Kh, eg, ohT, xpay)

            def hmap(h):
                e = h // 2
                return ((e // 2) if h % 2 == 0 else 2 + e // 2, (e % 2) * 64)

            def _hgrn_scan(g, sample, Vt, Vm, Qt, Kt, Kh, eg, ohT, xpay):
                oacc = banks[0:4]
                ATs = TT(P, "ATs", [128, 4, 512], BF16, split=1)
                ai = [0]
                S = TT(P, "S", [128, 2, 4, 64], F32, split=1)
                Sb = TT(P, "Sb", [128, 2, 512], BF16, split=1)
                P.memset(S[:], 0.0)
                started = {}

                def st(key):
                    if key not in started:
                        started[key] = True
                        return True
                    return False

                for tb in range(4):
                    for d in range(2):
                        ats = []
                        for par in range(2):
                            Ab = banks[6 + par]
                            for idx in range(4):
                                h = idx * 2 + par
                                j, r0 = h // 2, (h % 2) * 64
                                P.mm(Ab[:, idx * 128:(idx + 1) * 128], Kt[d][r0:r0 + 64, j, tb * 128:(tb + 1) * 128],
                                     Qt[d][r0:r0 + 64, j, tb * 128:(tb + 1) * 128], start=True, stop=True, signal=(idx == 3))
                            at = ATs[:, (ai[0] % 2) * 2 + par, :]
                            P.tt(at, Ab[:, :], maskT[:, d, :], ALU.mult)
                            ats.append(at)
                        ai[0] += 1
                        for par in range(2):
                            for idx in range(4):
                                h = idx * 2 + par
                                jj, po = hmap(h)
                                P.mm(oacc[jj][po:po + 64, tb * 128:(tb + 1) * 128], Vt[:, tb, h * 64:(h + 1) * 64],
                                     V(ats[par].ap[:, idx * 128:(idx + 1) * 128], ats[par].bufs), start=st((jj, po)), stop=False,
                                     signal=(idx == 3), skip_group_check=True)
                if int(os.environ.get("HSUB", "99")) < 4:
                    return
                if sample:
                    order = [list(range(32)), list(range(31, -1, -1))]
                    resets = []
                else:
                    order = [list(range(32)), list(range(15, -1, -1)) + list(range(31, 15, -1))]
                    resets = [16]
                zero_state = True
                Sflat = lambda d: V(S.t[:, d, :, :].rearrange("p j v -> p (j v)"), [S.bufs[d]])
                for i in range(32):
                    if i in resets:
                        P.memset(S[:], 0.0)
                        zero_state = True
                    for d in range(2):
                        dSb = banks[4 + d]
                        c = order[d][i]
                        tbk, m, par = c // 8, (c % 8) // 2, c % 2
                        rows = slice(m * 32, m * 32 + 32)
                        for h in range(8):
                            j, r0 = h // 2, (h % 2) * 64
                            P.mm(dSb[r0:r0 + 64, j * 64:(j + 1) * 64], Kh[d][rows, tbk, h * 64:(h + 1) * 64],
                                 Vm[par][rows, tbk, h * 64:(h + 1) * 64], start=True, stop=True,
                                 signal=(h == 7), skip_group_check=True, tile_position=(m * 32, r0))
                    if not zero_state:
                        sb = Sb[:, i % 2, :]
                        P.copy(sb, V(S.t[:, :, :, :].rearrange("p d j v -> p (d j v)"), S.bufs), eng="scalar")
                        for d in range(2):
                            c = order[d][i]
                            for h in range(8):
                                j, r0 = h // 2, (h % 2) * 64
                                jj, po = hmap(h)
                                lastmm = (i == 31 and d == 1)
                                P.mm(oacc[jj][po:po + 64, c * 16:(c + 1) * 16],
                                     V(sb.ap[r0:r0 + 64, (d * 4 + j) * 64:(d * 4 + j + 1) * 64], sb.bufs),
                                     Qt[d][r0:r0 + 64, j, c * 16:(c + 1) * 16], start=False, stop=lastmm,
                                     signal=(lastmm or h == 7), skip_group_check=True)
                    for d in range(2):
                        c = order[d][i]
                        if not zero_state:
                            P.tt(S[:, d, :, :], S[:, d, :, :],
                                 V(eg.t[:, d, :, c:c + 1].to_broadcast([128, 4, 64]), eg.bufs), ALU.mult)
                            P.tt(Sflat(d), Sflat(d), banks[4 + d][:, 0:256], ALU.add)
                        else:
                            P.copy(Sflat(d), banks[4 + d][:, 0:256])
                    zero_state = False
                    if (not sample) and i in (15, 31):
                        sq = i // 16
                        P.dma("sync", V(nst_d[sq].rearrange("d (j hh) k v -> (hh k) d j v", hh=2), []), S[:], is_output=True)
                for j in range(4):
                    P.copy(ohT[:, j, :], oacc[j][:, :], eng="scalar" if j % 2 else "vector")
                if sample:
                    P.copy(xpay[:, 0:512], V(S.t[:, :, :, :].rearrange("p d j v -> p (d j v)"), S.bufs))
                    P.dma("sync", V(cc2_in[:, :], [b_cc2i]), xpay[:])
                    P.custom("gpsimd", lambda e: e.collective_compute("AllGather", ALU.bypass, replica_groups=[[0, 1, 2, 3], [4, 5, 6, 7]],
                                                                     ins=[cc2_in.ap().opt()], outs=[cc2_out.ap().opt()]),
                             reads=[b_cc2i], writes=[b_cc2o], sem_buf=b_cc2o)

            def hgrn_fix(ohT, Qf):
                gath = TT(P, "gath", [128, 4, 520], F32)
                P.dma("sync", gath[:], V(cc2_out.ap().rearrange("(r p) f -> p r f", p=128), [b_cc2o]))
                Sin = TT(P, "Sin", [128, 2, 4, 64], F32)
                P.dma("sync", Sin[:], V(s0_d.rearrange("d (j hh) k v -> (hh k) d j v", hh=2), []))
                egr = TT(P, "egr", [128, 4, 8], F32)
                P.act(egr[:], gath[:, :, 512:520], AF.Exp)
                t1 = TT(P, "sfx1", [128, 4, 64], F32)
                for d in range(2):
                    rs_ = [0, 1, 2] if d == 0 else [3, 2, 1]
                    for r in rs_:
                        sd = S_d = Sin[:, d, :, :]
                        P.tt(t1[:], sd, V(egr.t[:, r, d * 4:(d + 1) * 4].unsqueeze(2).to_broadcast([128, 4, 64]), egr.bufs), ALU.mult)
                        P.tt(t1[:], t1[:], V(gath.t[:, r, d * 256:(d + 1) * 256].rearrange("p (j v) -> p j v", v=64), gath.bufs), ALU.add)
                        P.tt(t1[:], t1[:], sd, ALU.subtract)
                        P.stt(sd, t1[:], sel[:, d * 4 + r:d * 4 + r + 1], sd, ALU.mult, ALU.add)
                Sinb = TT(P, "Sinb", [128, 512], BF16)
                P.copy(Sinb[:], V(Sin.t[:, :, :, :].rearrange("p d j v -> p (d j v)"), Sin.bufs))
                for d in range(2):
                    for h in range(8):
                        j, r0 = h // 2, (h % 2) * 64
                        jj, po = hmap(h)
                        P.mm(banks[jj][po:po + 64, :], Sinb[r0:r0 + 64, (d * 4 + j) * 64:(d * 4 + j + 1) * 64], Qf[d][r0:r0 + 64, j, :],
                             start=(d == 0), stop=(d == 1), signal=(d == 1), skip_group_check=True)
                for jj in range(4):
                    P.tt(ohT[:, jj, :], ohT[:, jj, :], banks[jj][:, :], ALU.add)

            def hgrn_norm(ohT, hgs, orT):
                for j in range(4):
                    bk = rstd_gen([ohT[:, j, :]], 64, ones=bdiag[:], bk=bank4())
                    t = tmp()
                    P.tt(t, ohT[:, j, :], bk[:, :], ALU.mult)
                    P.stt(orT[:, j, :], t, hgng[:, 0:1], hgs[:, j, :], ALU.mult, ALU.mult)

            def merge(g, oaT, orT):
                mT = TT(P, "mT", [128, KC, NT], BF16, split=1)
                for jp in range(4):
                    wga, wgr, wua, wur = ring.next()
                    for cc in range(2):
                        dc = jp * 2 + cc
                        b_ga, b_gr, b_ua, b_ur = bank(), bank(), bank(), bank()
                        for kc in range(KC):
                            P.mm(b_ga[:, :], V(wga.ap[:, kc, cc * 128:(cc + 1) * 128], wga.bufs), hT[g][:, kc, :], start=(kc == 0), stop=(kc == KC - 1))
                        for kc in range(KC):
                            P.mm(b_gr[:, :], V(wgr.ap[:, kc, cc * 128:(cc + 1) * 128], wgr.bufs), hT[g][:, kc, :], start=(kc == 0), stop=(kc == KC - 1))
                        for kc in range(4):
                            P.mm(b_ua[:, :], V(wua.ap[:, kc, cc * 128:(cc + 1) * 128], wua.bufs), oaT[:, kc, :], start=(kc == 0), stop=(kc == 3))
                        for kc in range(4):
                            P.mm(b_ur[:, :], V(wur.ap[:, kc, cc * 128:(cc + 1) * 128], wur.bufs), orT[:, kc, :], start=(kc == 0), stop=(kc == 3))
                        sa, sr = tmp(), tmp()
                        P.act(sa, b_ga[:, :], AF.Sigmoid)
                        P.act(sr, b_gr[:, :], AF.Sigmoid)
                        P.tt(sa, sa, b_ua[:, :], ALU.mult)
                        P.tt(sr, sr, b_ur[:, :], ALU.mult)
                        P.tt(mT[:, dc, :], sa, sr, ALU.add)
                for jp in range(2):
                    (wo,) = ring.next()
                    for cc in range(4):
                        dc = jp * 4 + cc
                        bk = bank()
                        for kc in range(KC):
                            P.mm(bk[:, :], V(wo.ap[:, kc, cc * 128:(cc + 1) * 128], wo.bufs), mT[:, kc, :], start=(kc == 0), stop=(kc == KC - 1))
                        P.stt(xT[g][:, dc, :], bk[:, :], g5[:, dc, g:g + 1], xT[g][:, dc, :], ALU.mult, ALU.add)

            for ch in KGRP:
                group(int(ch))

        if stage >= 5:
            b_cc1i, b_cc1o, b_cc2i, b_cc2o = Buf("cc1i"), Buf("cc1o"), Buf("cc2i"), Buf("cc2o")
            ones_f = TT(P, "ones_f", [128, 1], F32)
            P.memset(ones_f[:], 1.0)
            with ExitStack() as mph:
                P.alloc = mph
                mixer()
                P.barrier()
                P.alloc = es

        with ExitStack() as ph:
            P.alloc = ph
            actT = [TT(P, "actT%d" % g, [128, NFF, NT], BF16, split=1) for g in range(2)]
            if stage >= 4:
                for g in range(2):
                    norm_mod(g, 2, 6, 7)
                gs = ffn(8, actT)
                ffn_down(8, actT, gs)
            P.barrier()
            P.alloc = es

        with ExitStack() as ph:
            P.alloc = ph
            ytok = [TT(P, "ytok%d" % i, [128, D], F32) for i in range(2)]
            yT = TT(P, "yT", [128, KC, NT], F32, split=1)
            for g in range(2):
                bk = rstd_of(g)
                for c in range(KC):
                    P.stt(yT[:, c, :], xT[g][:, c, :], finalg[:, c:c + 1], bk[:, :], ALU.mult, ALU.mult)
                for tb in range(4):
                    yt = ytok[(g * 4 + tb) % 2]
                    for half in range(2):
                        b2 = bank()
                        for q in range(4):
                            c = half * 4 + q
                            P.transpose(b2[:, q * 128:(q + 1) * 128], yT[:, c, tb * 128:(tb + 1) * 128], ident[:], signal=(q == 3))
                        P.copy(yt[:, half * 512:(half + 1) * 512], b2[:, :], eng="scalar" if half else "vector")
                    P.dma("sync", V(y_d[g][tb * 128:(tb + 1) * 128, :], []), yt[:], is_output=True)
            P.barrier()
            P.alloc = es

        P.finish()
        print("instr counts", P.ninstr, "dma sems", P.dma_used)
    return nc


_NC_CACHE = {}


def _rope_tables():
    t = np.arange(2048)
    row = (t // 64).astype(np.float32)
    col = (t % 64).astype(np.float32)
    inv = (10000.0 ** (-np.arange(8, dtype=np.float32) / 8)).astype(np.float32)
    cos = np.zeros((32, 2048), np.float32)
    sin = np.zeros((32, 2048), np.float32)
    for half, pos in ((0, row), (1, col)):
        ang = pos[None, :] * inv[:, None]
        c, s_ = np.cos(ang).astype(np.float32), np.sin(ang).astype(np.float32)
        cos[half * 16:half * 16 + 8] = c
        cos[half * 16 + 8:half * 16 + 16] = c
        sin[half * 16:half * 16 + 8] = -s_
        sin[half * 16 + 8:half * 16 + 16] = s_
    return cos, sin


_HPERM = np.array([0, 2, 4, 6, 1, 3, 5, 7])
_PERM = np.array([j + 8 if (j % 16) < 8 else j - 8 for j in range(32)])


def _prep_inputs(inp):
    f = lambda a: np.ascontiguousarray(np.asarray(a, dtype=np.float32))
    xp = f(inp["x_prompt"])
    xs = f(inp["x_sample"])
    c = f(inp["c"])
    cctx = f(inp["c_ctx"])
    w_in = f(inp["w_in"][0])
    w_qb = f(inp["w_qb"][0])
    w_kvb = f(inp["w_kvb"][0])
    kr = w_in[:, 640:672]
    qbx = []
    for h in range(8):
        rp = w_qb[:, h * 96 + 64:h * 96 + 96]
        qbx += [w_qb[:, h * 96:h * 96 + 64], rp, rp[:, _PERM]]
    kvb = w_kvb.reshape(256, 8, 128)
    s_ = np.arange(128)[:, None]
    t_ = np.arange(128)[None, :]
    same = (s_ // 16) == (t_ // 16)
    mf = (same & (s_ <= t_)).astype(np.float32)
    mb = (same & (s_ >= t_)).astype(np.float32)
    maskT = np.stack([np.tile(mf, (1, 4)), np.tile(mb, (1, 4))], axis=1)
    scanm = np.ones((128, NT), np.float32)
    scanm[:, ::16] = 0.0
    par = (np.arange(128) // 16) % 2
    rowm = np.stack([(par == 0), (par == 1)], axis=1).astype(np.float32)
    bdiag = np.kron(np.eye(2, dtype=np.float32), np.ones((64, 64), np.float32))
    cos, sin = _rope_tables()
    hg = f(inp["hg_gamma"])
    shared = {
        "w_ada": f(inp["w_ada"][0]),
        "b_ada": f(inp["b_ada"][0].reshape(72, 128).T),
        "norm_g": f(inp["norm_g"][0].reshape(3, KC, 128).transpose(2, 0, 1)),
        "final_g": f(inp["final_g"].reshape(KC, 128).T),
        "ident": np.eye(128, dtype=np.float32),
        "w_in": w_in,
        "w_krx": f(np.concatenate([kr, kr, kr, kr[:, _PERM]], axis=1)),
        "w_qbx": f(np.concatenate(qbx, axis=1)),
        "w_kvbx": f(np.concatenate([kvb[:, :, :64].reshape(256, 512), kvb[:, :, 64:].reshape(256, 512)], axis=1)),
        "w_up_attn": f(inp["w_up_attn"][0]),
        "w_up_rec_p": f(inp["w_up_rec"][0].reshape(8, 64, D)[_HPERM].reshape(512, D)),
        "w_hgate_p": f(w_in[:, 2720:3232].reshape(D, 8, 64)[:, _HPERM].reshape(D, 512)),
        "w_out": f(inp["w_out"][0]),
        "maskT": f(maskT),
        "scanm": scanm,
        "rowm": rowm,
        "bdiag": bdiag,
        "hgam": f(hg.reshape(2, 2, 4, 128).transpose(3, 0, 1, 2)),
        "hgng": f(np.tile(inp["hg_norm_g"][0], 2).reshape(128, 1)),
        "qng": f(inp["q_norm_g"][0].reshape(3, 128).T),
        "kvng": f(inp["kv_norm_g"][0].reshape(2, 128).T),
    }
    shared["ffn1_w1"] = f(inp["ffn1_w1"][0])
    shared["ffn1_w3"] = f(inp["ffn1_w3"][0])
    shared["ffn1_w2"] = f(inp["ffn1_w2"][0])
    shared["ffn2_w1"] = f(inp["ffn2_w1"][0])
    shared["ffn2_w3"] = f(inp["ffn2_w3"][0])
    shared["ffn2_w2"] = f(inp["ffn2_w2"][0])
    maps = []
    for core in range(8):
        b, seg = core // 4, core % 4
        m = dict(shared)
        m["xs"] = f(xs[b, seg * NT:(seg + 1) * NT])
        m["xp"] = f(xp[2 * core:2 * core + 2].reshape(NT, D))
        cs = np.stack([c[b], cctx], axis=0)
        m["csel"] = f(cs.reshape(2, KC, 128).transpose(2, 1, 0))
        rope = np.zeros((128, NT), np.float32)
        rope[64:96] = cos[:, seg * NT:(seg + 1) * NT]
        rope[96:128] = sin[:, seg * NT:(seg + 1) * NT]
        m["rope"] = rope
        sel = np.zeros((128, 8), np.float32)
        for r in range(4):
            sel[:, r] = 1.0 if r < seg else 0.0
            sel[:, 4 + r] = 1.0 if r > seg else 0.0
        m["sel"] = sel
        m["cckv"] = f(inp["cache_ckv"][b, 0])
        m["ckr"] = f(inp["cache_krope"][b, 0])
        m["s0"] = f(inp["state_hgrn"][b, 0])
        maps.append(m)
    return maps


def kernel(**inp):
    if "nc" not in _NC_CACHE:
        _NC_CACHE["nc"] = build_program(int(os.environ.get("KSTAGE", "99")))
    nc = _NC_CACHE["nc"]
    maps = _prep_inputs(inp)
    res = run_bass_kernel_spmd(nc, maps, core_ids=list(range(8)))
    R = res.results
    y_prompt = np.zeros((16, 256, D), np.float32)
    y_sample = np.zeros((2, 2048, D), np.float32)
    new_ckv = np.zeros((16, 1, 256, 256), np.float32)
    new_kr = np.zeros((16, 1, 256, 32), np.float32)
    new_st = np.zeros((16, 1, 2, 8, 64, 64), np.float32)
    for core in range(8):
        b, seg = core // 4, core % 4
        y_sample[b, seg * NT:(seg + 1) * NT] = R[core]["ys"]
        y_prompt[2 * core:2 * core + 2] = R[core]["yp"].reshape(2, 256, D)
        new_ckv[2 * core:2 * core + 2, 0] = R[core]["nckv"].reshape(2, 256, 256)
        new_kr[2 * core:2 * core + 2, 0] = R[core]["nkr"].reshape(2, 256, 32)
        new_st[2 * core:2 * core + 2, 0] = R[core]["nst"]
    return (y_prompt, y_sample, new_ckv, new_kr, new_st)
```

```python
import os
import numpy as np
from contextlib import ExitStack, contextmanager
import concourse.bass as bass
import concourse.mybir as mybir
from concourse.bass_utils import run_bass_kernel_spmd

F32 = mybir.dt.float32
BF16 = mybir.dt.bfloat16
AF = mybir.ActivationFunctionType
ALU = mybir.AluOpType
AX = mybir.AxisListType

ENGS = ["tensor", "vector", "scalar", "gpsimd", "sync"]
D = 1024
DFF = 2816
NT = 512
KC = 8
NFF = 22
EPS = 1e-6


class Buf:
    def __init__(self, name):
        self.name = name
        self.w = {}
        self.r = {}
        self.dsem = None
        self.dcount = 0
        self.excl = False


class V:
    def __init__(self, ap, bufs):
        self.ap = ap
        self.bufs = list(bufs)


class TT:
    def __init__(self, P, name, shape, dt, split=None, psum=False):
        P.uid += 1
        nm = "t%d_%s" % (P.uid, name)
        self.t = P.psum_t(nm, shape, dt) if psum else P.sbuf_t(nm, shape, dt)
        self.shape = list(shape)
        self.split = split
        n = 1 if split is None else shape[split]
        self.bufs = [Buf("%s.%d" % (name, i)) for i in range(n)]
        for b in self.bufs:
            b.excl = psum

    def __getitem__(self, idx):
        if not isinstance(idx, tuple):
            idx = (idx,)
        if self.split is None or len(idx) <= self.split:
            bufs = self.bufs
        else:
            s = idx[self.split]
            if isinstance(s, slice):
                bufs = self.bufs[s]
            else:
                bufs = [self.bufs[s]]
        return V(self.t[idx], bufs)


class Prog:
    def __init__(self, nc, es, n_dma_sems=90, same_engine_waits=True):
        self.nc = nc
        self.es = es
        self.streams = {e: [] for e in ENGS}
        self.sem = {e: es.enter_context(nc.semaphore("s_" + e)) for e in ENGS}
        self.count = {e: 0 for e in ENGS}
        self.known = {e: {} for e in ENGS}
        self.dma_pool = [es.enter_context(nc.semaphore("d%d" % i)) for i in range(n_dma_sems)]
        self.dma_used = 0
        self.same_engine_waits = same_engine_waits
        self.out_tickets = []
        self.ninstr = {e: 0 for e in ENGS}
        self.alloc = es
        self.uid = 0
        self.dma_latest = {}
        self.cur_bytes = 0
        self.peak_bytes = 0

    def sbuf_t(self, name, shape, dt):
        nb = int(np.prod(shape[1:])) * (2 if dt == BF16 else 4)
        self.cur_bytes += nb
        self.peak_bytes = max(self.peak_bytes, self.cur_bytes)

        def dec(nb=nb):
            self.cur_bytes -= nb

        t = self.alloc.enter_context(self.nc.sbuf_tensor(name, list(shape), dt))
        self.alloc.callback(dec)
        return t

    def psum_t(self, name, shape, dt=F32):
        return self.alloc.enter_context(self.nc.psum_tensor(name, list(shape), dt))

    def _gather(self, eng, reads, writes):
        waits = {}

        def add(d):
            for k, (s, v) in d.items():
                if k not in waits or waits[k][1] < v:
                    waits[k] = (s, v)

        for b in reads:
            add(b.w)
        for b in writes:
            add(b.w)
            add(b.r)
        need = []
        kn = self.known[eng]
        for k, (s, v) in waits.items():
            if k == eng:
                if eng == "tensor" or not self.same_engine_waits:
                    continue
                v = min(v, self.count[eng])
                if v <= 0:
                    continue
            if kn.get(k, 0) < v:
                kn[k] = v
                need.append((s, v))
        return need

    def _record(self, key, ticket, reads, writes):
        for b in writes:
            b.w = {key: ticket}
            b.r = {}
        for b in reads:
            if b in writes:
                continue
            old = b.r.get(key)
            if old is None or old[1] < ticket[1]:
                b.r[key] = ticket

    def op(self, eng, fn, reads=(), writes=(), signal=True):
        reads = list(reads)
        writes = list(writes)
        if eng != "tensor":
            writes = writes + [b for b in reads if b.excl and b not in writes]
            reads = [b for b in reads if not b.excl]
        need = self._gather(eng, reads, writes)
        sem = self.sem[eng]
        if signal:
            self.count[eng] += 1
            val = self.count[eng]
        else:
            val = self.count[eng] + 1

        def run(e, need=need, fn=fn, signal=signal, sem=sem):
            for s, v in need:
                e.wait_ge(s, v)
            ins = fn(e)
            if signal:
                ins.then_inc(sem, 1)

        self.streams[eng].append(run)
        self.ninstr[eng] += 1
        self._record(eng, (sem, val), reads, writes)

    def _dsem(self, primary):
        if primary.dsem is None:
            primary.dsem = self.dma_pool[self.dma_used]
            primary.dkey = "dma%d" % self.dma_used
            self.dma_used += 1

    def dma(self, queue, out, in_, is_output=False, extra_reads=(), **kw):
        reads = list(in_.bufs) + list(extra_reads)
        writes = list(out.bufs)
        primary = writes[0] if writes else reads[0]
        self._dsem(primary)
        need = self._gather(queue, reads, writes)
        primary.dcount += 16
        sem = primary.dsem
        ticket = (sem, primary.dcount)
        out_ap, in_ap = out.ap, in_.ap

        def run(e, need=need, sem=sem, out_ap=out_ap, in_ap=in_ap, kw=kw):
            for s, v in need:
                e.wait_ge(s, v)
            e.dma_start(out=out_ap, in_=in_ap, **kw).then_inc(sem, 16)

        self.streams[queue].append(run)
        self.ninstr[queue] += 1
        self._record(primary.dkey, ticket, reads, writes)
        self.dma_latest[primary.dkey] = ticket
        if is_output:
            self.out_tickets.append((primary.dkey, ticket))

    def custom(self, eng, fn, reads, writes, sem_buf, inc=1):
        reads = list(reads)
        writes = list(writes)
        self._dsem(sem_buf)
        need = self._gather(eng, reads, writes)
        sem_buf.dcount += inc
        sem = sem_buf.dsem
        ticket = (sem, sem_buf.dcount)

        def run(e, need=need, sem=sem, fn=fn, inc=inc):
            for s, v in need:
                e.wait_ge(s, v)
            fn(e).then_inc(sem, inc)

        self.streams[eng].append(run)
        self._record(sem_buf.dkey, ticket, reads, writes)
        self.dma_latest[sem_buf.dkey] = ticket

    def mm(self, out, lhsT, rhs, start=True, stop=True, signal=None, **kw):
        if signal is None:
            signal = stop
        self.op("tensor", lambda e: e.matmul(out.ap, lhsT=lhsT.ap, rhs=rhs.ap, start=start, stop=stop, **kw),
                reads=lhsT.bufs + rhs.bufs, writes=out.bufs, signal=signal)

    def transpose(self, out, in_, ident, signal=True):
        self.op("tensor", lambda e: e.transpose(out.ap, in_.ap, ident.ap),
                reads=in_.bufs + ident.bufs, writes=out.bufs, signal=signal)

    def act(self, out, in_, func, scale=1.0, bias=None, eng="scalar"):
        reads = list(in_.bufs)
        kw = {}
        if isinstance(scale, V):
            reads += scale.bufs
            kw["scale"] = scale.ap
        else:
            kw["scale"] = scale
        if isinstance(bias, V):
            reads += bias.bufs
            kw["bias"] = bias.ap
        elif bias is not None:
            kw["bias"] = bias
        self.op(eng, lambda e: e.activation(out=out.ap, in_=in_.ap, func=func, **kw), reads=reads, writes=out.bufs)

    def tt(self, out, a, b, op, eng="vector"):
        self.op(eng, lambda e: e.tensor_tensor(out=out.ap, in0=a.ap, in1=b.ap, op=op),
                reads=a.bufs + b.bufs, writes=out.bufs)

    def ts(self, out, a, s1, op0, s2=None, op1=None, eng="vector"):
        reads = list(a.bufs)
        s1a = s1
        s2a = s2
        if isinstance(s1, V):
            reads += s1.bufs
            s1a = s1.ap
        if isinstance(s2, V):
            reads += s2.bufs
            s2a = s2.ap
        kw = {}
        if op1 is not None:
            kw = dict(scalar2=s2a, op1=op1)
        else:
            kw = dict(scalar2=None)
        self.op(eng, lambda e: e.tensor_scalar(out=out.ap, in0=a.ap, scalar1=s1a, op0=op0, **kw),
                reads=reads, writes=out.bufs)

    def stt(self, out, a, s, b, op0, op1):
        reads = a.bufs + b.bufs
        sa = s
        if isinstance(s, V):
            reads = reads + s.bufs
            sa = s.ap
        self.op("vector", lambda e: e.scalar_tensor_tensor(out=out.ap, in0=a.ap, scalar=sa, in1=b.ap, op0=op0, op1=op1),
                reads=reads, writes=out.bufs)

    def copy(self, out, in_, eng="vector"):
        if eng == "scalar":
            self.act(out, in_, AF.Copy)
        else:
            self.op(eng, lambda e: e.tensor_copy(out=out.ap, in_=in_.ap), reads=in_.bufs, writes=out.bufs)

    def memset(self, out, val, eng="vector"):
        self.op(eng, lambda e: e.memset(out.ap, val), writes=out.bufs)

    def barrier(self, engs=("tensor", "vector", "scalar", "gpsimd"), queues=("sync",)):
        for e in tuple(engs) + tuple(queues):
            need = []
            for dk, (s_, v_) in self.dma_latest.items():
                if self.known[e].get(dk, 0) < v_:
                    self.known[e][dk] = v_
                    need.append((s_, v_))
            for o in engs:
                if o == e:
                    continue
                v = self.count[o]
                if v > 0 and self.known[e].get(o, 0) < v:
                    self.known[e][o] = v
                    need.append((self.sem[o], v))

            def run(eng, need=need):
                for s_, v_ in need:
                    eng.wait_ge(s_, v_)

            self.streams[e].append(run)

    def finish(self):
        final = {}
        for k, (s, v) in self.out_tickets:
            if k not in final or final[k][1] < v:
                final[k] = (s, v)
        fl = list(final.values())

        def run(e, fl=fl):
            for s, v in fl:
                e.wait_ge(s, v)

        self.streams["sync"].append(run)
        with self.nc.Block() as block:
            for name in ENGS:
                stream = self.streams[name]

                def body(e, stream=stream):
                    for f in stream:
                        f(e)

                getattr(block, name)(body)


class WeightRing:
    def __init__(self, P, nslots, slot_elems):
        self.P = P
        self.slots = [TT(P, "wslot%d" % i, [128, slot_elems], BF16) for i in range(nslots)]
        self.n = nslots
        self.pieces = []
        self.issued = 0
        self.taken = 0

    def plan(self, pieces):
        self.pieces += pieces

    def _issue(self, i):
        slot = self.slots[i % self.n]
        off = 0
        for (src_ap, shape) in self.pieces[i]:
            ne = int(np.prod(shape))
            dst = slot.t[:, off:off + ne]
            if len(shape) == 2:
                dst = dst.rearrange("p (a b) -> p a b", b=shape[1])
            self.P.dma("gpsimd", V(dst, slot.bufs), V(src_ap, []))
            off += ne

    def next(self):
        i = self.taken
        while self.issued < min(len(self.pieces), i + self.n):
            self._issue(self.issued)
            self.issued += 1
        self.taken += 1
        slot = self.slots[i % self.n]
        views = []
        off = 0
        for (src_ap, shape) in self.pieces[i]:
            ne = int(np.prod(shape))
            t = slot.t[:, off:off + ne]
            if len(shape) == 2:
                t = t.rearrange("p (a b) -> p a b", b=shape[1])
            views.append(V(t, slot.bufs))
            off += ne
        return views


def build_program(stage=99):
    nc = bass.Bass("TRN2", target_bir_lowering=False)

    def din(name, shape, dt=F32):
        return nc.dram_tensor(name, list(shape), dt, kind="ExternalInput").ap()

    def dout(name, shape, dt=F32):
        return nc.dram_tensor(name, list(shape), dt, kind="ExternalOutput").ap()

    x_d = [din("xs", [NT, D]), din("xp", [NT, D])]
    csel_d = din("csel", [128, KC, 2])
    wada_d = din("w_ada", [D, 9 * D])
    bada_d = din("b_ada", [128, 72])
    normg_d = din("norm_g", [128, 3, KC])
    finalg_d = din("final_g", [128, KC])
    ident_d = din("ident", [128, 128])
    ffn_d = [[din("ffn%d_w%d" % (f, k), [D, DFF] if k != 2 else [DFF, D]) for k in (1, 3, 2)] for f in (1, 2)]
    win_d = din("w_in", [D, 5280])
    wkrx_d = din("w_krx", [D, 128])
    wqbx_d = din("w_qbx", [384, 1024])
    wkvbx_d = din("w_kvbx", [256, 1024])
    wua_d = din("w_up_attn", [512, D])
    wur_d = din("w_up_rec_p", [512, D])
    whg_d = din("w_hgate_p", [D, 512])
    wout_d = din("w_out", [D, D])
    rope_d = din("rope", [128, NT])
    maskT_d = din("maskT", [128, 2, 512])
    scanm_d = din("scanm", [128, NT])
    rowm_d = din("rowm", [128, 2])
    bdiag_d = din("bdiag", [128, 128])
    hgam_d = din("hgam", [128, 2, 2, 4])
    hgng_d = din("hgng", [128, 1])
    qng_d = din("qng", [128, 3])
    kvng_d = din("kvng", [128, 2])
    sel_d = din("sel", [128, 8])
    cckv_d = din("cckv", [256, 256])
    ckr_d = din("ckr", [256, 32])
    s0_d = din("s0", [2, 8, 64, 64])
    cc1_in = nc.dram_tensor("cc1_in", [128, 1536], BF16)
    cc1_out = nc.dram_tensor("cc1_out", [512, 1536], BF16)
    cc2_in = nc.dram_tensor("cc2_in", [128, 520], F32)
    cc2_out = nc.dram_tensor("cc2_out", [512, 520], F32)
    y_d = [dout("ys", [NT, D]), dout("yp", [NT, D])]
    nckv_d = dout("nckv", [NT, 256])
    nkr_d = dout("nkr", [NT, 32])
    nst_d = dout("nst", [2, 2, 8, 64, 64])

    with ExitStack() as es:
        P = Prog(nc, es)
        ring = WeightRing(P, 3, 6144)

        ident = TT(P, "ident", [128, 128], F32)
        P.dma("sync", ident[:], V(ident_d[:, :], []))
        ones_bf = TT(P, "ones_bf", [128, 128], BF16)
        P.memset(ones_bf[:], 1.0)
        epsb = TT(P, "epsb", [128, 1], F32)
        P.memset(epsb[:], EPS)
        xT = [TT(P, "xT%d" % g, [128, KC, NT], F32, split=1) for g in range(2)]
        hT = [None, None]
        banks = [TT(P, "bank%d" % i, [128, 512], F32, psum=True) for i in range(8)]
        bank_rr = [0]

        def bank():
            b = banks[bank_rr[0] % 8]
            bank_rr[0] += 1
            return b

        normg = TT(P, "normg", [128, 3, KC], F32)
        P.dma("sync", normg[:], V(normg_d[:, :, :], []))
        finalg = TT(P, "finalg", [128, KC], F32)
        P.dma("sync", finalg[:], V(finalg_d[:, :], []))
        bada = TT(P, "bada", [128, 72], F32)
        P.dma("sync", bada[:], V(bada_d[:, :], []))
        csel = TT(P, "csel", [128, KC, 2], F32)
        P.dma("sync", csel[:], V(csel_d[:, :, :], []))
        scb = TT(P, "scb", [128, KC, 2], BF16)
        P.act(scb[:], csel[:], AF.Silu)
        mod = TT(P, "mod", [128, 72, 2], F32)

        wada_v = wada_d.rearrange("(kc p) n -> p kc n", p=128)

        def ada_piece(i):
            return [[(wada_v[:, :, i * D + h * 512: i * D + (h + 1) * 512], [KC, 512])] for h in range(2)]

        def ffn_pieces(f):
            w1, w3, w2 = ffn_d[f]
            w1v = w1.rearrange("(kc p) n -> p kc n", p=128)
            w3v = w3.rearrange("(kc p) n -> p kc n", p=128)
            w2v = w2.rearrange("(kc p) n -> p kc n", p=128)
            ps = []
            for j in range(NFF // 2):
                ps.append([(w1v[:, :, j * 256:(j + 1) * 256], [KC, 256]), (w3v[:, :, j * 256:(j + 1) * 256], [KC, 256])])
            for j in range(4):
                ps.append([(w2v[:, :, j * 256:(j + 1) * 256], [NFF, 256])])
            return ps

        ring.plan(ada_piece(0) + ada_piece(1))
        ada_rest = [(i, h) for i in range(2, 9) for h in range(2)]
        ada_sched = {}
        pos = 0
        for j in range(11):
            nh = 2 if j < 3 else 1
            ada_sched[j] = ada_rest[pos:pos + nh]
            pos += nh
        fp0 = ffn_pieces(0)
        for j in range(11):
            ring.plan([fp0[j]])
            for (i, h) in ada_sched[j]:
                ring.plan([ada_piece(i)[h]])
        ring.plan(fp0[11:])
        winv = win_d.rearrange("(kc p) n -> p kc n", p=128)
        wuav = wua_d.rearrange("(kc p) n -> p kc n", p=128)
        wurv = wur_d.rearrange("(kc p) n -> p kc n", p=128)
        woutv = wout_d.rearrange("(kc p) n -> p kc n", p=128)
        if stage >= 5:
            KSUB = int(os.environ.get("KSUB", "99"))
            KGRP = os.environ.get("KGRP", "10")
            for g in [int(ch) for ch in KGRP]:
                ring.plan([[(winv[:, :, 0:640], [KC, 640])]])
                if KSUB >= 2:
                    whgv = whg_d.rearrange("(kc p) n -> p kc n", p=128)
                    for c0 in (672, 2208, 2720, 1184, 1696):
                        if c0 == 2720:
                            ring.plan([[(whgv[:, :, :], [KC, 512])]])
                        else:
                            ring.plan([[(winv[:, :, c0:c0 + 512], [KC, 512])]])
                for j in range(4 if KSUB >= 5 else 0):
                    ring.plan([[(winv[:, :, 3232 + j * 256:3232 + (j + 1) * 256], [KC, 256]),
                                (winv[:, :, 4256 + j * 256:4256 + (j + 1) * 256], [KC, 256]),
                                (wuav[:, :, j * 256:(j + 1) * 256], [4, 256]),
                                (wurv[:, :, j * 256:(j + 1) * 256], [4, 256])]])
                for j in range(2 if KSUB >= 5 else 0):
                    ring.plan([[(woutv[:, :, j * 512:(j + 1) * 512], [KC, 512])]])
        ring.plan(ffn_pieces(1))

        with ExitStack() as ph:
            P.alloc = ph
            xtok = [TT(P, "xtok%d" % i, [128, D], F32) for i in range(2)]
            for g in range(2):
                for tb in range(4):
                    xt = xtok[(g * 4 + tb) % 2]
                    P.dma("sync", xt[:], V(x_d[g][tb * 128:(tb + 1) * 128, :], []))
                    for half in range(2):
                        bk = bank()
                        for q in range(4):
                            c = half * 4 + q
                            P.transpose(bk[:, q * 128:(q + 1) * 128], xt[:, c * 128:(c + 1) * 128], ident[:], signal=(q == 3))
                        P.copy(xT[g][:, half * 4:half * 4 + 4, tb * 128:(tb + 1) * 128],
                               V(bk.t[:, :].rearrange("p (q t) -> p q t", t=128), bk.bufs),
                               eng="scalar" if half else "vector")
            P.barrier()
            P.alloc = es

        def ada_half(i, h):
            bk = bank()
            (wv,) = ring.next()
            for cc in range(4):
                for kc in range(KC):
                    P.mm(bk[:, cc * 2:cc * 2 + 2], wv_slice(wv, kc, cc), scb[:, kc, :], start=(kc == 0), stop=(kc == KC - 1))
            c0 = i * 8 + h * 4
            P.tt(mod[:, c0:c0 + 4, :], V(bk.t[:, 0:8].rearrange("p (c g) -> p c g", g=2), bk.bufs),
                 V(bada.t[:, c0:c0 + 4].unsqueeze(2).to_broadcast([128, 4, 2]), bada.bufs), ALU.add)

        def ada_compute(i):
            ada_half(i, 0)
            ada_half(i, 1)

        def wv_slice(wv, kc, cc):
            return V(wv.ap[:, kc, cc * 128:(cc + 1) * 128], wv.bufs)

        nscr = TT(P, "nscr", [128, 2, NT], BF16, split=1)
        nscr_f = TT(P, "nscr_f", [128, 2, NT], F32, split=1)
        amod = TT(P, "amod", [128, KC], F32)
        rr = [0]

        def rstd_gen(srcs, nfeat, ones=None, bk=None):
            if bk is None:
                bk = bank()
            if ones is None:
                ones = ones_bf[:]
            for c, sv in enumerate(srcs):
                s = nscr[:, rr[0] % 2, :]
                rr[0] += 1
                P.act(s, sv, AF.Square)
                P.mm(bk[:, :], ones, s, start=(c == 0), stop=(c == len(srcs) - 1), signal=True)
            P.act(bk[:, :], bk[:, :], AF.Ln, scale=1.0 / nfeat, bias=epsb[:, 0:1])
            P.act(bk[:, :], bk[:, :], AF.Exp, scale=-0.5)
            return bk

        def rstd_of(g):
            return rstd_gen([xT[g][:, c, :] for c in range(KC)], D)

        def norm_mod(g, ni, i_shift, i_scale):
            P.ts(amod[:], mod[:, i_scale * 8:(i_scale + 1) * 8, g], 1.0, ALU.add)
            P.tt(amod[:], amod[:], normg[:, ni, :], ALU.mult)
            bk = rstd_of(g)
            for c in range(KC):
                s = nscr_f[:, rr[0] % 2, :]
                rr[0] += 1
                P.tt(s, xT[g][:, c, :], bk[:, :], ALU.mult)
                P.act(hT[g][:, c, :], s, AF.Identity, scale=amod[:, c:c + 1], bias=mod[:, i_shift * 8 + c, g:g + 1])

        def ffn(i_gate, actT, extra=None):
            gs = TT(P, "gs%d" % i_gate, [128, KC, 2], F32)
            for j in range(NFF // 2):
                if extra is not None and j > 0:
                    extra(j - 1)
                w1v, w3v = ring.next()
                for cc in range(2):
                    ch = j * 2 + cc
                    for g in range(2):
                        ba, bb = bank(), bank()
                        for kc in range(KC):
                            P.mm(ba[:, :], wv_slice(w1v, kc, cc), hT[g][:, kc, :], start=(kc == 0), stop=(kc == KC - 1))
                        for kc in range(KC):
                            P.mm(bb[:, :], wv_slice(w3v, kc, cc), hT[g][:, kc, :], start=(kc == 0), stop=(kc == KC - 1))
                        s = nscr_f[:, rr[0] % 2, :]
                        rr[0] += 1
                        P.act(s, ba[:, :], AF.Silu)
                        P.tt(actT[g][:, ch, :], s, bb[:, :], ALU.mult)
            return gs

        def ffn_down(i_gate, actT, gs):
            P.ts(gs[:], mod[:, i_gate * 8:(i_gate + 1) * 8, :], 0.5, ALU.mult)
            for j in range(4):
                (w2v,) = ring.next()
                for cc in range(2):
                    dc = j * 2 + cc
                    for g in range(2):
                        bk = bank()
                        for kc in range(NFF):
                            P.mm(bk[:, :], wv_slice(w2v, kc, cc), actT[g][:, kc, :], start=(kc == 0), stop=(kc == NFF - 1))
                        P.stt(xT[g][:, dc, :], bk[:, :], gs[:, dc, g:g + 1], xT[g][:, dc, :], ALU.mult, ALU.add)

        if stage >= 1:
            ada_compute(0)
            ada_compute(1)
        with ExitStack() as ph:
            P.alloc = ph
            actT = [TT(P, "actT%d" % g, [128, NFF, NT], BF16, split=1) for g in range(2)]
            for g in range(2):
                hT[g] = TT(P, "hT%d" % g, [128, KC, NT], BF16, split=1)
            if stage >= 2:
                for g in range(2):
                    norm_mod(g, 0, 0, 1)
            if stage >= 3:
                def ada_extra(j):
                    for (i, h) in ada_sched[j]:
                        ada_half(i, h)

                gs = ffn(2, actT, extra=ada_extra)
                ada_extra(10)
                ffn_down(2, actT, gs)
            P.barrier()
            P.alloc = es

        def mixer():
            ph = P.alloc
            QKS = (64 + 32) ** -0.5
            rope = TT(P, "rope", [128, NT], F32)
            P.dma("sync", rope[:], V(rope_d[:, :], []))
            maskT = TT(P, "maskT", [128, 2, 512], BF16)
            P.dma("gpsimd", maskT[:], V(maskT_d[:, :, :], []))
            scanm = TT(P, "scanm", [128, NT], F32)
            P.dma("sync", scanm[:], V(scanm_d[:, :], []))
            rowm = TT(P, "rowm", [128, 2], F32)
            P.dma("sync", rowm[:], V(rowm_d[:, :], []))
            bdiag = TT(P, "bdiag", [128, 128], BF16)
            P.dma("gpsimd", bdiag[:], V(bdiag_d[:, :], []))
            hgam = TT(P, "hgam", [128, 2, 2, 4], F32)
            P.dma("sync", hgam[:], V(hgam_d[:, :, :, :], []))
            hgng = TT(P, "hgng", [128, 1], F32)
            P.dma("sync", hgng[:], V(hgng_d[:, :], []))
            qng = TT(P, "qng", [128, 3], F32)
            P.dma("sync", qng[:], V(qng_d[:, :], []))
            kvng = TT(P, "kvng", [128, 2], F32)
            P.dma("sync", kvng[:], V(kvng_d[:, :], []))
            sel = TT(P, "sel", [128, 8], F32)
            P.dma("sync", sel[:], V(sel_d[:, :], []))
            lb = TT(P, "lb", [128, 2, 4], F32)
            oml = TT(P, "oml", [128, 2, 4], F32)
            P.tt(lb[:], hgam[:, :, 0, :], hgam[:, :, 1, :], ALU.subtract)
            P.act(lb[:], lb[:], AF.Sigmoid)
            P.ts(oml[:], lb[:], -1.0, ALU.mult, 1.0, ALU.add)
            g5 = TT(P, "g5", [128, KC, 2], F32)
            P.copy(g5[:], mod[:, 40:48, :])

            tmpf = TT(P, "tmpf", [128, 4, NT], F32, split=1)
            tr = [0]

            @contextmanager
            def phase():
                outer = P.alloc
                with ExitStack() as st_:
                    P.alloc = st_
                    yield
                    P.barrier()
                P.alloc = outer

            def tmp():
                t = tmpf[:, tr[0] % 4, :]
                tr[0] += 1
                return t

            def proj_chunks(wv, ncols_chunks, g, fn):
                for ci in range(ncols_chunks):
                    bk = bank()
                    for kc in range(KC):
                        P.mm(bk[:, :], V(wv.ap[:, kc, ci * 128:(ci + 1) * 128], wv.bufs), hT[g][:, kc, :],
                             start=(kc == 0), stop=(kc == KC - 1))
                    fn(ci, bk)

            def group(g):
                sample = (g == 0)
                with phase():
                    hT[g] = TT(P, "hTm%d" % g, [128, KC, NT], BF16, split=1)
                    norm_mod(g, 1, 3, 4)
                    Q = TT(P, "Q", [128, 8, NT], BF16, split=1)
                    ckvT = TT(P, "ckvT", [128, 2, NT], BF16)
                    krb = TT(P, "krb", [128, NT], BF16)
                    P.memset(krb[:], 0.0)
                    ohT = TT(P, "ohT", [128, 4, NT], F32, split=1)
                    hgs = TT(P, "hgs", [128, 4, NT], BF16, split=1)
                    oaT = TT(P, "oaT", [128, 4, NT], BF16, split=1)
                    orT = TT(P, "orT", [128, 4, NT], BF16, split=1)
                    Qf = [TT(P, "Qf%d" % d, [128, 4, NT], BF16, split=1) for d in range(2)] if sample else None
                    with phase():
                        front(g, sample, Q, ckvT, krb)
                    if KSUB >= 2:
                        with phase():
                            hgrn(g, sample, ohT, hgs, Qf)
                    if KSUB >= 3:
                        with phase():
                            attention(g, sample, Q, ckvT, krb, oaT)
                    if KSUB >= 4:
                        with phase():
                            if sample:
                                hgrn_fix(ohT, Qf)
                            hgrn_norm(ohT, hgs, orT)
                    if KSUB >= 5:
                        with phase():
                            merge(g, oaT, orT)

            def front(g, sample, Q, ckvT, krb):
                wkrx = TT(P, "wkrx", [128, KC, 128], BF16)
                P.dma("gpsimd", wkrx[:], V(wkrx_d.rearrange("(kc p) n -> p kc n", p=128), []))
                wqbx = TT(P, "wqbx", [128, 3, 1024], BF16)
                P.dma("gpsimd", wqbx[:], V(wqbx_d.rearrange("(kc p) n -> p kc n", p=128), []))
                qkT = TT(P, "qkT", [128, 5, NT], F32, split=1)
                (w1v,) = ring.next()
                proj_chunks(w1v, 5, g, lambda ci, bk: P.copy(qkT[:, ci, :], bk[:, :], eng="scalar"))
                qnT = TT(P, "qnT", [128, 3, NT], BF16)
                bk = rstd_gen([qkT[:, c, :] for c in range(3)], 384)
                for c in range(3):
                    P.stt(qnT[:, c, :], qkT[:, c, :], qng[:, c:c + 1], bk[:, :], ALU.mult, ALU.mult)
                bk = rstd_gen([qkT[:, 3 + c, :] for c in range(2)], 256)
                for c in range(2):
                    P.stt(qkT[:, 3 + c, :], qkT[:, 3 + c, :], kvng[:, c:c + 1], bk[:, :], ALU.mult, ALU.mult)
                    P.copy(ckvT[:, c, :], qkT[:, 3 + c, :], eng="scalar")
                krf = TT(P, "krf", [128, NT], F32)
                bk = bank()
                for kc in range(KC):
                    P.mm(bk[:, :], wkrx[:, kc, :], hT[g][:, kc, :], start=(kc == 0), stop=(kc == KC - 1))
                P.copy(krf[:], bk[:, :], eng="scalar")
                if not sample:
                    otok = TT(P, "otok", [128, 2, 288], F32, split=1)
                    for tb in range(4):
                        bk = bank()
                        for c in range(2):
                            P.transpose(bk[:, c * 128:(c + 1) * 128], qkT[:, 3 + c, tb * 128:(tb + 1) * 128], ident[:], signal=False)
                        P.transpose(bk[:, 256:288], krf[0:32, tb * 128:(tb + 1) * 128], ident[0:32, 0:32], signal=True)
                        P.copy(otok[:, tb % 2, :], bk[:, 0:288])
                        P.dma("sync", V(nckv_d[tb * 128:(tb + 1) * 128, :], []), otok[:, tb % 2, 0:256], is_output=True)
                        P.dma("sync", V(nkr_d[tb * 128:(tb + 1) * 128, :], []), otok[:, tb % 2, 256:288], is_output=True)
                    P.copy(krb[64:96, :], krf[64:96, :])
                else:
                    t1 = tmp()
                    t2 = tmp()
                    P.tt(V(t1.ap[64:96, :], t1.bufs), krf[64:96, :], rope[64:96, :], ALU.mult)
                    P.tt(V(t2.ap[64:96, :], t2.bufs), krf[96:128, :], rope[96:128, :], ALU.mult)
                    P.tt(krb[64:96, :], V(t1.ap[64:96, :], t1.bufs), V(t2.ap[64:96, :], t2.bufs), ALU.add)
                for h in range(8):
                    bk = bank()
                    for kc in range(3):
                        P.mm(bk[:, :], wqbx[:, kc, h * 128:(h + 1) * 128], qnT[:, kc, :], start=(kc == 0), stop=(kc == 2))
                    if sample:
                        P.copy(Q[0:64, h, :], bk[0:64, :], eng="scalar")
                        t1 = tmp()
                        t2 = tmp()
                        P.tt(V(t1.ap[64:96, :], t1.bufs), bk[64:96, :], rope[64:96, :], ALU.mult)
                        P.tt(V(t2.ap[64:96, :], t2.bufs), bk[96:128, :], rope[96:128, :], ALU.mult)
                        P.tt(Q[64:96, h, :], V(t1.ap[64:96, :], t1.bufs), V(t2.ap[64:96, :], t2.bufs), ALU.add)
                    else:
                        P.copy(Q[0:96, h, :], bk[0:96, :], eng="scalar")
                if sample:
                    P.dma("sync", V(cc1_in[:, 0:1024].rearrange("p (c t) -> p c t", t=NT), [b_cc1i]), ckvT[:])
                    P.dma("sync", V(cc1_in[:, 1024:1536], [b_cc1i]), krb[:, :])
                    P.custom("gpsimd", lambda e: e.collective_compute("AllGather", ALU.bypass, replica_groups=[[0, 1, 2, 3], [4, 5, 6, 7]],
                                                                     ins=[cc1_in.ap().opt()], outs=[cc1_out.ap().opt()]),
                             reads=[b_cc1i], writes=[b_cc1o], sem_buf=b_cc1o)

            def attention(g, sample, Q, ckvT, krb, oaT):
                nk = 2304 if sample else NT
                nkt = nk // 128
                wkvbx = TT(P, "wkvbx", [128, 2, 1024], BF16)
                P.dma("gpsimd", wkvbx[:], V(wkvbx_d.rearrange("(kc p) n -> p kc n", p=128), []))
                if sample:
                    krT = TT(P, "krT", [128, nk], BF16)
                    ckvA = TT(P, "ckvA", [128, 2, nk], BF16)
                    ctok = TT(P, "ctok", [128, 2, 288], F32)
                    for t2_ in range(2):
                        P.dma("sync", ctok[:, t2_, 0:256], V(cckv_d[t2_ * 128:(t2_ + 1) * 128, :], []))
                        P.dma("sync", ctok[:, t2_, 256:288], V(ckr_d[t2_ * 128:(t2_ + 1) * 128, :], []))
                    for t2_ in range(2):
                        bk = bank()
                        for c in range(2):
                            P.transpose(bk[:, c * 128:(c + 1) * 128], ctok[:, t2_, c * 128:(c + 1) * 128], ident[:], signal=False)
                        P.transpose(bk[0:32, 256:384], ctok[:, t2_, 256:288], ident[:], signal=True)
                        P.copy(ckvA[:, :, t2_ * 128:(t2_ + 1) * 128], V(bk.t[:, 0:256].rearrange("p (c t) -> p c t", t=128), bk.bufs))
                        P.copy(krT[64:96, t2_ * 128:(t2_ + 1) * 128], bk[0:32, 256:384], eng="scalar")
                    for r in range(4):
                        P.dma("sync", V(ckvA.t[:, :, 256 + r * NT:256 + (r + 1) * NT], ckvA.bufs),
                              V(cc1_out[r * 128:(r + 1) * 128, 0:1024].rearrange("p (c t) -> p c t", t=NT), [b_cc1o]))
                        P.dma("sync", V(krT.t[64:96, 256 + r * NT:256 + (r + 1) * NT], krT.bufs),
                              V(cc1_out[r * 128 + 64:r * 128 + 96, 1024:1536], [b_cc1o]))
                else:
                    krT, ckvA = krb, ckvT
                Vt = TT(P, "Vt", [128, nkt, 512], BF16, split=1)
                for kt in range(nkt):
                    bk = bank()
                    for kc in range(2):
                        P.mm(bk[:, :], ckvA[:, kc, kt * 128:(kt + 1) * 128], wkvbx[:, kc, 512:1024], start=(kc == 0), stop=(kc == 1))
                    P.copy(Vt[:, kt, :], bk[:, :], eng="scalar" if kt % 2 else "vector")
                KnT = TT(P, "KnT", [96, 2, nk], BF16, split=1)
                pT = TT(P, "pT", [128, 5, NT], BF16, split=1)
                pr = [0]
                rs = TT(P, "rs", [64, 2, NT], F32, split=1)
                blocks = [(0, NT, list(range(nkt)))] if sample else [(0, 256, [0, 1]), (256, 512, [2, 3])]
                for hp in range(4):
                    heads = (2 * hp, 2 * hp + 1)
                    kns = {}
                    for h in heads:
                        kn = KnT[:, h % 2, :]
                        kns[h] = kn
                        n0 = 0
                        while n0 < nk:
                            n1 = min(nk, n0 + 512)
                            bk = bank4()
                            for kc in range(2):
                                P.mm(bk[0:64, 0:n1 - n0], wkvbx[:, kc, h * 64:(h + 1) * 64], ckvA[:, kc, n0:n1], start=(kc == 0), stop=(kc == 1))
                            P.copy(V(kn.ap[0:64, n0:n1], kn.bufs), bk[0:64, 0:n1 - n0], eng="scalar" if (n0 // 512) % 2 else "vector")
                            n0 = n1
                        P.copy(V(kn.ap[64:96, :], kn.bufs), krT[64:96, 0:nk], eng="gpsimd")
                    for (q0, q1, kts) in blocks:
                        nq = q1 - q0
                        accs = {heads[0]: (banks[4], banks[5]), heads[1]: (banks[6], banks[7])}

                        def issue_score(h, kt):
                            sc_ = bank4()
                            P.mm(sc_[:, 0:nq], V(kns[h].ap[:, kt * 128:(kt + 1) * 128], kns[h].bufs), Q[0:96, h, q0:q1], start=True, stop=True)
                            return sc_

                        sc_next = {h: issue_score(h, kts[0]) for h in heads}
                        for i, kt in enumerate(kts):
                            sc = dict(sc_next)
                            if i + 1 < len(kts):
                                sc_next = {h: issue_score(h, kts[i + 1]) for h in heads}
                            last = (i == len(kts) - 1)
                            for h in heads:
                                o_ps, s_ps = accs[h]
                                p = pT[:, pr[0] % 5, 0:nq]
                                pr[0] += 1
                                P.act(p, sc[h][:, 0:nq], AF.Exp, scale=QKS)
                                P.mm(o_ps[0:64, 0:nq], Vt[:, kt, h * 64:(h + 1) * 64], p, start=(i == 0), stop=last, signal=True)
                                P.mm(s_ps[0:64, 0:nq], ones_bf[:, 0:64], p, start=(i == 0), stop=last, signal=True)
                        for h in heads:
                            o_ps, s_ps = accs[h]
                            r = rs[:, h % 2, 0:nq]
                            P.act(r, s_ps[0:64, 0:nq], AF.Ln)
                            P.act(r, r, AF.Exp, scale=-1.0)
                            r0 = (h % 2) * 64
                            P.tt(oaT[r0:r0 + 64, h // 2, q0:q1], o_ps[0:64, 0:nq], r, ALU.mult)

            b4 = [0]

            def bank4():
                b = banks[b4[0] % 4]
                b4[0] += 1
                return b

            def hgrn(g, sample, ohT, hgs, Qf):
                qh = TT(P, "qh", [128, 4, NT], F32, split=1)
                Vt = TT(P, "hV", [128, 4, 512], BF16, split=1)
                Vm = [TT(P, "hVm%d" % i, [128, 4, 512], BF16, split=1) for i in range(2)]
                Qt = [TT(P, "Qt%d" % d, [128, 4, NT], BF16, split=1) for d in range(2)]
                Kt = [TT(P, "Kt%d" % d, [128, 4, NT], BF16, split=1) for d in range(2)]
                Kh = [TT(P, "Kh%d" % d, [128, 4, 512], BF16, split=1) for d in range(2)]
                eg = TT(P, "eg", [128, 2, 4, 32], F32)
                xpay = TT(P, "xpay", [128, 520], F32) if sample else None
                (wv,) = ring.next()
                proj_chunks(wv, 4, g, lambda ci, bk: P.act(qh[:, ci, :], bk[:, :], AF.Silu))
                (wv,) = ring.next()
                HSKIP = os.environ.get("HSKIP", "")
                for tb in range(4):
                    if "v" in HSKIP:
                        break
                    bk = bank()
                    for kc in range(KC):
                        P.mm(bk[:, :], hT[g][:, kc, tb * 128:(tb + 1) * 128], V(wv.ap[:, kc, :], wv.bufs), start=(kc == 0), stop=(kc == KC - 1))
                    P.copy(Vt[:, tb, :], bk[:, :], eng="scalar")
                    if "m" in HSKIP:
                        continue
                    P.ts(Vm[0][:, tb, :], bk[:, :], rowm[:, 0:1], ALU.mult)
                    P.ts(Vm[1][:, tb, :], bk[:, :], rowm[:, 1:2], ALU.mult)
                (wv,) = ring.next()
                proj_chunks(wv, 4, g, lambda ci, bk: P.act(hgs[:, ci, :], bk[:, :], AF.Silu))
                HSUB = int(os.environ.get("HSUB", "99"))
                if HSUB < 2:
                    ring.next()
                    ring.next()
                    return
                with phase():
                    lg = TT(P, "lg", [128, 2, NT], F32, split=1)
                    bb = TT(P, "bb", [128, 2, NT], F32, split=1)
                    ff = TT(P, "ff", [128, 2, NT], F32, split=1)
                    khf = TT(P, "khf", [128, 2, NT], F32, split=1)
                    gch = TT(P, "gch", [128, 2, 32], F32, split=1)
                    tot = TT(P, "tot", [128, 2, 1], F32, split=1)
                    ii = [0]
                    for d in range(2):
                        (wv,) = ring.next()

                        def fgate(j, bk, d=d):
                            k = ii[0] % 2
                            ii[0] += 1
                            f, l, b, kf = ff[:, k, :], lg[:, k, :], bb[:, k, :], khf[:, k, :]
                            gc = gch[:, k, :]
                            P.act(f, bk[:, :], AF.Sigmoid)
                            P.ts(f, f, oml[:, d, j:j + 1], ALU.mult, lb[:, d, j:j + 1], ALU.add)
                            P.act(l, f, AF.Ln)
                            P.ts(f, f, -1.0, ALU.mult, 1.0, ALU.add)
                            P.op("vector", lambda e: e.tensor_tensor_scan(out=b.ap, data0=scanm.t[:, :], data1=l.ap, initial=0.0,
                                                                         op0=ALU.mult, op1=ALU.add),
                                 reads=l.bufs + scanm.bufs, writes=b.bufs)
                            b3 = V(b.ap.rearrange("p (n c) -> p n c", c=16), b.bufs)
                            P.copy(gc, V(b3.ap[:, :, 15], b.bufs))
                            gbc = V(gc.ap.unsqueeze(2).to_broadcast([128, 32, 16]), gc.bufs)
                            if sample:
                                t = tmp()
                                P.op("vector", lambda e, t=t: e.tensor_tensor_scan(out=t.ap, data0=ones_f.t[:, 0:1].to_broadcast([128, NT]), data1=l.ap,
                                                                                  initial=0.0, op0=ALU.mult, op1=ALU.add),
                                     reads=l.bufs + ones_f.bufs, writes=t.bufs)
                                P.copy(xpay[:, 512 + d * 4 + j:512 + d * 4 + j + 1], V(t.ap[:, NT - 1:NT], t.bufs))
                                if d == 1:
                                    tt_ = tot[:, k, :]
                                    P.copy(tt_, V(t.ap[:, NT - 1:NT], t.bufs))
                                    P.stt(t, t, -1.0, l, ALU.mult, ALU.add)
                                    P.ts(t, t, tt_, ALU.add)
                                P.act(t, t, AF.Exp)
                                P.tt(Qf[d][:, j, :], qh[:, j, :], t, ALU.mult)
                            if d == 1:
                                P.tt(b, l, b, ALU.subtract)
                                P.tt(b3, b3, gbc, ALU.add)
                            P.act(eg[:, d, j, :], gc, AF.Exp)
                            t = tmp()
                            P.act(t, b, AF.Exp)
                            P.tt(Qt[d][:, j, :], qh[:, j, :], t, ALU.mult)
                            t = tmp()
                            P.act(t, b, AF.Exp, scale=-1.0)
                            P.tt(Kt[d][:, j, :], f, t, ALU.mult)
                            t = tmp()
                            P.tt(V(t.ap.rearrange("p (n c) -> p n c", c=16), t.bufs), gbc, b3, ALU.subtract)
                            P.act(t, t, AF.Exp)
                            P.tt(kf, f, t, ALU.mult)
                            bk2 = bank()
                            for tb in range(4):
                                P.transpose(bk2[:, tb * 128:(tb + 1) * 128], V(kf.ap[:, tb * 128:(tb + 1) * 128], kf.bufs), ident[:], signal=(tb == 3))
                            P.copy(Kh[d][:, :, j * 128:(j + 1) * 128], V(bk2.t[:, :].rearrange("p (t f) -> p t f", f=128), bk2.bufs), eng="scalar")

                        proj_chunks(wv, 4, g, fgate)
                if HSUB >= 3:
                    _hgrn_scan(g, sample, Vt, Vm, Qt, Kt, Kh, eg, ohT, xpay)

            def hmap(h):
                e = h // 2
                return ((e // 2) if h % 2 == 0 else 2 + e // 2, (e % 2) * 64)

            def _hgrn_scan(g, sample, Vt, Vm, Qt, Kt, Kh, eg, ohT, xpay):
                oacc = banks[0:4]
                ATs = TT(P, "ATs", [128, 4, 512], BF16, split=1)
                ai = [0]
                S = TT(P, "S", [128, 2, 4, 64], F32, split=1)
                Sb = TT(P, "Sb", [128, 2, 512], BF16, split=1)
                started = {}

                def st(key):
                    if key not in started:
                        started[key] = True
                        return True
                    return False

                for tb in range(4):
                    for d in range(2):
                        ats = []
                        for par in range(2):
                            Ab = banks[6 + par]
                            for idx in range(4):
                                h = idx * 2 + par
                                j, r0 = h // 2, (h % 2) * 64
                                P.mm(Ab[:, idx * 128:(idx + 1) * 128], Kt[d][r0:r0 + 64, j, tb * 128:(tb + 1) * 128],
                                     Qt[d][r0:r0 + 64, j, tb * 128:(tb + 1) * 128], start=True, stop=True, signal=(idx == 3))
                            at = ATs[:, (ai[0] % 2) * 2 + par, :]
                            P.tt(at, Ab[:, :], maskT[:, d, :], ALU.mult)
                            ats.append(at)
                        ai[0] += 1
                        for par in range(2):
                            for idx in range(4):
                                h = idx * 2 + par
                                jj, po = hmap(h)
                                P.mm(oacc[jj][po:po + 64, tb * 128:(tb + 1) * 128], Vt[:, tb, h * 64:(h + 1) * 64],
                                     V(ats[par].ap[:, idx * 128:(idx + 1) * 128], ats[par].bufs), start=st((jj, po)), stop=False,
                                     signal=(idx == 3), skip_group_check=True)
                if int(os.environ.get("HSUB", "99")) < 4:
                    return
                if sample:
                    order = [list(range(32)), list(range(31, -1, -1))]
                    resets = []
                else:
                    order = [list(range(32)), list(range(15, -1, -1)) + list(range(31, 15, -1))]
                    resets = [16]
                zero_state = True
                S2 = [S, TT(P, "S2", [128, 2, 4, 64], F32, split=1)]
                Tm = TT(P, "Tm", [128, 2, 256], F32, split=1)
                Sflat = lambda s_, d: V(s_.t[:, d, :, :].rearrange("p j v -> p (j v)"), [s_.bufs[d]])
                cur = 0
                for i in range(32):
                    if i in resets:
                        zero_state = True
                    Sc, Sn = S2[cur], S2[1 - cur]
                    dbank = [banks[4 + 2 * (i % 2)], banks[5 + 2 * (i % 2)]]
                    for d in range(2):
                        dSb = dbank[d]
                        c = order[d][i]
                        tbk, m, par = c // 8, (c % 8) // 2, c % 2
                        rows = slice(m * 32, m * 32 + 32)
                        for h in range(8):
                            j, r0 = h // 2, (h % 2) * 64
                            P.mm(dSb[r0:r0 + 64, j * 64:(j + 1) * 64], Kh[d][rows, tbk, h * 64:(h + 1) * 64],
                                 Vm[par][rows, tbk, h * 64:(h + 1) * 64], start=True, stop=True,
                                 signal=(h == 7), skip_group_check=True, tile_position=(m * 32, r0))
                    if not zero_state:
                        sb = Sb[:, i % 2, :]
                        P.copy(sb, V(Sc.t[:, :, :, :].rearrange("p d j v -> p (d j v)"), Sc.bufs), eng="scalar")
                        for d in range(2):
                            c = order[d][i]
                            for h in range(8):
                                j, r0 = h // 2, (h % 2) * 64
                                jj, po = hmap(h)
                                lastmm = (i == 31 and d == 1)
                                P.mm(oacc[jj][po:po + 64, c * 16:(c + 1) * 16],
                                     V(sb.ap[r0:r0 + 64, (d * 4 + j) * 64:(d * 4 + j + 1) * 64], sb.bufs),
                                     Qt[d][r0:r0 + 64, j, c * 16:(c + 1) * 16], start=False, stop=lastmm,
                                     signal=(lastmm or h == 7), skip_group_check=True)
                    for d in range(2):
                        c = order[d][i]
                        if not zero_state:
                            P.tt(V(Tm.t[:, d, :].rearrange("p (j v) -> p j v", v=64), [Tm.bufs[d]]), Sc[:, d, :, :],
                                 V(eg.t[:, d, :, c:c + 1].to_broadcast([128, 4, 64]), eg.bufs), ALU.mult)
                            P.tt(Sflat(Sn, d), Tm[:, d, :], dbank[d][:, 0:256], ALU.add)
                        else:
                            P.copy(Sflat(Sn, d), dbank[d][:, 0:256])
                    zero_state = False
                    cur = 1 - cur
                    if (not sample) and i in (15, 31):
                        sq = i // 16
                        P.dma("sync", V(nst_d[sq].rearrange("d (j hh) k v -> (hh k) d j v", hh=2), []), S2[cur][:], is_output=True)
                S = S2[cur]
                for j in range(4):
                    P.copy(ohT[:, j, :], oacc[j][:, :], eng="scalar" if j % 2 else "vector")
                if sample:
                    P.copy(xpay[:, 0:512], V(S.t[:, :, :, :].rearrange("p d j v -> p (d j v)"), S.bufs))
                    P.dma("sync", V(cc2_in[:, :], [b_cc2i]), xpay[:])
                    P.custom("gpsimd", lambda e: e.collective_compute("AllGather", ALU.bypass, replica_groups=[[0, 1, 2, 3], [4, 5, 6, 7]],
                                                                     ins=[cc2_in.ap().opt()], outs=[cc2_out.ap().opt()]),
                             reads=[b_cc2i], writes=[b_cc2o], sem_buf=b_cc2o)

            def hgrn_fix(ohT, Qf):
                gath = TT(P, "gath", [128, 4, 520], F32)
                P.dma("sync", gath[:], V(cc2_out.ap().rearrange("(r p) f -> p r f", p=128), [b_cc2o]))
                Sin = TT(P, "Sin", [128, 2, 4, 64], F32)
                P.dma("sync", Sin[:], V(s0_d.rearrange("d (j hh) k v -> (hh k) d j v", hh=2), []))
                egr = TT(P, "egr", [128, 4, 8], F32)
                P.act(egr[:], gath[:, :, 512:520], AF.Exp)
                t1 = TT(P, "sfx1", [128, 4, 64], F32)
                for d in range(2):
                    rs_ = [0, 1, 2] if d == 0 else [3, 2, 1]
                    for r in rs_:
                        sd = S_d = Sin[:, d, :, :]
                        P.tt(t1[:], sd, V(egr.t[:, r, d * 4:(d + 1) * 4].unsqueeze(2).to_broadcast([128, 4, 64]), egr.bufs), ALU.mult)
                        P.tt(t1[:], t1[:], V(gath.t[:, r, d * 256:(d + 1) * 256].rearrange("p (j v) -> p j v", v=64), gath.bufs), ALU.add)
                        P.tt(t1[:], t1[:], sd, ALU.subtract)
                        P.stt(sd, t1[:], sel[:, d * 4 + r:d * 4 + r + 1], sd, ALU.mult, ALU.add)
                Sinb = TT(P, "Sinb", [128, 512], BF16)
                P.copy(Sinb[:], V(Sin.t[:, :, :, :].rearrange("p d j v -> p (d j v)"), Sin.bufs))
                for d in range(2):
                    for h in range(8):
                        j, r0 = h // 2, (h % 2) * 64
                        jj, po = hmap(h)
                        P.mm(banks[jj][po:po + 64, :], Sinb[r0:r0 + 64, (d * 4 + j) * 64:(d * 4 + j + 1) * 64], Qf[d][r0:r0 + 64, j, :],
                             start=(d == 0), stop=(d == 1), signal=(d == 1), skip_group_check=True)
                for jj in range(4):
                    P.tt(ohT[:, jj, :], ohT[:, jj, :], banks[jj][:, :], ALU.add)

            def hgrn_norm(ohT, hgs, orT):
                for j in range(4):
                    bk = rstd_gen([ohT[:, j, :]], 64, ones=bdiag[:], bk=bank4())
                    t = tmp()
                    P.tt(t, ohT[:, j, :], bk[:, :], ALU.mult)
                    P.stt(orT[:, j, :], t, hgng[:, 0:1], hgs[:, j, :], ALU.mult, ALU.mult)

            def merge(g, oaT, orT):
                mT = TT(P, "mT", [128, KC, NT], BF16, split=1)
                for jp in range(4):
                    wga, wgr, wua, wur = ring.next()
                    for cc in range(2):
                        dc = jp * 2 + cc
                        b_ga, b_gr, b_ua, b_ur = bank(), bank(), bank(), bank()
                        for kc in range(KC):
                            P.mm(b_ga[:, :], V(wga.ap[:, kc, cc * 128:(cc + 1) * 128], wga.bufs), hT[g][:, kc, :], start=(kc == 0), stop=(kc == KC - 1))
                        for kc in range(KC):
                            P.mm(b_gr[:, :], V(wgr.ap[:, kc, cc * 128:(cc + 1) * 128], wgr.bufs), hT[g][:, kc, :], start=(kc == 0), stop=(kc == KC - 1))
                        for kc in range(4):
                            P.mm(b_ua[:, :], V(wua.ap[:, kc, cc * 128:(cc + 1) * 128], wua.bufs), oaT[:, kc, :], start=(kc == 0), stop=(kc == 3))
                        for kc in range(4):
                            P.mm(b_ur[:, :], V(wur.ap[:, kc, cc * 128:(cc + 1) * 128], wur.bufs), orT[:, kc, :], start=(kc == 0), stop=(kc == 3))
                        sa, sr = tmp(), tmp()
                        P.act(sa, b_ga[:, :], AF.Sigmoid)
                        P.act(sr, b_gr[:, :], AF.Sigmoid)
                        P.tt(sa, sa, b_ua[:, :], ALU.mult)
                        P.tt(sr, sr, b_ur[:, :], ALU.mult)
                        P.tt(mT[:, dc, :], sa, sr, ALU.add)
                for jp in range(2):
                    (wo,) = ring.next()
                    for cc in range(4):
                        dc = jp * 4 + cc
                        bk = bank()
                        for kc in range(KC):
                            P.mm(bk[:, :], V(wo.ap[:, kc, cc * 128:(cc + 1) * 128], wo.bufs), mT[:, kc, :], start=(kc == 0), stop=(kc == KC - 1))
                        P.stt(xT[g][:, dc, :], bk[:, :], g5[:, dc, g:g + 1], xT[g][:, dc, :], ALU.mult, ALU.add)

            for ch in KGRP:
                group(int(ch))

        if stage >= 5:
            b_cc1i, b_cc1o, b_cc2i, b_cc2o = Buf("cc1i"), Buf("cc1o"), Buf("cc2i"), Buf("cc2o")
            ones_f = TT(P, "ones_f", [128, 1], F32)
            P.memset(ones_f[:], 1.0)
            with ExitStack() as mph:
                P.alloc = mph
                mixer()
                P.barrier()
                P.alloc = es

        with ExitStack() as ph:
            P.alloc = ph
            actT = [TT(P, "actT%d" % g, [128, NFF, NT], BF16, split=1) for g in range(2)]
            for g in range(2):
                hT[g] = TT(P, "hT%d" % g, [128, KC, NT], BF16, split=1)
            if stage >= 4:
                for g in range(2):
                    norm_mod(g, 2, 6, 7)
                gs = ffn(8, actT)
                ffn_down(8, actT, gs)
            P.barrier()
            P.alloc = es

        with ExitStack() as ph:
            P.alloc = ph
            ytok = [TT(P, "ytok%d" % i, [128, D], F32) for i in range(2)]
            yT = TT(P, "yT", [128, KC, NT], F32, split=1)
            for g in range(2):
                bk = rstd_of(g)
                for c in range(KC):
                    P.stt(yT[:, c, :], xT[g][:, c, :], finalg[:, c:c + 1], bk[:, :], ALU.mult, ALU.mult)
                for tb in range(4):
                    yt = ytok[(g * 4 + tb) % 2]
                    for half in range(2):
                        b2 = bank()
                        for q in range(4):
                            c = half * 4 + q
                            P.transpose(b2[:, q * 128:(q + 1) * 128], yT[:, c, tb * 128:(tb + 1) * 128], ident[:], signal=(q == 3))
                        P.copy(yt[:, half * 512:(half + 1) * 512], b2[:, :], eng="scalar" if half else "vector")
                    P.dma("sync", V(y_d[g][tb * 128:(tb + 1) * 128, :], []), yt[:], is_output=True)
            P.barrier()
            P.alloc = es

        P.finish()
        print("instr counts", P.ninstr, "dma sems", P.dma_used, "peak sbuf bytes/partition", P.peak_bytes)
    return nc


_NC_CACHE = {}


def _rope_tables():
    t = np.arange(2048)
    row = (t // 64).astype(np.float32)
    col = (t % 64).astype(np.float32)
    inv = (10000.0 ** (-np.arange(8, dtype=np.float32) / 8)).astype(np.float32)
    cos = np.zeros((32, 2048), np.float32)
    sin = np.zeros((32, 2048), np.float32)
    for half, pos in ((0, row), (1, col)):
        ang = pos[None, :] * inv[:, None]
        c, s_ = np.cos(ang).astype(np.float32), np.sin(ang).astype(np.float32)
        cos[half * 16:half * 16 + 8] = c
        cos[half * 16 + 8:half * 16 + 16] = c
        sin[half * 16:half * 16 + 8] = -s_
        sin[half * 16 + 8:half * 16 + 16] = s_
    return cos, sin


_HPERM = np.array([0, 2, 4, 6, 1, 3, 5, 7])
_PERM = np.array([j + 8 if (j % 16) < 8 else j - 8 for j in range(32)])


def _prep_inputs(inp):
    f = lambda a: np.ascontiguousarray(np.asarray(a, dtype=np.float32))
    xp = f(inp["x_prompt"])
    xs = f(inp["x_sample"])
    c = f(inp["c"])
    cctx = f(inp["c_ctx"])
    w_in = f(inp["w_in"][0])
    w_qb = f(inp["w_qb"][0])
    w_kvb = f(inp["w_kvb"][0])
    kr = w_in[:, 640:672]
    qbx = []
    for h in range(8):
        rp = w_qb[:, h * 96 + 64:h * 96 + 96]
        qbx += [w_qb[:, h * 96:h * 96 + 64], rp, rp[:, _PERM]]
    kvb = w_kvb.reshape(256, 8, 128)
    s_ = np.arange(128)[:, None]
    t_ = np.arange(128)[None, :]
    same = (s_ // 16) == (t_ // 16)
    mf = (same & (s_ <= t_)).astype(np.float32)
    mb = (same & (s_ >= t_)).astype(np.float32)
    maskT = np.stack([np.tile(mf, (1, 4)), np.tile(mb, (1, 4))], axis=1)
    scanm = np.ones((128, NT), np.float32)
    scanm[:, ::16] = 0.0
    par = (np.arange(128) // 16) % 2
    rowm = np.stack([(par == 0), (par == 1)], axis=1).astype(np.float32)
    bdiag = np.kron(np.eye(2, dtype=np.float32), np.ones((64, 64), np.float32))
    cos, sin = _rope_tables()
    hg = f(inp["hg_gamma"])
    shared = {
        "w_ada": f(inp["w_ada"][0]),
        "b_ada": f(inp["b_ada"][0].reshape(72, 128).T),
        "norm_g": f(inp["norm_g"][0].reshape(3, KC, 128).transpose(2, 0, 1)),
        "final_g": f(inp["final_g"].reshape(KC, 128).T),
        "ident": np.eye(128, dtype=np.float32),
        "w_in": w_in,
        "w_krx": f(np.concatenate([kr, kr, kr, kr[:, _PERM]], axis=1)),
        "w_qbx": f(np.concatenate(qbx, axis=1)),
        "w_kvbx": f(np.concatenate([kvb[:, :, :64].reshape(256, 512), kvb[:, :, 64:].reshape(256, 512)], axis=1)),
        "w_up_attn": f(inp["w_up_attn"][0]),
        "w_up_rec_p": f(inp["w_up_rec"][0].reshape(8, 64, D)[_HPERM].reshape(512, D)),
        "w_hgate_p": f(w_in[:, 2720:3232].reshape(D, 8, 64)[:, _HPERM].reshape(D, 512)),
        "w_out": f(inp["w_out"][0]),
        "maskT": f(maskT),
        "scanm": scanm,
        "rowm": rowm,
        "bdiag": bdiag,
        "hgam": f(hg.reshape(2, 2, 4, 128).transpose(3, 0, 1, 2)),
        "hgng": f(np.tile(inp["hg_norm_g"][0], 2).reshape(128, 1)),
        "qng": f(inp["q_norm_g"][0].reshape(3, 128).T),
        "kvng": f(inp["kv_norm_g"][0].reshape(2, 128).T),
    }
    shared["ffn1_w1"] = f(inp["ffn1_w1"][0])
    shared["ffn1_w3"] = f(inp["ffn1_w3"][0])
    shared["ffn1_w2"] = f(inp["ffn1_w2"][0])
    shared["ffn2_w1"] = f(inp["ffn2_w1"][0])
    shared["ffn2_w3"] = f(inp["ffn2_w3"][0])
    shared["ffn2_w2"] = f(inp["ffn2_w2"][0])
    maps = []
    for core in range(8):
        b, seg = core // 4, core % 4
        m = dict(shared)
        m["xs"] = f(xs[b, seg * NT:(seg + 1) * NT])
        m["xp"] = f(xp[2 * core:2 * core + 2].reshape(NT, D))
        cs = np.stack([c[b], cctx], axis=0)
        m["csel"] = f(cs.reshape(2, KC, 128).transpose(2, 1, 0))
        rope = np.zeros((128, NT), np.float32)
        rope[64:96] = cos[:, seg * NT:(seg + 1) * NT]
        rope[96:128] = sin[:, seg * NT:(seg + 1) * NT]
        m["rope"] = rope
        sel = np.zeros((128, 8), np.float32)
        for r in range(4):
            sel[:, r] = 1.0 if r < seg else 0.0
            sel[:, 4 + r] = 1.0 if r > seg else 0.0
        m["sel"] = sel
        m["cckv"] = f(inp["cache_ckv"][b, 0])
        m["ckr"] = f(inp["cache_krope"][b, 0])
        m["s0"] = f(inp["state_hgrn"][b, 0])
        maps.append(m)
    return maps


def kernel(**inp):
    if "nc" not in _NC_CACHE:
        _NC_CACHE["nc"] = build_program(int(os.environ.get("KSTAGE", "99")))
    nc = _NC_CACHE["nc"]
    maps = _prep_inputs(inp)
    res = run_bass_kernel_spmd(nc, maps, core_ids=list(range(8)))
    R = res.results
    y_prompt = np.zeros((16, 256, D), np.float32)
    y_sample = np.zeros((2, 2048, D), np.float32)
    new_ckv = np.zeros((16, 1, 256, 256), np.float32)
    new_kr = np.zeros((16, 1, 256, 32), np.float32)
    new_st = np.zeros((16, 1, 2, 8, 64, 64), np.float32)
    for core in range(8):
        b, seg = core // 4, core % 4
        y_sample[b, seg * NT:(seg + 1) * NT] = R[core]["ys"]
        y_prompt[2 * core:2 * core + 2] = R[core]["yp"].reshape(2, 256, D)
        new_ckv[2 * core:2 * core + 2, 0] = R[core]["nckv"].reshape(2, 256, 256)
        new_kr[2 * core:2 * core + 2, 0] = R[core]["nkr"].reshape(2, 256, 32)
        new_st[2 * core:2 * core + 2, 0] = R[core]["nst"]
    return (y_prompt, y_sample, new_ckv, new_kr, new_st)
```
